# Optimizing a Trainium2 kernel written in Bass

```python
import functools
import jax, jax.numpy as jnp
from jax import lax
import numpy as np

D_MODEL = 4096
BATCH = 2
SEQ = 8192
DEPTH = 1
DEC_BATCH = 16
DEC_SEQ = 64
PAST_LEN = 4096

CHUNK = 64
QBLOCK = 128
EPS = 1e-6
MLA_HEADS = D_MODEL // 256
QK_NOPE = 128
QK_ROPE = 64
V_HEAD = 128
Q_LORA = D_MODEL // 4
KV_LORA = D_MODEL // 8
ROPE_THETA = 10000.0
MLA_SCALE = (QK_NOPE + QK_ROPE) ** -0.5
MLA_WIDTH = MLA_HEADS * V_HEAD
SB_HEADS = D_MODEL // 256
SB_HEAD_DIM = 128
SB_WIDTH = SB_HEADS * SB_HEAD_DIM
SB_SCALE = SB_HEAD_DIM ** -0.5
N_MEM = 256
MEM_HEADS = 4
MEM_HEAD_DIM = 128
MEM_WIDTH = MEM_HEADS * MEM_HEAD_DIM
MEM_SCALE = MEM_HEAD_DIM ** -0.5
D_FF = ((8 * D_MODEL + 768 - 1) // 768) * 256
_O1 = Q_LORA
_O2 = _O1 + KV_LORA
_O3 = _O2 + QK_ROPE
_O4 = _O3 + SB_WIDTH
_O5 = _O4 + SB_WIDTH
_O6 = _O5 + SB_WIDTH
IN_SPLITS = (_O1, _O2, _O3, _O4, _O5, _O6)
IN_WIDTH = _O6 + 2 * D_MODEL

kernel_name = "mla_stickbreak_gated_streaming_encoder"


def _rmsnorm(x, g):
    x32 = x.astype(jnp.float32)
    y = x32 * lax.rsqrt(jnp.mean(x32 * x32, axis=-1, keepdims=True) + EPS)
    return (y * g.astype(jnp.float32)).astype(x.dtype)


def _rope(x, pos):
    half = x.shape[-1] // 2
    inv = ROPE_THETA ** (-jnp.arange(half, dtype=jnp.float32) / half)
    ang = pos.astype(jnp.float32)[:, None] * inv[None, :]
    cos = jnp.cos(ang)[None, :, None, :]
    sin = jnp.sin(ang)[None, :, None, :]
    x32 = x.astype(jnp.float32)
    x1, x2 = x32[..., :half], x32[..., half:]
    return jnp.concatenate([x1 * cos - x2 * sin, x1 * sin + x2 * cos], axis=-1).astype(x.dtype)


def _sweep_queries(block_fn, q_arrays, q_pos):
    t = q_pos.shape[0]
    if t <= QBLOCK:
        return block_fn(*q_arrays, q_pos)
    nb = t // QBLOCK

    def to_blocks(a):
        return jnp.swapaxes(a.reshape((a.shape[0], nb, QBLOCK) + a.shape[2:]), 0, 1)

    blocks = tuple(to_blocks(a) for a in q_arrays) + (q_pos.reshape(nb, QBLOCK),)
    out = lax.map(lambda args: block_fn(*args), blocks)
    out = jnp.swapaxes(out, 0, 1)
    return out.reshape((out.shape[0], t) + out.shape[3:])


def _mla_block(q_lat, q_rope, q_pos, c_kv, k_rope, k_pos, w_uv):
    s = (jnp.einsum('bqhc,bkc->bhqk', q_lat, c_kv)
         + jnp.einsum('bqhr,bkr->bhqk', q_rope, k_rope)).astype(jnp.float32) * MLA_SCALE
    allowed = (k_pos // CHUNK)[None, :] <= (q_pos // CHUNK)[:, None]
    s = jnp.where(allowed[None, None], s, -jnp.inf)
    p = jax.nn.softmax(s, axis=-1).astype(c_kv.dtype)
    o_lat = jnp.einsum('bhqk,bkc->bqhc', p, c_kv)
    return jnp.einsum('bqhc,chv->bqhv', o_lat, w_uv)


def _sb_block(q, q_pos, k, v, k_pos):
    z = jnp.einsum('bqhd,bkhd->bhqk', q, k).astype(jnp.float32) * SB_SCALE
    valid = (k_pos[None, :] < q_pos[:, None])[None, None]
    log_keep = jnp.where(valid, jax.nn.log_sigmoid(-z), 0.0)
    suffix = lax.cumsum(log_keep, axis=3, reverse=True) - log_keep
    w = jnp.where(valid, jnp.exp(jax.nn.log_sigmoid(z) + suffix), 0.0)
    return jnp.einsum('bhqk,bkhd->bqhd', w.astype(v.dtype), v)


def _mem_kv(mem, g_mem, w_mk, w_mv):
    b, m, _ = mem.shape
    mn = _rmsnorm(mem, g_mem)
    k = (mn @ w_mk).reshape(b, m, MEM_HEADS, MEM_HEAD_DIM)
    v = (mn @ w_mv).reshape(b, m, MEM_HEADS, MEM_HEAD_DIM)
    return k, v


def _layer(x, pos, past, mem_k, mem_v, g_mix, w_in, b_gate, g_q_lat, w_uq, g_kv_lat, w_uk, w_uv,
           w_branch_a, w_branch_b, w_out, g_xattn, w_mq, w_mo, g_ffn, w_gate, w_up, w_down):
    b, t, _ = x.shape
    h = _rmsnorm(x, g_mix)
    c_q, c_kv, k_rope_in, sb_q, sb_k, sb_v, gate_logits = jnp.split(h @ w_in, IN_SPLITS, axis=-1)
    q = jnp.einsum('btc,chd->bthd', _rmsnorm(c_q, g_q_lat), w_uq)
    q_lat = jnp.einsum('bthn,chn->bthc', q[..., :QK_NOPE], w_uk)
    q_rope = _rope(q[..., QK_NOPE:], pos)
    ckv = _rmsnorm(c_kv, g_kv_lat)
    krope = _rope(k_rope_in[:, :, None, :], pos)[:, :, 0, :]
    sbq = sb_q.reshape(b, t, SB_HEADS, SB_HEAD_DIM)
    sbk = sb_k.reshape(b, t, SB_HEADS, SB_HEAD_DIM)
    sbv = sb_v.reshape(b, t, SB_HEADS, SB_HEAD_DIM)
    if past is None:
        all_ckv, all_krope, all_k, all_v = ckv, krope, sbk, sbv
    else:
        p_ckv, p_krope, p_k, p_v = past
        all_ckv = jnp.concatenate([p_ckv, ckv], axis=1)
        all_krope = jnp.concatenate([p_krope, krope], axis=1)
        all_k = jnp.concatenate([p_k, sbk], axis=1)
        all_v = jnp.concatenate([p_v, sbv], axis=1)
    k_pos = jnp.arange(all_ckv.shape[1])
    mla_fn = functools.partial(_mla_block, c_kv=all_ckv, k_rope=all_krope, k_pos=k_pos, w_uv=w_uv)
    o_a = _sweep_queries(mla_fn, (q_lat, q_rope), pos).reshape(b, t, MLA_WIDTH)
    sb_fn = functools.partial(_sb_block, k=all_k, v=all_v, k_pos=k_pos)
    o_b = _sweep_queries(sb_fn, (sbq,), pos).reshape(b, t, SB_WIDTH)
    gates = jax.nn.sigmoid(gate_logits + b_gate)
    gate_a, gate_b = gates[..., :D_MODEL], gates[..., D_MODEL:]
    merged = gate_a * (o_a @ w_branch_a) + gate_b * (o_b @ w_branch_b)
    x = x + merged @ w_out
    mq = (_rmsnorm(x, g_xattn) @ w_mq).reshape(b, t, MEM_HEADS, MEM_HEAD_DIM)
    s = jnp.einsum('bthd,bmhd->bhtm', mq, mem_k).astype(jnp.float32) * MEM_SCALE
    p = jax.nn.softmax(s, axis=-1).astype(mem_v.dtype)
    x = x + jnp.einsum('bhtm,bmhd->bthd', p, mem_v).reshape(b, t, MEM_WIDTH) @ w_mo
    h = _rmsnorm(x, g_ffn)
    x = x + (jax.nn.silu(h @ w_gate) * (h @ w_up)) @ w_down
    return x, (ckv, krope, sbk, sbv)


def setup_inputs(seed: int = 0) -> dict:
    key = jax.random.key(seed)
    ks = iter(jax.random.split(key, 40))
    f32 = jnp.float32

    def nrm(shape, scale=1.0):
        return jax.random.normal(next(ks), shape, f32) * scale

    def gain(shape):
        return 1.0 + 0.05 * jax.random.normal(next(ks), shape, f32)

    L = DEPTH
    return {
        'x_prompt': nrm((BATCH, SEQ, D_MODEL)),
        'x_sample': nrm((DEC_BATCH, DEC_SEQ, D_MODEL)),
        'cache_mla_ckv': nrm((L, DEC_BATCH, PAST_LEN, KV_LORA)),
        'cache_mla_krope': nrm((L, DEC_BATCH, PAST_LEN, QK_ROPE)),
        'cache_sb_k': nrm((L, DEC_BATCH, PAST_LEN, SB_HEADS, SB_HEAD_DIM)),
        'cache_sb_v': nrm((L, DEC_BATCH, PAST_LEN, SB_HEADS, SB_HEAD_DIM)),
        'cache_mem_k': nrm((L, DEC_BATCH, N_MEM, MEM_HEADS, MEM_HEAD_DIM)),
        'cache_mem_v': nrm((L, DEC_BATCH, N_MEM, MEM_HEADS, MEM_HEAD_DIM)),
        'mem_prompt': nrm((BATCH, N_MEM, D_MODEL)),
        'g_mix': gain((L, D_MODEL)),
        'w_in': nrm((L, D_MODEL, IN_WIDTH), D_MODEL ** -0.5),
        'b_gate': nrm((L, 2 * D_MODEL), 0.1),
        'g_q_lat': gain((L, Q_LORA)),
        'w_uq': nrm((L, Q_LORA, MLA_HEADS, QK_NOPE + QK_ROPE), Q_LORA ** -0.5),
        'g_kv_lat': gain((L, KV_LORA)),
        'w_uk': nrm((L, KV_LORA, MLA_HEADS, QK_NOPE), KV_LORA ** -0.5),
        'w_uv': nrm((L, KV_LORA, MLA_HEADS, V_HEAD), KV_LORA ** -0.5),
        'w_branch_a': nrm((L, MLA_WIDTH, D_MODEL), MLA_WIDTH ** -0.5),
        'w_branch_b': nrm((L, SB_WIDTH, D_MODEL), SB_WIDTH ** -0.5),
        'w_out': nrm((L, D_MODEL, D_MODEL), D_MODEL ** -0.5),
        'g_xattn': gain((L, D_MODEL)),
        'g_mem': gain((L, D_MODEL)),
        'w_mq': nrm((L, D_MODEL, MEM_WIDTH), D_MODEL ** -0.5),
        'w_mk': nrm((L, D_MODEL, MEM_WIDTH), D_MODEL ** -0.5),
        'w_mv': nrm((L, D_MODEL, MEM_WIDTH), D_MODEL ** -0.5),
        'w_mo': nrm((L, MEM_WIDTH, D_MODEL), MEM_WIDTH ** -0.5),
        'g_ffn': gain((L, D_MODEL)),
        'w_gate': nrm((L, D_MODEL, D_FF), D_MODEL ** -0.5),
        'w_up': nrm((L, D_MODEL, D_FF), D_MODEL ** -0.5),
        'w_down': nrm((L, D_FF, D_MODEL), D_FF ** -0.5),
        'g_final': gain((D_MODEL,)),
    }


def reference(x_prompt, x_sample, cache_mla_ckv, cache_mla_krope, cache_sb_k, cache_sb_v,
              cache_mem_k, cache_mem_v, mem_prompt, g_mix, w_in, b_gate, g_q_lat, w_uq, g_kv_lat,
              w_uk, w_uv, w_branch_a, w_branch_b, w_out, g_xattn, g_mem, w_mq, w_mk, w_mv, w_mo,
              g_ffn, w_gate, w_up, w_down, g_final):
    pos_p = jnp.arange(x_prompt.shape[1])
    past_len = cache_mla_ckv.shape[2]
    pos_s = past_len + jnp.arange(x_sample.shape[1])
    xp, xs = x_prompt, x_sample
    p_ckv, p_krope, p_k, p_v, p_mk, p_mv = [], [], [], [], [], []
    s_ckv, s_krope, s_k, s_v = [], [], [], []
    for l in range(DEPTH):
        w = (g_mix[l], w_in[l], b_gate[l], g_q_lat[l], w_uq[l], g_kv_lat[l], w_uk[l], w_uv[l],
             w_branch_a[l], w_branch_b[l], w_out[l], g_xattn[l], w_mq[l], w_mo[l], g_ffn[l],
             w_gate[l], w_up[l], w_down[l])
        mk, mv = _mem_kv(mem_prompt, g_mem[l], w_mk[l], w_mv[l])
        xp, (ckv, kr, k, v) = _layer(xp, pos_p, None, mk, mv, *w)
        p_ckv.append(ckv); p_krope.append(kr); p_k.append(k); p_v.append(v)
        p_mk.append(mk); p_mv.append(mv)
        past = (cache_mla_ckv[l], cache_mla_krope[l], cache_sb_k[l], cache_sb_v[l])
        xs, (ckv, kr, k, v) = _layer(xs, pos_s, past, cache_mem_k[l], cache_mem_v[l], *w)
        s_ckv.append(ckv); s_krope.append(kr); s_k.append(k); s_v.append(v)
    y_prompt = _rmsnorm(xp, g_final)
    y_sample = _rmsnorm(xs, g_final)
    new_p_ckv = jnp.stack(p_ckv)
    new_p_krope = jnp.stack(p_krope)
    new_p_sb_k = jnp.stack(p_k)
    new_p_sb_v = jnp.stack(p_v)
    new_p_mem_k = jnp.stack(p_mk)
    new_p_mem_v = jnp.stack(p_mv)
    new_s_ckv = jnp.stack(s_ckv)
    new_s_krope = jnp.stack(s_krope)
    new_s_sb_k = jnp.stack(s_k)
    new_s_sb_v = jnp.stack(s_v)
    return (y_prompt, y_sample, new_p_ckv, new_p_krope, new_p_sb_k, new_p_sb_v, new_p_mem_k,
            new_p_mem_v, new_s_ckv, new_s_krope, new_s_sb_k, new_s_sb_v)
```

```python
import numpy as np
import concourse.bass as bass
import concourse.mybir as mybir
from concourse.bass_utils import run_bass_kernel_spmd

F32 = mybir.dt.float32
BF16 = mybir.dt.bfloat16
AF = mybir.ActivationFunctionType
ALU = mybir.AluOpType
AX = mybir.AxisListType

D = 4096; KC = 32; SEQ = 8192; NOWN = 2176; NPO = 2048
EPS = 1e-6
MLA_SCALE = 192 ** -0.5
SB_SCALE = 128 ** -0.5
MEM_SCALE = 128 ** -0.5
NEG = -30000.0
PAST = 4096; ZK = 4160
SB_LIMIT = 229376
SB_BASE = 16384 + 512

ALL_STAGES = ("PM", "PA", "PB1", "PB2", "PB3")


class Buf:
    def __init__(self, name, multi=False, excl=False):
        self.name = name; self.w = {}; self.r = {}; self.multi = multi; self.excl = excl


class Eng:
    def __init__(self, nc, obj, name, is_pe=False):
        self.obj = obj; self.sem = nc.alloc_semaphore("e_" + name); self.cnt = 0
        self.seen = {}; self.is_pe = is_pe; self.name = name

    def wait_map(self, m):
        for sem, val in m.items():
            if self.is_pe and sem is self.sem:
                continue
            if self.seen.get(sem, 0) >= val:
                continue
            self.obj.wait_ge(sem, val)
            self.seen[sem] = val


class Slot:
    def __init__(self, nc, name):
        self.sem = nc.alloc_semaphore("d_" + name); self.cnt = 0


class K:
    def __init__(self, stages):
        self.stages = stages
        nc = self.nc = bass.Bass("TRN2", target_bir_lowering=False)
        self.pe = Eng(nc, nc.tensor, "pe", True)
        self.act = Eng(nc, nc.scalar, "act")
        self.dve = Eng(nc, nc.vector, "dve")
        self.pool = Eng(nc, nc.gpsimd, "pool")
        self.sp = Eng(nc, nc.sync, "sp")
        self.engs = [self.pe, self.act, self.dve, self.pool, self.sp]
        self.sb_off = SB_BASE
        self.inputs = {}
        self.outputs = {}
        self.slots = []
        self.all_slots = []
        self.nname = 0

    def _deps(self, eng, reads, writes):
        for b in reads:
            eng.wait_map(b.w)
            if b.excl:
                eng.wait_map({s_: v_ for s_, v_ in b.r.items() if s_ is not eng.sem})
        for b in writes:
            if not b.multi:
                eng.wait_map(b.w)
            eng.wait_map(b.r)

    def _record(self, sem, val, reads, writes):
        for b in reads:
            if b.r.get(sem, 0) < val:
                b.r[sem] = val
        for b in writes:
            if b.multi:
                if b.w.get(sem, 0) < val:
                    b.w[sem] = val
            else:
                b.w = {sem: val}; b.r = {}

    def op(self, eng, fn, reads=(), writes=()):
        self._deps(eng, reads, writes)
        ins = fn(eng.obj)
        eng.cnt += 1
        ins.then_inc(eng.sem, 1)
        self._record(eng.sem, eng.cnt, reads, writes)

    def mm(self, mms, reads, writes):
        eng = self.pe
        self._deps(eng, reads, writes)
        ins = None
        for (o, l, r, st, sp_) in mms:
            ins = eng.obj.matmul(o, l, r, start=st, stop=sp_)
        eng.cnt += 1
        ins.then_inc(eng.sem, 1)
        self._record(eng.sem, eng.cnt, reads, writes)

    def slot(self, name=None):
        s = Slot(self.nc, f"{name or 's'}_{len(self.all_slots)}")
        if not (name or "").startswith("wc_"):
            self.slots.append(s)
        self.all_slots.append(s)
        return s

    def dma(self, q, out, in_, reads, writes, slot):
        self._deps(q, reads, writes)
        ins = q.obj.dma_start(out=out, in_=in_)
        slot.cnt += 16
        ins.then_inc(slot.sem, 16)
        self._record(slot.sem, slot.cnt, reads, writes)

    def barrier(self):
        m = {}
        for e in self.engs:
            if e.cnt:
                m[e.sem] = e.cnt
        for s in self.slots:
            if s.cnt:
                m[s.sem] = s.cnt
        for e in self.engs:
            mm = dict(m)
            mm.pop(e.sem, None) if e.is_pe else None
            e.wait_map(mm)

    def sb(self, name, shape, dt, off=None):
        sz = int(np.prod(shape[1:])) * (4 if dt == F32 else 2)
        if off is None:
            off = self.sb_off
            self.sb_off += (sz + 63) // 64 * 64
        assert off + sz <= SB_LIMIT, (name, off, sz)
        self.nname += 1
        return self.nc.alloc_sbuf_tensor_at(f"{name}_{self.nname}", list(shape), dt, offset=off)

    def din(self, name, shape, dt=F32):
        t = self.nc.dram_tensor(name, list(shape), dt, kind="ExternalInput")
        self.inputs[name] = (tuple(shape), dt)
        return t.ap()

    def dout(self, name, shape, dt=F32):
        t = self.nc.dram_tensor(name, list(shape), dt, kind="ExternalOutput")
        self.outputs[name] = (tuple(shape), dt)
        return t.ap()

    def dscr(self, name, shape, dt=BF16):
        return self.nc.dram_tensor(name, list(shape), dt).ap()


def weight_table():
    return {
        "w_cq": (8, 32, 128), "w_ckvF": (4, 32, 128), "w_ckvT": (2, 32, 256), "w_kr": (2, 32, 64),
        "w_sbq": (16, 32, 128), "w_sbk": (16, 32, 128), "w_sbv": (8, 32, 256), "w_gate": (64, 32, 128),
        "w_uqn": (16, 8, 128), "w_uqr": (32, 8, 64), "w_ukT": (1, 64, 128), "w_uvr": (1, 64, 128),
        "w_ba": (32, 16, 128), "w_bb": (32, 16, 128), "w_out": (32, 32, 128),
        "w_mq": (4, 32, 128), "w_mkF": (4, 32, 128), "w_mkT": (2, 32, 256), "w_mvT": (2, 32, 256),
        "w_mo": (32, 4, 128), "w_fg": (86, 32, 128), "w_fu": (86, 32, 128), "w_fd": (32, 86, 128),
    }


STAGE_W = {
    "PM": ["w_mkF", "w_mkT", "w_mvT"],
    "PA": ["w_ckvF", "w_ckvT", "w_kr", "w_sbk", "w_sbv"],
    "PB1": ["w_ckvF", "w_ckvT", "w_kr", "w_sbk", "w_sbv", "w_cq", "w_sbq", "w_uqn", "w_uqr", "w_ukT"],
    "PB2": ["w_uvr"],
    "PB3": ["w_gate", "w_ba", "w_bb", "w_out", "w_mq", "w_mo", "w_fg", "w_fu", "w_fd"],
}


STOP_AT = 0


class _Stop(Exception):
    pass


def cp(n):
    if STOP_AT == n:
        raise _Stop()


def build(stages=ALL_STAGES):
    k = K(stages)
    nc = k.nc
    pe, act, dve, pool, sp = k.pe, k.act, k.dve, k.pool, k.sp
    WT = weight_table()
    need_w = []
    for s in stages:
        for w in STAGE_W[s]:
            if w not in need_w:
                need_w.append(w)

    Wf = {}; Wb = {}; Wbuf = {}
    for w in need_w:
        G, kcw, N = WT[w]
        Wf[w] = k.din(w, [G * 128, kcw * N])
        Wb[w] = k.dscr(w + "_bf", [G * 128, kcw * N])
        Wbuf[w] = Buf(w, multi=True)
    consts = k.din("consts", [128, 1024])
    gains = k.din("gains", [128, 256])
    gkv_row = k.din("gkv_row", [128, 512])
    NMSK = 4 * 512 + 4 * 128 + 64
    masks = k.din("masks", [128, NMSK])
    xT_own = k.din("xT_own", [128, KC, NOWN])
    ropeC_own = k.din("ropeC_own", [64, NOWN]); ropeS_own = k.din("ropeS_own", [64, NOWN])
    if "PA" in stages:
        xT_seq = k.din("xT_seq", [128, KC, SEQ])
        ropeC_seq = k.din("ropeC_seq", [64, SEQ]); ropeS_seq = k.din("ropeS_seq", [64, SEQ])
    if "PM" in stages:
        memT = k.din("memT", [128, KC, 256])
    if "PB1" in stages:
        c_ckvT = k.din("c_ckvT", [2 * 128, 4 * PAST]); c_ckv = k.din("c_ckv", [2 * 128, 32 * 512])
        c_krT = k.din("c_krT", [2 * 64, PAST]); c_kT = k.din("c_kT", [2 * 16 * 128, PAST])
        c_v = k.din("c_v", [2 * 16 * 128, 32 * 128])
    if "PB3" in stages:
        c_memkT = k.din("c_memkT", [128, 2 * 4 * 256]); c_memv = k.din("c_memv", [128, 2 * 2 * 512])

    o_yT = k.dout("o_yT", [128, KC, NOWN])
    o_ckv = k.dout("o_ckv", [NOWN, 512]); o_krT = k.dout("o_krT", [64, NOWN])
    o_kT = k.dout("o_kT", [16, 128, NOWN]); o_v = k.dout("o_v", [NOWN, 2048])
    o_memk = k.dout("o_memk", [256, 512]); o_memv = k.dout("o_memv", [256, 512])

    S_ckvT = k.dscr("S_ckvT", [128, 4, SEQ]); S_ckv = k.dscr("S_ckv", [128, 64, 512]); S_krT = k.dscr("S_krT", [64, SEQ])
    S_kT = k.dscr("S_kT", [16, 128, SEQ]); S_v = k.dscr("S_v", [16, 128, 64, 128])
    Z_ckvT = k.dscr("Z_ckvT", [2, 128, 4, ZK]); Z_ckv = k.dscr("Z_ckv", [2, 128, 33, 512]); Z_krT = k.dscr("Z_krT", [2, 64, ZK])
    Z_kT = k.dscr("Z_kT", [2, 16, 128, ZK]); Z_v = k.dscr("Z_v", [2, 16, 128, 33, 128])
    Q_lat = k.dscr("Q_lat", [128, 4, 16, NOWN]); Q_rope = k.dscr("Q_rope", [64, 16, NOWN]); Q_sb = k.dscr("Q_sb", [128, 16, NOWN])
    O_a = k.dscr("O_a", [128, 16, NOWN]); O_b = k.dscr("O_b", [128, 16, NOWN])
    B_S = Buf("S_scr", True); B_Z = Buf("Z_scr", True); B_Q = Buf("Q_scr", True); B_O = Buf("O_scr", True)
    B_out = Buf("outs", True)

    cst_f = k.sb("cst_f", [128, 1024], F32)
    cst = k.sb("cst", [128, 1024], BF16)
    gn = k.sb("gn", [128, 256], F32)
    gkvr = k.sb("gkvr", [128, 512], F32)
    msk = k.sb("msk", [128, NMSK], BF16)
    MK = k.sb("MK", [128, 4, 256], BF16); MV = k.sb("MV", [128, 2, 512], BF16); MKb = Buf("MK", True)
    kmx = k.sb("kmx", [128, 8], F32); kmxb = Buf("kmx")
    epsb = k.sb("epsb", [128, 1], F32)
    B_c = Buf("consts")
    ident = cst[:, 0:128]; Umat = cst[:, 128:256]; ones = cst[:, 256:384]
    ARENA0 = k.sb_off
    WR_CH = 4; CHB = 8192
    wring = k.sb("wring", [128, WR_CH * CHB // 2], BF16)
    wr_bufs = [Buf(f"wr{i}") for i in range(WR_CH)]
    wr_slots = [k.slot(f"wr{i}") for i in range(WR_CH)]
    wr_pos = [0]
    ARENA = k.sb_off

    banks = [nc.alloc_psum_tensor(f"bank{i}", [128, 512], F32) for i in range(8)]
    bbuf = [Buf(f"bank{i}", excl=True) for i in range(8)]

    blk = nc.Block()
    blk.__enter__()
    try:
        ld = k.slot("ld")
        k.dma(sp, cst_f[:], consts, [], [B_c], ld)
        k.dma(sp, gn[:], gains, [], [B_c], ld)
        k.dma(sp, gkvr[:], gkv_row, [], [B_c], ld)
        k.dma(pool, msk[:], masks, [], [B_c], ld)
        k.op(dve, lambda e: e.tensor_copy(out=cst[:], in_=cst_f[:]), [B_c], [B_c])
        k.op(dve, lambda e: e.memset(kmx[:], 0.0), [], [kmxb])
        k.op(dve, lambda e: e.memset(epsb[:], EPS), [B_c], [B_c])
        cp(1)

        for w in need_w:
            k.dma(pool, Wb[w], Wf[w], [], [Wbuf[w]], k.slot("wc_" + w))
        cp(2)

        def wload(w, g0, ng, kc0=0, nk=None):
            G, kcw, N = WT[w]
            nk = kcw if nk is None else nk
            nbytes = ng * nk * N * 2
            nch = (nbytes + CHB - 1) // CHB
            assert nch <= WR_CH
            if wr_pos[0] + nch > WR_CH:
                wr_pos[0] = 0
            c0 = wr_pos[0]; wr_pos[0] += nch
            bufs = wr_bufs[c0:c0 + nch]
            base = c0 * CHB // 2
            dst = wring[:, base:base + ng * nk * N].rearrange("p (g k n) -> p g k n", g=ng, k=nk, n=N)
            src = Wb[w].rearrange("(g p) (k n) -> p g k n", p=128, n=N)[:, g0:g0 + ng, kc0:kc0 + nk, :]
            k.dma(sp, dst, src, [Wbuf[w]], bufs, wr_slots[c0])
            return dst, bufs

        bank_rr = [0]

        def next_bank(lo=0, hi=8):
            b = lo + (bank_rr[0] % (hi - lo)); bank_rr[0] += 1
            return b

        def gemm_fm(w, groups, actT, abuf, T, evac, M=128, kc0=0, nk=None, gpl=1, blo=0, bhi=8):
            G, kcw, N = WT[w]
            nk_ = kcw if nk is None else nk
            groups = list(groups)
            loads = [groups[i:i + gpl] for i in range(0, len(groups), gpl)]
            pend = []

            def issue(i):
                gs = loads[i]
                pend.append(wload(w, gs[0], len(gs), kc0, nk_))
            for i in range(min(2, len(loads))):
                issue(i)
            for i, gs in enumerate(loads):
                wt, wb = pend.pop(0)
                for gi, g in enumerate(gs):
                    b = next_bank(blo, bhi)
                    mms = [(banks[b][0:M, 0:T], wt[:, gi, kk, 0:M], actT[:, kk, 0:T], kk == 0, kk == nk_ - 1) for kk in range(nk_)]
                    k.mm(mms, wb + [abuf], [bbuf[b]])
                    evac(g, banks[b], bbuf[b])
                if i + 2 < len(loads):
                    issue(i + 2)

        def rstd_from_bank(bank_ap, bb, n, out_ap, obuf):
            k.op(act, lambda e: e.activation(out=out_ap, in_=bank_ap, func=AF.Sqrt, bias=epsb[:, 0:1], scale=1.0 / n), [bb, B_c], [obuf])
            k.op(dve, lambda e: e.reciprocal(out=out_ap, in_=out_ap), [obuf], [obuf])

        def front(x_ap, T, gcol0, hT, hbuf, xs, xsb, xslots, sq, sqb, rrow, rbuf, NP=8):
            npc = KC // NP
            b = next_bank()
            for pc in range(npc):
                s = pc % 2
                k.dma(sp, xs[s][:, 0:NP, 0:T], x_ap[:, pc * NP:(pc + 1) * NP, :], [], [xsb[s]], xslots[s])
                k.op(act, lambda e: e.activation(out=sq[s][:, 0:NP, 0:T], in_=xs[s][:, 0:NP, 0:T], func=AF.Square), [xsb[s]], [sqb[s]])
                mms = [(banks[b][:, 0:T], ones, sq[s][:, j, 0:T], pc == 0 and j == 0, pc == npc - 1 and j == NP - 1) for j in range(NP)]
                k.mm(mms, [sqb[s], B_c], [bbuf[b]])
            rstd_from_bank(banks[b][:, 0:T], bbuf[b], D, rrow[:, 0:T], rbuf)
            for pc in range(npc):
                s = pc % 2
                k.dma(sp, xs[s][:, 0:NP, 0:T], x_ap[:, pc * NP:(pc + 1) * NP, :], [], [xsb[s]], xslots[s])
                for j in range(NP):
                    kc = pc * NP + j
                    eng = dve
                    k.op(eng, lambda e: e.scalar_tensor_tensor(out=hT[:, kc, 0:T], in0=xs[s][:, j, 0:T], scalar=gn[:, gcol0 + kc:gcol0 + kc + 1], in1=rrow[:, 0:T], op0=ALU.mult, op1=ALU.mult), [xsb[s], rbuf, B_c], [hbuf])

        def norm_sb(xt, xb, T, gcol0, hT, hbuf, sq, sqb, rrow, rbuf, NP=8):
            npc = KC // NP
            b = next_bank()
            for pc in range(npc):
                s = pc % 2
                k.op(act, lambda e: e.activation(out=sq[s][:, 0:NP, 0:T], in_=xt[:, pc * NP:(pc + 1) * NP, 0:T], func=AF.Square), [xb], [sqb[s]])
                mms = [(banks[b][:, 0:T], ones, sq[s][:, j, 0:T], pc == 0 and j == 0, pc == npc - 1 and j == NP - 1) for j in range(NP)]
                k.mm(mms, [sqb[s], B_c], [bbuf[b]])
            rstd_from_bank(banks[b][:, 0:T], bbuf[b], D, rrow[:, 0:T], rbuf)
            for kc in range(KC):
                eng = dve
                k.op(eng, lambda e: e.scalar_tensor_tensor(out=hT[:, kc, 0:T], in0=xt[:, kc, 0:T], scalar=gn[:, gcol0 + kc:gcol0 + kc + 1], in1=rrow[:, 0:T], op0=ALU.mult, op1=ALU.mult), [xb, rbuf, B_c], [hbuf])

        def proj_arena(TM):
            k.sb_off = ARENA
            a = {}
            a["xs"] = [k.sb(f"xs{i}", [128, 8, TM], F32) for i in range(2)]
            a["xsb"] = [Buf(f"xs{i}") for i in range(2)]
            a["xslots"] = [k.slot(f"xs{i}") for i in range(2)]
            a["sq"] = [k.sb(f"sq{i}", [128, 8, TM], BF16) for i in range(2)]
            a["sqb"] = [Buf(f"sq{i}") for i in range(2)]
            a["hT"] = k.sb("hT", [128, KC, TM], BF16); a["hb"] = Buf("hT", True)
            a["rrow"] = k.sb("rrow", [128, TM], F32); a["rb"] = Buf("rrow")
            return a

        def mk_stage(name, shape, dt, n):
            return [(k.sb(f"{name}{i}", shape, dt), Buf(f"{name}{i}"), k.slot(f"{name}{i}")) for i in range(n)], [0]

        def nxt(st):
            lst, pos = st
            r = lst[pos[0] % len(lst)]; pos[0] += 1
            return r

        def kv_stage(TM):
            st = {}
            st["bf512"] = mk_stage("stb", [128, 512], BF16, 3)
            st["f512"] = mk_stage("stf", [128, 512], F32, 3)
            st["craw"] = k.sb("craw", [128, 4, TM], F32); st["crawb"] = Buf("craw", True)
            st["csq"] = k.sb("csq", [128, 4, TM], BF16); st["csqb"] = Buf("csq", True)
            st["srow"] = k.sb("srow", [128, TM], F32); st["srb"] = Buf("srow")
            st["cT"] = k.sb("cT", [128, 4, TM], BF16); st["cTb"] = Buf("cT", True); st["cTs"] = k.slot("cTs")
            st["rc"] = k.sb("rc", [64, TM], F32); st["rs"] = k.sb("rs", [64, TM], F32); st["rcb"] = Buf("rc"); st["rsl"] = k.slot("rsl")
            st["krf"] = [k.sb(f"krf{i}", [64, TM], F32) for i in range(2)]; st["krfb"] = [Buf("krf0"), Buf("krf1")]
            st["krs"] = k.slot("krs"); st["krs2"] = k.slot("krs2")
            st["krb16"] = k.sb("krb16", [64, TM], BF16); st["krb16b"] = Buf("krb16")
            st["ctm"] = k.sb("ctm", [128, TM // 128, 512], F32); st["ctmb"] = Buf("ctm", True)
            st["acc"] = k.sb("acc", [128, 8], F32); st["accb"] = Buf("acc", True)
            st["junk"] = k.sb("junk", [128, 256], F32); st["junkb"] = Buf("junk", True)
            st["s1"] = k.sb("s1", [128, 1], F32); st["s1b"] = Buf("s1")
            st["tmpc"] = k.sb("tmpc", [128, 1], F32); st["tmpb"] = Buf("tmpc")
            return st

        def kv_project(a, T, ropeC_ap, ropeS_ap, kst, scr=None, out=None, kmax=None):
            hT, hb = a["hT"], a["hb"]
            ntb = (T + 127) // 128
            TB = min(T, 128)
            def ev_k(h, bank, bb):
                if scr is not None:
                    t, tb_, sl = nxt(kst["bf512"])
                    k.op(act, lambda e: e.activation(out=t[:, 0:T], in_=bank[:, 0:T], func=AF.Copy), [bb], [tb_])
                    for (dst, c0, c1) in scr["kT"](h):
                        k.dma(sp, dst, t[:, c0:c1], [tb_], [scr["buf"]], sl)
                if out is not None:
                    t, tb_, sl = nxt(kst["f512"])
                    k.op(dve, lambda e: e.tensor_copy(out=t[:, 0:T], in_=bank[:, 0:T]), [bb], [tb_])
                    k.dma(sp, out["kT"](h), t[:, 0:T], [tb_], [B_out], sl)
            gemm_fm("w_sbk", range(16), hT, hb, T, ev_k)
            craw = kst["craw"]; crawb = kst["crawb"]; csq = kst["csq"]; csqb = kst["csqb"]
            def ev_c(cc, bank, bb):
                k.op(act, lambda e: e.activation(out=craw[:, cc, 0:T], in_=bank[:, 0:T], func=AF.Copy), [bb], [crawb])
                k.op(act, lambda e: e.activation(out=csq[:, cc, 0:T], in_=bank[:, 0:T], func=AF.Square), [bb], [csqb])
            srow = kst["srow"]; srb = kst["srb"]
            cT = kst["cT"]; cTb = kst["cTb"]; cTs = kst["cTs"]
            if scr is not None:
                gemm_fm("w_ckvF", range(4), hT, hb, T, ev_c)
                b = next_bank()
                k.mm([(banks[b][:, 0:T], ones, csq[:, cc, 0:T], cc == 0, cc == 3) for cc in range(4)], [csqb, B_c], [bbuf[b]])
                rstd_from_bank(banks[b][:, 0:T], bbuf[b], 512, srow[:, 0:T], srb)
                for cc in range(4):
                    k.op(dve, lambda e: e.scalar_tensor_tensor(out=cT[:, cc, 0:T], in0=craw[:, cc, 0:T], scalar=gn[:, 168 + cc:169 + cc], in1=srow[:, 0:T], op0=ALU.mult, op1=ALU.mult), [crawb, srb, B_c], [cTb])
                for (dst, c0, c1) in scr["ckvT"]():
                    k.dma(sp, dst, cT[:, :, c0:c1], [cTb], [scr["buf"]], cTs)
                if kmax is not None:
                    k.op(act, lambda e: e.activation(out=csq[:, :, 0:T], in_=cT[:, :, 0:T], func=AF.Square), [cTb], [csqb])
                    b = next_bank()
                    k.mm([(banks[b][:, 0:T], ones, csq[:, cc, 0:T], cc == 0, cc == 3) for cc in range(4)], [csqb, B_c], [bbuf[b]])
                    k.op(dve, lambda e: e.tensor_reduce(out=kst["tmpc"][:, 0:1], in_=banks[b][:, 0:T], axis=AX.X, op=ALU.max), [bbuf[b]], [kst["tmpb"]])
                    k.op(dve, lambda e: e.tensor_tensor(out=kmx[:, kmax[0]:kmax[0] + 1], in0=kmx[:, kmax[0]:kmax[0] + 1], in1=kst["tmpc"][:, 0:1], op=ALU.max), [kst["tmpb"], kmxb], [kmxb])
            rc = kst["rc"]; rs = kst["rs"]; rcb = kst["rcb"]; rsl = kst["rsl"]
            k.dma(sp, rc[:, 0:T], ropeC_ap, [], [rcb], rsl)
            k.dma(sp, rs[:, 0:T], ropeS_ap, [], [rcb], rsl)
            krf = kst["krf"]; krfb = kst["krfb"]
            def ev_r(g, bank, bb):
                tab = rc if g == 0 else rs
                k.op(dve, lambda e: e.tensor_tensor(out=krf[g][:, 0:T], in0=bank[0:64, 0:T], in1=tab[:, 0:T], op=ALU.mult), [bb, rcb], [krfb[g]])
            gemm_fm("w_kr", range(2), hT, hb, T, ev_r, M=64, gpl=2)
            k.op(dve, lambda e: e.tensor_tensor(out=krf[0][:, 0:T], in0=krf[0][:, 0:T], in1=krf[1][:, 0:T], op=ALU.add), [krfb[0], krfb[1]], [krfb[0]])
            if out is not None:
                k.dma(sp, out["krT"](), krf[0][:, 0:T], [krfb[0]], [B_out], kst["krs"])
            if scr is not None:
                krb16 = kst["krb16"]; krb16b = kst["krb16b"]
                k.op(act, lambda e: e.activation(out=krb16[:, 0:T], in_=krf[0][:, 0:T], func=AF.Copy), [krfb[0]], [krb16b])
                for (dst, c0, c1) in scr["krT"]():
                    k.dma(sp, dst, krb16[:, c0:c1], [krb16b], [scr["buf"]], kst["krs2"])
                if kmax is not None:
                    k.op(act, lambda e: e.activation(out=csq[0:64, 0, 0:T], in_=krb16[:, 0:T], func=AF.Square), [krb16b], [csqb])
                    b = next_bank()
                    k.mm([(banks[b][:, 0:T], ones[0:64, :], csq[0:64, 0, 0:T], True, True)], [csqb, B_c], [bbuf[b]])
                    k.op(dve, lambda e: e.tensor_reduce(out=kst["tmpc"][:, 0:1], in_=banks[b][:, 0:T], axis=AX.X, op=ALU.max), [bbuf[b]], [kst["tmpb"]])
                    k.op(dve, lambda e: e.tensor_tensor(out=kmx[:, kmax[1]:kmax[1] + 1], in0=kmx[:, kmax[1]:kmax[1] + 1], in1=kst["tmpc"][:, 0:1], op=ALU.max), [kst["tmpb"], kmxb], [kmxb])
            ctm = kst["ctm"]; ctmb = kst["ctmb"]; acc = kst["acc"]; accb = kst["accb"]
            k.op(dve, lambda e: e.memset(acc[:], 0.0), [], [accb])

            def tm_group(w, g, evac_tb):
                wt, wb = wload(w, g, 1)
                for tb in range(ntb):
                    b = next_bank()
                    mms = [(banks[b][0:TB, 0:256], hT[:, kk, tb * 128:tb * 128 + TB], wt[:, 0, kk, :], kk == 0, kk == KC - 1) for kk in range(KC)]
                    k.mm(mms, wb + [hb], [bbuf[b]])
                    evac_tb(g, tb, banks[b], bbuf[b])

            def ev_ctm(g, tb, bank, bb):
                k.op(act, lambda e: e.activation(out=ctm[0:TB, tb, g * 256:(g + 1) * 256], in_=bank[0:TB, 0:256], func=AF.Copy), [bb], [ctmb])
                k.op(act, lambda e: e.activation(out=kst["junk"][0:TB, 0:256], in_=bank[0:TB, 0:256], func=AF.Square, accum_out=acc[0:TB, tb * 2 + g:tb * 2 + g + 1]), [bb, accb], [accb, kst["junkb"]])
            for g in range(2):
                tm_group("w_ckvT", g, ev_ctm)
            s1 = kst["s1"]; s1b = kst["s1b"]
            for tb in range(ntb):
                k.op(dve, lambda e: e.tensor_tensor(out=s1[0:TB, 0:1], in0=acc[0:TB, tb * 2:tb * 2 + 1], in1=acc[0:TB, tb * 2 + 1:tb * 2 + 2], op=ALU.add), [accb], [s1b])
                k.op(act, lambda e: e.activation(out=s1[0:TB, 0:1], in_=s1[0:TB, 0:1], func=AF.Sqrt, bias=epsb[0:TB, 0:1], scale=1.0 / 512), [s1b, B_c], [s1b])
                k.op(dve, lambda e: e.reciprocal(out=s1[0:TB, 0:1], in_=s1[0:TB, 0:1]), [s1b], [s1b])
                t, tb_, sl = nxt(kst["f512"])
                k.op(dve, lambda e: e.scalar_tensor_tensor(out=t[0:TB, 0:512], in0=ctm[0:TB, tb, :], scalar=s1[0:TB, 0:1], in1=gkvr[0:TB, :], op0=ALU.mult, op1=ALU.mult), [ctmb, s1b, B_c], [tb_])
                if out is not None:
                    k.dma(sp, out["ckv"](tb), t[0:TB, 0:512], [tb_], [B_out], sl)
                if scr is not None:
                    t2, t2b, sl2 = nxt(kst["bf512"])
                    k.op(act, lambda e: e.activation(out=t2[0:TB, 0:512], in_=t[0:TB, 0:512], func=AF.Copy), [tb_], [t2b])
                    for (dst, p0, p1) in scr["ckv"](tb):
                        k.dma(sp, dst, t2[p0:p1, 0:512], [t2b], [scr["buf"]], sl2)

            def ev_v(g, tb, bank, bb):
                if out is not None:
                    t, tb_, sl = nxt(kst["f512"])
                    k.op(dve, lambda e: e.tensor_copy(out=t[0:TB, 0:256], in_=bank[0:TB, 0:256]), [bb], [tb_])
                    k.dma(sp, out["v"](tb, g), t[0:TB, 0:256], [tb_], [B_out], sl)
                if scr is not None:
                    t2, t2b, sl2 = nxt(kst["bf512"])
                    k.op(act, lambda e: e.activation(out=t2[0:TB, 0:256], in_=bank[0:TB, 0:256], func=AF.Copy), [bb], [t2b])
                    for (dst, p0, p1) in scr["v"](tb, g):
                        k.dma(sp, dst, t2[p0:p1, 0:256].rearrange("p (h d) -> p h d", h=2), [t2b], [scr["buf"]], sl2)
            for g in range(8):
                tm_group("w_sbv", g, ev_v)

        if "PM" in stages:
            a = proj_arena(256)
            T = 256
            front(memT, T, 128, a["hT"], a["hb"], a["xs"], a["xsb"], a["xslots"], a["sq"], a["sqb"], a["rrow"], a["rb"])
            cp(3)
            stf = mk_stage("mstf", [128, 256], F32, 2)

            def ev_mk(h, bank, bb):
                k.op(act, lambda e: e.activation(out=MK[:, h, :], in_=bank[:, 0:256], func=AF.Copy), [bb], [MKb])
            gemm_fm("w_mkF", range(4), a["hT"], a["hb"], T, ev_mk)
            cp(4)
            for (w, dst, tomv) in (("w_mkT", o_memk, False), ("w_mvT", o_memv, True)):
                for g in range(2):
                    wt, wb = wload(w, g, 1)
                    cp(5)
                    for tb in range(2):
                        b = next_bank()
                        mms = [(banks[b][:, 0:256], a["hT"][:, kk, tb * 128:(tb + 1) * 128], wt[:, 0, kk, :], kk == 0, kk == KC - 1) for kk in range(KC)]
                        k.mm(mms, wb + [a["hb"]], [bbuf[b]])
                        cp(6)
                        t, tb_, sl = nxt(stf)
                        k.op(dve, lambda e: e.tensor_copy(out=t[:, 0:256], in_=banks[b][:, 0:256]), [bbuf[b]], [tb_])
                        cp(7)
                        k.dma(sp, dst[tb * 128:(tb + 1) * 128, g * 256:(g + 1) * 256], t[:, 0:256], [tb_], [B_out], sl)
                        cp(8)
                        if tomv:
                            cp(11)
                            k.op(act, lambda e: e.activation(out=MV[:, tb, g * 256:(g + 1) * 256], in_=t[:, 0:256], func=AF.Copy), [tb_], [MKb])
                            cp(12)
                if not tomv:
                    cp(10)
            cp(9)
            k.barrier()

        if "PA" in stages:
            a = proj_arena(512)
            kst = kv_stage(512)
            for t in range(16):
                T = 512; t0 = t * 512
                front(xT_seq[:, :, t0:t0 + T], T, 0, a["hT"], a["hb"], a["xs"], a["xsb"], a["xslots"], a["sq"], a["sqb"], a["rrow"], a["rb"])
                scr = {
                    "buf": B_S,
                    "kT": lambda h: [(S_kT[h, :, t0:t0 + 512], 0, 512)],
                    "ckvT": lambda: [(S_ckvT[:, :, t0:t0 + 512], 0, 512)],
                    "krT": lambda: [(S_krT[:, t0:t0 + 512], 0, 512)],
                    "ckv": lambda tb: [(S_ckv[:, t * 4 + tb, :], 0, 128)],
                    "v": lambda tb, g: [(S_v[2 * g:2 * g + 2, :, t * 4 + tb, :].rearrange("h p d -> p h d"), 0, 128)],
                }
                kv_project(a, T, ropeC_seq[:, t0:t0 + T], ropeS_seq[:, t0:t0 + T], kst, scr=scr, out=None, kmax=(0, 1))
            k.barrier()

        if "PB1" in stages:
            TM = 256
            own_tiles1 = [(i * 256, 256) for i in range(8)] + [(2048, 128)]
            a = proj_arena(TM)
            kst = kv_stage(TM)
            zs = k.slot("zcast")
            for s in range(2):
                k.dma(pool, Z_ckvT[s, :, :, 0:PAST], c_ckvT[s * 128:(s + 1) * 128, :].rearrange("p (c n) -> p c n", c=4), [], [B_Z], zs)
                k.dma(pool, Z_ckv[s, :, 0:32, :], c_ckv[s * 128:(s + 1) * 128, :].rearrange("p (j n) -> p j n", j=32), [], [B_Z], zs)
                k.dma(pool, Z_krT[s, :, 0:PAST], c_krT[s * 64:(s + 1) * 64, :], [], [B_Z], zs)
                k.dma(pool, Z_kT[s, :, :, 0:PAST], c_kT[s * 2048:(s + 1) * 2048, :].rearrange("(h p) n -> h p n", h=16), [], [B_Z], zs)
                k.dma(pool, Z_v[s, :, :, 0:32, :], c_v[s * 2048:(s + 1) * 2048, :].rearrange("(h p) (j d) -> h p j d", h=16, d=128), [], [B_Z], zs)
            cqraw = k.sb("cqraw", [128, 8, TM], F32); cqrawb = Buf("cqraw", True)
            cqsq = k.sb("cqsq", [128, 8, TM], BF16); cqsqb = Buf("cqsq", True)
            cqn = k.sb("cqn", [128, 8, TM], BF16); cqnb = Buf("cqn", True)
            qn = k.sb("qn", [128, 16, TM], BF16); qnb = Buf("qn", True)
            qst = mk_stage("qst", [128, 4, TM], BF16, 2)
            sbqst = mk_stage("sbqst", [128, TM], BF16, 3)
            qr0 = k.sb("qr0", [64, TM], F32); qr1 = k.sb("qr1", [64, TM], F32); qrb = [Buf("qr0"), Buf("qr1")]
            qrst = mk_stage("qrst", [64, TM], BF16, 2)
            wuk = k.sb("wuk", [128, 64, 128], BF16); wukb = Buf("wuk"); wuks = k.slot("wuk")
            k.dma(sp, wuk[:], Wb["w_ukT"].rearrange("p (k n) -> p k n", n=128), [Wbuf["w_ukT"]], [wukb], wuks)
            for (t0, T) in own_tiles1:
                front(xT_own[:, :, t0:t0 + T], T, 0, a["hT"], a["hb"], a["xs"], a["xsb"], a["xslots"], a["sq"], a["sqb"], a["rrow"], a["rb"])
                is_s = (T == 128)
                out = {
                    "kT": lambda h: o_kT[h, :, t0:t0 + T],
                    "krT": lambda: o_krT[:, t0:t0 + T],
                    "ckv": lambda tb: o_ckv[t0 + tb * 128:t0 + tb * 128 + 128, :],
                    "v": lambda tb, g: o_v[t0 + tb * 128:t0 + tb * 128 + 128, g * 256:(g + 1) * 256],
                }
                scr = None
                if is_s:
                    scr = {
                        "buf": B_Z,
                        "kT": lambda h: [(Z_kT[s, h, :, PAST:ZK], s * 64, s * 64 + 64) for s in range(2)],
                        "ckvT": lambda: [(Z_ckvT[s, :, :, PAST:ZK], s * 64, s * 64 + 64) for s in range(2)],
                        "krT": lambda: [(Z_krT[s, :, PAST:ZK], s * 64, s * 64 + 64) for s in range(2)],
                        "ckv": lambda tb: [(Z_ckv[s, 0:64, 32, :], s * 64, s * 64 + 64) for s in range(2)],
                        "v": lambda tb, g: [(Z_v[s, 2 * g:2 * g + 2, 0:64, 32, :].rearrange("h p d -> p h d"), s * 64, s * 64 + 64) for s in range(2)],
                    }
                kv_project(a, T, ropeC_own[:, t0:t0 + T], ropeS_own[:, t0:t0 + T], kst, scr=scr, out=out, kmax=None)
                hT, hb = a["hT"], a["hb"]

                def ev_cq(g, bank, bb):
                    k.op(act, lambda e: e.activation(out=cqraw[:, g, 0:T], in_=bank[:, 0:T], func=AF.Copy), [bb], [cqrawb])
                    k.op(act, lambda e: e.activation(out=cqsq[:, g, 0:T], in_=bank[:, 0:T], func=AF.Square), [bb], [cqsqb])
                gemm_fm("w_cq", range(8), hT, hb, T, ev_cq)
                b = next_bank()
                k.mm([(banks[b][:, 0:T], ones, cqsq[:, g, 0:T], g == 0, g == 7) for g in range(8)], [cqsqb, B_c], [bbuf[b]])
                srow = kst["srow"]; srb = kst["srb"]
                rstd_from_bank(banks[b][:, 0:T], bbuf[b], 1024, srow[:, 0:T], srb)
                for g in range(8):
                    eng = dve
                    k.op(eng, lambda e: e.scalar_tensor_tensor(out=cqn[:, g, 0:T], in0=cqraw[:, g, 0:T], scalar=gn[:, 160 + g:161 + g], in1=srow[:, 0:T], op0=ALU.mult, op1=ALU.mult), [cqrawb, srb, B_c], [cqnb])

                def ev_sbq(h, bank, bb):
                    t, tb_, sl = nxt(sbqst)
                    k.op(act, lambda e: e.activation(out=t[:, 0:T], in_=bank[:, 0:T], func=AF.Copy), [bb], [tb_])
                    k.dma(sp, Q_sb[:, h, t0:t0 + T], t[:, 0:T], [tb_], [B_Q], sl)
                gemm_fm("w_sbq", range(16), hT, hb, T, ev_sbq)

                def ev_qn(h, bank, bb):
                    k.op(act, lambda e: e.activation(out=qn[:, h, 0:T], in_=bank[:, 0:T], func=AF.Copy), [bb], [qnb])
                gemm_fm("w_uqn", range(16), cqn, cqnb, T, ev_qn, gpl=4)
                rc = kst["rc"]; rs = kst["rs"]; rcb = kst["rcb"]

                def ev_qr(g, bank, bb):
                    h, z = g // 2, g % 2
                    if z == 0:
                        k.op(dve, lambda e: e.tensor_tensor(out=qr0[:, 0:T], in0=bank[0:64, 0:T], in1=rc[:, 0:T], op=ALU.mult), [bb, rcb], [qrb[0]])
                    else:
                        k.op(dve, lambda e: e.tensor_tensor(out=qr1[:, 0:T], in0=bank[0:64, 0:T], in1=rs[:, 0:T], op=ALU.mult), [bb, rcb], [qrb[1]])
                        t, tb_, sl = nxt(qrst)
                        k.op(pool, lambda e: e.tensor_tensor(out=t[:, 0:T], in0=qr0[:, 0:T], in1=qr1[:, 0:T], op=ALU.add), [qrb[0], qrb[1]], [tb_])
                        k.dma(sp, Q_rope[:, h, t0:t0 + T], t[:, 0:T], [tb_], [B_Q], sl)
                gemm_fm("w_uqr", range(32), cqn, cqnb, T, ev_qr, M=64, gpl=8)
                for h in range(16):
                    t, tb_, sl = nxt(qst)
                    for cc in range(4):
                        b = next_bank()
                        k.mm([(banks[b][:, 0:T], wuk[:, h * 4 + cc, :], qn[:, h, 0:T], True, True)], [wukb, qnb], [bbuf[b]])
                        if cc % 2 == 0:
                            k.op(act, lambda e: e.activation(out=t[:, cc, 0:T], in_=banks[b][:, 0:T], func=AF.Copy), [bbuf[b]], [tb_])
                        else:
                            k.op(dve, lambda e: e.tensor_copy(out=t[:, cc, 0:T], in_=banks[b][:, 0:T]), [bbuf[b]], [tb_])
                    k.dma(sp, Q_lat[:, :, h, t0:t0 + T], t[:, :, 0:T], [tb_], [B_Q], sl)
            k.barrier()

        if "PB2" in stages:
            k.sb_off = ARENA0
            wuv = k.sb("wuv", [128, 64, 128], BF16); wuvb = Buf("wuv"); wuvs = k.slot("wuv")
            k.dma(sp, wuv[:], Wb["w_uvr"].rearrange("p (k n) -> p k n", n=128), [Wbuf["w_uvr"]], [wuvb], wuvs)
            qlat = k.sb("qlat", [128, 4, 2048], BF16); qrope = k.sb("qrope", [64, 2048], BF16); sbq = k.sb("sbq", [128, 2048], BF16)
            qb_ = Buf("qtiles"); qsl = k.slot("qtiles")
            qsq = k.sb("qsq", [128, 4, 512], BF16); qsqb = Buf("qsq")
            rrow = k.sb("r_row", [128, 512], F32); rrb = Buf("r_row")
            rbf = k.sb("r_bf", [1, 512], BF16); rbfb = Buf("r_bf")
            KT = [dict(ckvT=k.sb(f"ktc{i}", [128, 4, 512], BF16), krT=k.sb(f"ktr{i}", [64, 512], BF16), ckv=k.sb(f"ktv{i}", [128, 4, 512], BF16), b=Buf(f"kt{i}"), s=k.slot(f"kt{i}")) for i in range(2)]
            PT = [(k.sb(f"PT{i}", [128, 512], BF16), Buf(f"PT{i}")) for i in range(2)]
            linv = k.sb("linv", [128, 512], F32); linvb = Buf("linv")
            olat = k.sb("olat", [128, 4, 512], BF16); olatb = Buf("olat", True)
            oast = k.sb("oast", [128, 2048], BF16); oastb = Buf("oast", True); oasl = k.slot("oast")
            obst = k.sb("obst", [128, 2048], BF16); obstb = Buf("obst", True); obsl = k.slot("obst")
            NKMAX = 8704
            SK = [dict(kT=k.sb(f"skT{i}", [128, NKMAX], BF16), v=k.sb(f"sv{i}", [128, 68, 128], BF16), b=Buf(f"sk{i}"), s=k.slot(f"sk{i}")) for i in range(2)]
            ebuf = [(k.sb(f"e{i}", [128, 512], F32), Buf(f"e{i}")) for i in range(2)]
            spb = [(k.sb(f"sp{i}", [128, 512], BF16), Buf(f"sp{i}")) for i in range(2)]
            e2b = [(k.sb(f"e2{i}", [128, 512], F32), Buf(f"e2{i}")) for i in range(2)]
            wTb = [(k.sb(f"wT{i}", [128, 512], BF16), Buf(f"wT{i}")) for i in range(2)]
            carry = [(k.sb(f"carry{i}", [1, 128], BF16), Buf(f"carry{i}")) for i in range(2)]
            km2 = k.sb("km2", [128, 2], F32); km2b = Buf("km2")
            ztmp = k.sb("ztmp", [128, 4, 512], BF16); ztmpb = Buf("ztmp"); ztsl = k.slot("ztmp")

            qsets = []
            for m in range(16):
                tiles = [dict(k0=kt * 512, nb=4, bs=128, diag=(kt == m), jb0=kt * 4) for kt in range(m + 1)]
                qsets.append(dict(t0=m * 128, NQ=128, src="S", s=None, tiles=tiles))
            for s in range(2):
                tiles = [dict(k0=kt * 512, nb=4, bs=128, diag=False, jb0=kt * 4) for kt in range(8)]
                tiles.append(dict(k0=PAST, nb=1, bs=64, diag=True, jb0=32))
                qsets.append(dict(t0=2048 + s * 64, NQ=64, src="Z", s=s, tiles=tiles))

            def kmax_pass(src_ckvT, src_krT, n, ccol, rcol):
                k.dma(sp, ztmp[:, :, 0:n], src_ckvT, [B_Z], [ztmpb], ztsl)
                k.op(act, lambda e: e.activation(out=qsq[:, :, 0:n], in_=ztmp[:, :, 0:n], func=AF.Square), [ztmpb], [qsqb])
                b = next_bank()
                k.mm([(banks[b][:, 0:n], ones, qsq[:, cc, 0:n], cc == 0, cc == 3) for cc in range(4)], [qsqb, B_c], [bbuf[b]])
                k.op(dve, lambda e: e.tensor_reduce(out=km2[:, 0:1], in_=banks[b][:, 0:n], axis=AX.X, op=ALU.max), [bbuf[b]], [km2b])
                k.op(dve, lambda e: e.tensor_tensor(out=kmx[:, ccol:ccol + 1], in0=kmx[:, ccol:ccol + 1], in1=km2[:, 0:1], op=ALU.max), [km2b, kmxb], [kmxb])
                k.dma(sp, ztmp[0:64, 0, 0:n], src_krT, [B_Z], [ztmpb], ztsl)
                k.op(act, lambda e: e.activation(out=qsq[0:64, 0, 0:n], in_=ztmp[0:64, 0, 0:n], func=AF.Square), [ztmpb], [qsqb])
                b = next_bank()
                k.mm([(banks[b][:, 0:n], ones[0:64, :], qsq[0:64, 0, 0:n], True, True)], [qsqb, B_c], [bbuf[b]])
                k.op(dve, lambda e: e.tensor_reduce(out=km2[:, 0:1], in_=banks[b][:, 0:n], axis=AX.X, op=ALU.max), [bbuf[b]], [km2b])
                k.op(dve, lambda e: e.tensor_tensor(out=kmx[:, rcol:rcol + 1], in0=kmx[:, rcol:rcol + 1], in1=km2[:, 0:1], op=ALU.max), [km2b, kmxb], [kmxb])
            for s in range(2):
                for kt in range(8):
                    kmax_pass(Z_ckvT[s, :, :, kt * 512:(kt + 1) * 512], Z_krT[s, :, kt * 512:(kt + 1) * 512], 512, 2 + 2 * s, 3 + 2 * s)
                kmax_pass(Z_ckvT[s, :, :, PAST:ZK], Z_krT[s, :, PAST:ZK], 64, 2 + 2 * s, 3 + 2 * s)

            kt_rr = [0]; pt_rr = [0]; sk_rr = [0]; st_rr = [0]
            for qs in qsets:
                t0, NQ, src, s = qs["t0"], qs["NQ"], qs["src"], qs["s"]
                HG = 512 // NQ; NG = 16 // HG
                scr_buf = B_S if src == "S" else B_Z
                for cc in range(4):
                    k.dma(sp, qlat[:, cc, 0:16 * NQ].rearrange("p (h q) -> p h q", q=NQ), Q_lat[:, cc, :, t0:t0 + NQ], [B_Q], [qb_], qsl)
                k.dma(sp, qrope[:, 0:16 * NQ].rearrange("p (h q) -> p h q", q=NQ), Q_rope[:, :, t0:t0 + NQ], [B_Q], [qb_], qsl)
                k.dma(sp, sbq[:, 0:16 * NQ].rearrange("p (h q) -> p h q", q=NQ), Q_sb[:, :, t0:t0 + NQ], [B_Q], [qb_], qsl)
                ccol, rcol = (0, 1) if src == "S" else (2 + 2 * s, 3 + 2 * s)
                k.op(dve, lambda e: e.tensor_tensor(out=km2[:, 1:2], in0=kmx[:, ccol:ccol + 1], in1=kmx[:, rcol:rcol + 1], op=ALU.add), [kmxb], [km2b])
                for hg in range(NG):
                    c0q = hg * 512
                    qrv = qrope[:, c0q:c0q + 512]
                    k.op(act, lambda e: e.activation(out=qsq[:, :, :], in_=qlat[:, :, c0q:c0q + 512], func=AF.Square), [qb_], [qsqb])
                    b = 7
                    k.mm([(banks[b][:, :], ones, qsq[:, cc, :], cc == 0, False) for cc in range(4)], [qsqb, B_c], [bbuf[b]])
                    k.op(act, lambda e: e.activation(out=qsq[0:64, 0, :], in_=qrv, func=AF.Square), [qb_, bbuf[b]], [qsqb])
                    k.mm([(banks[b][:, :], ones[0:64, :], qsq[0:64, 0, :], False, True)], [qsqb, B_c], [bbuf[b]])
                    k.op(act, lambda e: e.activation(out=rrow[:, :], in_=banks[b][:, :], func=AF.Sqrt, scale=km2[:, 1:2]), [bbuf[b], km2b], [rrb])
                    k.op(dve, lambda e: e.tensor_scalar(out=rbf[0:1, :], in0=rrow[0:1, :], scalar1=-1.02, scalar2=None, op0=ALU.mult), [rrb], [rbfb])
                    first = True
                    ntl = len(qs["tiles"])
                    for ti, tl in enumerate(qs["tiles"]):
                        K_ = KT[kt_rr[0] % 2]; kt_rr[0] += 1
                        nb, bs, k0, jb0 = tl["nb"], tl["bs"], tl["k0"], tl["jb0"]
                        nk = nb * bs
                        if src == "S":
                            k.dma(sp, K_["ckvT"][:, :, 0:nk], S_ckvT[:, :, k0:k0 + nk], [scr_buf], [K_["b"]], K_["s"])
                            k.dma(sp, K_["krT"][:, 0:nk], S_krT[:, k0:k0 + nk], [scr_buf], [K_["b"]], K_["s"])
                            k.dma(sp, K_["ckv"][0:bs, 0:nb, :], S_ckv[0:bs, jb0:jb0 + nb, :], [scr_buf], [K_["b"]], K_["s"])
                        else:
                            k.dma(sp, K_["ckvT"][:, :, 0:nk], Z_ckvT[s, :, :, k0:k0 + nk], [scr_buf], [K_["b"]], K_["s"])
                            k.dma(sp, K_["krT"][:, 0:nk], Z_krT[s, :, k0:k0 + nk], [scr_buf], [K_["b"]], K_["s"])
                            k.dma(sp, K_["ckv"][0:bs, 0:nb, :], Z_ckv[s, 0:bs, jb0:jb0 + nb, :], [scr_buf], [K_["b"]], K_["s"])
                        for j in range(nb):
                            bS = next_bank(0, 2)
                            use_mask = tl["diag"] and src == "S"
                            mms = [(banks[bS][0:bs, :], K_["ckvT"][:, cc, j * bs:(j + 1) * bs], qlat[:, cc, c0q:c0q + 512], cc == 0, False) for cc in range(4)]
                            mms.append((banks[bS][0:bs, :], K_["krT"][:, j * bs:(j + 1) * bs], qrv, False, False))
                            mms.append((banks[bS][0:bs, :], ones[0:1, 0:bs], rbf[0:1, :], False, not use_mask))
                            if use_mask:
                                mms.append((banks[bS][0:bs, :], ident, msk[:, j * 512:(j + 1) * 512], False, True))
                            k.mm(mms, [K_["b"], qb_, rbfb, B_c], [bbuf[bS]])
                            pt, ptb = PT[pt_rr[0] % 2]; pt_rr[0] += 1
                            k.op(act, lambda e: e.activation(out=pt[0:bs, :], in_=banks[bS][0:bs, :], func=AF.Exp, scale=MLA_SCALE), [bbuf[bS]], [ptb])
                            last = (ti == ntl - 1 and j == nb - 1)
                            mms = [(banks[2 + cc][:, :], K_["ckv"][0:bs, j, cc * 128:(cc + 1) * 128], pt[0:bs, :], first, last) for cc in range(4)]
                            mms.append((banks[6][:, :], ones[0:bs, :], pt[0:bs, :], first, last))
                            k.mm(mms, [K_["b"], ptb, B_c], [bbuf[2], bbuf[3], bbuf[4], bbuf[5], bbuf[6]])
                            first = False
                    k.op(dve, lambda e: e.reciprocal(out=linv[:, :], in_=banks[6][:, :]), [bbuf[6]], [linvb])
                    for cc in range(4):
                        k.op(dve, lambda e: e.tensor_tensor(out=olat[:, cc, :], in0=banks[2 + cc][:, :], in1=linv[:, :], op=ALU.mult), [bbuf[2 + cc], linvb], [olatb])
                    b = 7
                    mms = []
                    for hh in range(HG):
                        for cc in range(4):
                            mms.append((banks[b][:, hh * NQ:(hh + 1) * NQ], wuv[:, (hg * HG + hh) * 4 + cc, :], olat[:, cc, hh * NQ:(hh + 1) * NQ], cc == 0, cc == 3))
                    k.mm(mms, [wuvb, olatb], [bbuf[b]])
                    k.op(act, lambda e: e.activation(out=oast[:, c0q:c0q + 512], in_=banks[b][:, :], func=AF.Copy), [bbuf[b]], [oastb])
                k.dma(sp, O_a[:, :, t0:t0 + NQ], oast[:, 0:16 * NQ].rearrange("p (h q) -> p h q", q=NQ), [oastb], [B_O], oasl)
                nkeys = sum(tl["nb"] * tl["bs"] for tl in qs["tiles"])
                for h in range(16):
                    S_ = SK[sk_rr[0] % 2]; sk_rr[0] += 1
                    if src == "S":
                        k.dma(sp, S_["kT"][:, 0:nkeys], S_kT[h, :, 0:nkeys], [scr_buf], [S_["b"]], S_["s"])
                        k.dma(sp, S_["v"][:, 0:nkeys // 128, :], S_v[h, :, 0:nkeys // 128, :], [scr_buf], [S_["b"]], S_["s"])
                    else:
                        k.dma(sp, S_["kT"][:, 0:nkeys], Z_kT[s, h, :, 0:nkeys], [scr_buf], [S_["b"]], S_["s"])
                        k.dma(sp, S_["v"][:, 0:33, :], Z_v[s, h, :, 0:33, :], [scr_buf], [S_["b"]], S_["s"])
                    firstC = True
                    ntl = len(qs["tiles"])
                    cprev = None
                    qh = sbq[:, h * NQ:(h + 1) * NQ]
                    for ti in range(ntl - 1, -1, -1):
                        tl = qs["tiles"][ti]
                        nb, bs, k0, jb0 = tl["nb"], tl["bs"], tl["k0"], tl["jb0"]
                        W_ = nb * NQ
                        i2 = st_rr[0] % 2; st_rr[0] += 1
                        bA = next_bank(0, 2)
                        mms = []
                        for j in range(nb):
                            mms.append((banks[bA][0:bs, j * NQ:(j + 1) * NQ], S_["kT"][:, k0 + j * bs:k0 + (j + 1) * bs], qh, True, not tl["diag"]))
                            if tl["diag"]:
                                if src == "S":
                                    mms.append((banks[bA][0:bs, j * NQ:(j + 1) * NQ], ident, msk[:, 2048 + j * 128:2048 + (j + 1) * 128], False, True))
                                else:
                                    mms.append((banks[bA][0:bs, j * NQ:(j + 1) * NQ], ident[0:64, 0:64], msk[0:64, 2560:2624], False, True))
                        k.mm(mms, [S_["b"], qb_, B_c], [bbuf[bA]])
                        e_, eb = ebuf[i2]; sp_, spb_ = spb[i2]; e2_, e2b_ = e2b[i2]; w_, wb_ = wTb[i2]
                        k.op(act, lambda e: e.activation(out=e_[0:bs, 0:W_], in_=banks[bA][0:bs, 0:W_], func=AF.Exp, scale=SB_SCALE), [bbuf[bA]], [eb])
                        k.op(act, lambda e: e.activation(out=sp_[0:bs, 0:W_], in_=e_[0:bs, 0:W_], func=AF.Ln, bias=1.0, scale=1.0), [eb], [spb_])
                        bB = next_bank(2, 4)
                        mms = [(banks[bB][0:bs, 0:W_], Umat[0:bs, 0:bs], sp_[0:bs, 0:W_], True, False)]
                        for sh in range(1, nb):
                            mms.append((banks[bB][0:bs, 0:(nb - sh) * NQ], ones[0:bs, 0:bs], sp_[0:bs, sh * NQ:W_], False, False))
                        rd = [spb_, B_c]
                        if cprev is not None:
                            for j in range(nb):
                                mms.append((banks[bB][0:bs, j * NQ:(j + 1) * NQ], ones[0:1, 0:bs], cprev[0][0:1, 0:NQ], False, False))
                            rd.append(cprev[1])
                        mms[-1] = mms[-1][:4] + (True,)
                        k.mm(mms, rd, [bbuf[bB]])
                        k.op(act, lambda e: e.activation(out=e2_[0:bs, 0:W_], in_=banks[bB][0:bs, 0:W_], func=AF.Exp, scale=-1.0), [bbuf[bB]], [e2b_])
                        cnew = carry[i2]
                        if ti > 0:
                            k.op(dve, lambda e: e.tensor_copy(out=cnew[0][0:1, 0:NQ], in_=banks[bB][0:1, 0:NQ]), [bbuf[bB]], [cnew[1]])
                        k.op(pool, lambda e: e.tensor_tensor(out=w_[0:bs, 0:W_], in0=e_[0:bs, 0:W_], in1=e2_[0:bs, 0:W_], op=ALU.mult), [eb, e2b_], [wb_])
                        mms = []
                        for j in range(nb):
                            lastC = (ti == 0 and j == nb - 1)
                            mms.append((banks[4][:, 0:NQ], S_["v"][0:bs, jb0 + j, :], w_[0:bs, j * NQ:(j + 1) * NQ], firstC, lastC))
                            firstC = False
                        k.mm(mms, [S_["b"], wb_], [bbuf[4]])
                        cprev = cnew
                    k.op(act, lambda e: e.activation(out=obst[:, h * NQ:(h + 1) * NQ], in_=banks[4][:, 0:NQ], func=AF.Copy), [bbuf[4]], [obstb])
                k.dma(sp, O_b[:, :, t0:t0 + NQ], obst[:, 0:16 * NQ].rearrange("p (h q) -> p h q", q=NQ), [obstb], [B_O], obsl)
            k.barrier()

        own_tiles = [(0, 512), (512, 512), (1024, 512), (1536, 512), (2048, 128)]
        if "PB3" in stages:
            A0 = ARENA
            RA = Buf("p3_A", True)
            hT = k.sb("p3_hT", [128, KC, 512], BF16, off=A0)
            oa = k.sb("p3_oa", [128, 16, 512], BF16, off=A0 + 32768); ob = k.sb("p3_ob", [128, 16, 512], BF16, off=A0 + 49152)
            x1 = k.sb("p3_x1", [128, KC, 512], F32, off=A0)
            oasl2 = k.slot("p3_o"); x1sl = k.slot("p3_x1")
            mg = k.sb("p3_mg", [128, KC, 512], BF16, off=A0 + 65536); RB = Buf("p3_B", True)
            h2 = mg
            R0 = A0 + 98304
            RC = Buf("p3_C", True)
            ff = k.sb("p3_ff", [128, 22, 512], BF16, off=R0)
            xs = [k.sb(f"p3_xs{i}", [128, 4, 512], F32, off=R0 + i * 8192) for i in range(2)]
            sqf = [k.sb(f"p3_sqf{i}", [128, 4, 512], BF16, off=R0 + 16384 + i * 4096) for i in range(2)]
            xslots = [k.slot(f"p3xs{i}") for i in range(2)]
            mqT = k.sb("p3_mqT", [128, 4, 512], BF16, off=R0); atT = k.sb("p3_atT", [128, 4, 512], BF16, off=R0 + 4096)
            PTt = k.sb("p3_PT", [128, 2, 4, 512], BF16, off=R0 + 8192)
            Pn = [k.sb(f"p3_P{i}", [128, 256], BF16, off=R0 + 16384 + i * 512) for i in range(2)]
            ZMK = k.sb("p3_ZMK", [128, 2, 4, 256], BF16, off=R0 + 20480); ZMV = k.sb("p3_ZMV", [128, 2, 2, 512], BF16, off=R0 + 24576)
            k.sb_off = R0 + 32768
            sq = [k.sb(f"p3_sq{i}", [128, 4, 512], BF16) for i in range(2)]; sqb = [Buf("p3sq0"), Buf("p3sq1")]
            rrow = k.sb("p3_rrow", [128, 512], F32); rb = Buf("p3_rrow")
            ga = [(k.sb(f"p3_ga{i}", [128, 512], F32), Buf(f"p3_ga{i}")) for i in range(2)]
            tt = [(k.sb(f"p3_tt{i}", [128, 512], F32), Buf(f"p3_tt{i}")) for i in range(2)]
            st4 = [(k.sb(f"p3_st{i}", [128, 4], F32), Buf(f"p3_st{i}")) for i in range(2)]
            zms = k.slot("zm")
            yst = mk_stage("p3_y", [128, 512], F32, 3)
            bank_bf = [banks[i].bitcast(BF16) for i in range(8)]
            rr3 = [0]
            for (t0, T) in own_tiles:
                front(xT_own[:, :, t0:t0 + T], T, 0, hT, RA, xs, [RC, RC], xslots, sqf, [RC, RC], rrow, rb, NP=4)
                k.dma(sp, oa[:, :, 0:T], O_a[:, :, t0:t0 + T], [B_O], [RA], oasl2)
                k.dma(sp, ob[:, :, 0:T], O_b[:, :, t0:t0 + T], [B_O], [RA], oasl2)
                for j in range(32):
                    res = []
                    for (gidx, wbn, osrc, bcol) in ((j, "w_ba", oa, 172 + j), (32 + j, "w_bb", ob, 204 + j)):
                        wt, wb = wload("w_gate", gidx, 1)
                        bG = next_bank()
                        k.mm([(banks[bG][:, 0:T], wt[:, 0, kk, :], hT[:, kk, 0:T], kk == 0, kk == KC - 1) for kk in range(KC)], wb + [RA], [bbuf[bG]])
                        g_, gb_ = ga[rr3[0] % 2]
                        k.op(act, lambda e: e.activation(out=g_[:, 0:T], in_=banks[bG][:, 0:T], func=AF.Sigmoid, bias=gn[:, bcol:bcol + 1], scale=1.0), [bbuf[bG], B_c], [gb_])
                        wt2, wb2 = wload(wbn, j, 1)
                        bB = next_bank()
                        k.mm([(banks[bB][:, 0:T], wt2[:, 0, kk, :], osrc[:, kk, 0:T], kk == 0, kk == 15) for kk in range(16)], wb2 + [RA], [bbuf[bB]])
                        t_, tb_ = tt[rr3[0] % 2]; rr3[0] += 1
                        k.op(dve, lambda e: e.tensor_tensor(out=t_[:, 0:T], in0=banks[bB][:, 0:T], in1=g_[:, 0:T], op=ALU.mult), [bbuf[bB], gb_], [tb_])
                        res.append((t_, tb_))
                    k.op(pool, lambda e: e.tensor_tensor(out=mg[:, j, 0:T], in0=res[0][0][:, 0:T], in1=res[1][0][:, 0:T], op=ALU.add), [res[0][1], res[1][1]], [RB])
                for pc in range(4):
                    k.dma(sp, x1[:, pc * 8:(pc + 1) * 8, 0:T], xT_own[:, pc * 8:(pc + 1) * 8, t0:t0 + T], [], [RA], x1sl)

                def ev_res(j, bank, bb):
                    k.op(dve, lambda e: e.tensor_tensor(out=x1[:, j, 0:T], in0=bank[:, 0:T], in1=x1[:, j, 0:T], op=ALU.add), [bb, RA], [RA])
                gemm_fm("w_out", range(32), mg, RB, T, ev_res)
                norm_sb(x1, RA, T, 32, h2, RB, sq, sqb, rrow, rb, NP=4)

                def ev_mq(h, bank, bb):
                    k.op(act, lambda e: e.activation(out=mqT[:, h, 0:T], in_=bank[:, 0:T], func=AF.Copy), [bb], [RC])
                gemm_fm("w_mq", range(4), h2, RB, T, ev_mq)
                if T == 512:
                    msets = [(tb * 128, 128, None) for tb in range(4)]
                else:
                    k.dma(pool, ZMK[:], c_memkT.rearrange("p (s h m) -> p s h m", s=2, h=4), [], [RC], zms)
                    k.dma(pool, ZMV[:], c_memv.rearrange("p (s j n) -> p s j n", s=2, j=2), [], [RC], zms)
                    msets = [(s * 64, 64, s) for s in range(2)]
                for (c0, nq, s) in msets:
                    for h in range(4):
                        bS = next_bank(0, 4)
                        krhs = MK[:, h, :] if s is None else ZMK[:, s, h, :]
                        k.mm([(banks[bS][0:nq, 0:256], mqT[:, h, c0:c0 + nq], krhs, True, True)], [RC, MKb], [bbuf[bS]])
                        s4, s4b = st4[rr3[0] % 2]; p_ = Pn[rr3[0] % 2]; rr3[0] += 1
                        k.op(dve, lambda e: e.memset(s4[:, :], 0.0), [], [s4b])
                        k.op(dve, lambda e: e.tensor_reduce(out=s4[0:nq, 0:1], in_=banks[bS][0:nq, 0:256], axis=AX.X, op=ALU.max), [bbuf[bS], s4b], [s4b])
                        k.op(dve, lambda e: e.tensor_scalar(out=s4[0:nq, 1:2], in0=s4[0:nq, 0:1], scalar1=-MEM_SCALE, scalar2=None, op0=ALU.mult), [s4b], [s4b])
                        k.op(act, lambda e: e.activation(out=p_[0:nq, :], in_=banks[bS][0:nq, 0:256], func=AF.Exp, bias=s4[0:nq, 1:2], scale=MEM_SCALE, accum_out=s4[0:nq, 2:3]), [bbuf[bS], s4b, RC], [RC, s4b])
                        k.op(dve, lambda e: e.reciprocal(out=s4[0:nq, 3:4], in_=s4[0:nq, 2:3]), [s4b], [s4b])
                        k.op(dve, lambda e: e.tensor_scalar(out=p_[0:nq, :], in0=p_[0:nq, :], scalar1=s4[0:nq, 3:4], scalar2=None, op0=ALU.mult), [s4b, RC], [RC])
                        for jb in range(2):
                            bT = next_bank(4, 8)
                            k.op(pe, lambda e: e.transpose(out=bank_bf[bT][:, 0:nq], in_=p_[0:nq, jb * 128:(jb + 1) * 128], identity=ident[0:nq, 0:nq]), [RC, B_c], [bbuf[bT]])
                            k.op(act, lambda e: e.activation(out=PTt[:, jb, h, c0:c0 + nq], in_=bank_bf[bT][:, 0:nq], func=AF.Copy), [bbuf[bT]], [RC])
                for (c0, nq, s) in (msets if T != 512 else [(0, 512, None)]):
                    for h in range(4):
                        b = next_bank()
                        mms = []
                        for jb in range(2):
                            vl = MV[:, jb, h * 128:(h + 1) * 128] if s is None else ZMV[:, s, jb, h * 128:(h + 1) * 128]
                            mms.append((banks[b][:, 0:nq], vl, PTt[:, jb, h, c0:c0 + nq], jb == 0, jb == 1))
                        k.mm(mms, [RC, MKb], [bbuf[b]])
                        k.op(act, lambda e: e.activation(out=atT[:, h, c0:c0 + nq], in_=banks[b][:, 0:nq], func=AF.Copy), [bbuf[b]], [RC])
                gemm_fm("w_mo", range(32), atT, RC, T, ev_res, gpl=8)
                norm_sb(x1, RA, T, 64, h2, RB, sq, sqb, rrow, rb, NP=4)
                f0 = 0
                for nq_ in (22, 22, 21, 21):
                    for fl in range(nq_):
                        f = f0 + fl
                        wt, wb = wload("w_fg", f, 1)
                        bG = next_bank()
                        k.mm([(banks[bG][:, 0:T], wt[:, 0, kk, :], h2[:, kk, 0:T], kk == 0, kk == KC - 1) for kk in range(KC)], wb + [RB], [bbuf[bG]])
                        g_, gb_ = ga[rr3[0] % 2]; rr3[0] += 1
                        k.op(act, lambda e: e.activation(out=g_[:, 0:T], in_=banks[bG][:, 0:T], func=AF.Silu), [bbuf[bG]], [gb_])
                        wt2, wb2 = wload("w_fu", f, 1)
                        bU = next_bank()
                        k.mm([(banks[bU][:, 0:T], wt2[:, 0, kk, :], h2[:, kk, 0:T], kk == 0, kk == KC - 1) for kk in range(KC)], wb2 + [RB], [bbuf[bU]])
                        k.op(dve, lambda e: e.tensor_tensor(out=ff[:, fl, 0:T], in0=banks[bU][:, 0:T], in1=g_[:, 0:T], op=ALU.mult), [bbuf[bU], gb_], [RC])
                    gemm_fm("w_fd", range(32), ff, RC, T, ev_res, kc0=f0, nk=nq_)
                    f0 += nq_
                b = next_bank()
                for pc in range(8):
                    s_ = pc % 2
                    k.op(act, lambda e: e.activation(out=sq[s_][:, :, 0:T], in_=x1[:, pc * 4:(pc + 1) * 4, 0:T], func=AF.Square), [RA], [sqb[s_]])
                    k.mm([(banks[b][:, 0:T], ones, sq[s_][:, j, 0:T], pc == 0 and j == 0, pc == 7 and j == 3) for j in range(4)], [sqb[s_], B_c], [bbuf[b]])
                rstd_from_bank(banks[b][:, 0:T], bbuf[b], D, rrow[:, 0:T], rb)
                for kc in range(KC):
                    t, tb_, sl = nxt(yst)
                    eng = dve
                    k.op(eng, lambda e: e.scalar_tensor_tensor(out=t[:, 0:T], in0=x1[:, kc, 0:T], scalar=gn[:, 96 + kc:97 + kc], in1=rrow[:, 0:T], op0=ALU.mult, op1=ALU.mult), [RA, rb, B_c], [tb_])
                    k.dma(sp, o_yT[:, kc, t0:t0 + T], t[:, 0:T], [tb_], [B_out], sl)

    except _Stop:
        pass
    k.slots = k.all_slots
    k.barrier()
    blk.__exit__(None, None, None)
    return k


def _fm(w, N=128):
    Kd, NC = w.shape
    kc = Kd // 128; G = NC // N
    return np.ascontiguousarray(w.reshape(kc, 128, G, N).transpose(2, 1, 0, 3)).reshape(G * 128, kc * N)


def prep_weights(inp, need):
    w_in = inp["w_in"][0]
    out = {}
    def put(name, arr):
        if name in need:
            out[name] = arr
    if any(n in need for n in ("w_cq", "w_ckvF", "w_ckvT", "w_kr", "w_sbq", "w_sbk", "w_sbv", "w_gate")):
        put("w_cq", _fm(w_in[:, 0:1024])); put("w_ckvF", _fm(w_in[:, 1024:1536])); put("w_ckvT", _fm(w_in[:, 1024:1536], 256))
        kr = w_in[:, 1536:1600]
        krz = np.concatenate([kr[:, 32:64], kr[:, 0:32]], axis=1)
        put("w_kr", _fm(np.concatenate([kr, krz], axis=1), 64))
        put("w_sbq", _fm(w_in[:, 1600:3648])); put("w_sbk", _fm(w_in[:, 3648:5696])); put("w_sbv", _fm(w_in[:, 5696:7744], 256))
        put("w_gate", _fm(w_in[:, 7744:15936]))
    if "w_uqn" in need:
        wuq = inp["w_uq"][0]
        put("w_uqn", _fm(np.ascontiguousarray(wuq[:, :, 0:128]).reshape(1024, 2048)))
        r = wuq[:, :, 128:192]
        rz = np.concatenate([r[:, :, 32:64], r[:, :, 0:32]], axis=2)
        put("w_uqr", _fm(np.ascontiguousarray(np.stack([r, rz], axis=2)).reshape(1024, 16 * 2 * 64), 64))
    if "w_ukT" in need:
        wuk = inp["w_uk"][0]
        put("w_ukT", np.ascontiguousarray(wuk.reshape(4, 128, 16, 128).transpose(3, 2, 0, 1)).reshape(128, 64 * 128))
    if "w_uvr" in need:
        wuv = inp["w_uv"][0]
        put("w_uvr", np.ascontiguousarray(wuv.reshape(4, 128, 16, 128).transpose(1, 2, 0, 3)).reshape(128, 64 * 128))
    for nm, key in (("w_ba", "w_branch_a"), ("w_bb", "w_branch_b"), ("w_out", "w_out"), ("w_mq", "w_mq"), ("w_mkF", "w_mk"),
                    ("w_mo", "w_mo"), ("w_fg", "w_gate"), ("w_fu", "w_up"), ("w_fd", "w_down")):
        if nm in need:
            put(nm, _fm(inp[key][0]))
    if "w_mkT" in need:
        put("w_mkT", _fm(inp["w_mk"][0], 256)); put("w_mvT", _fm(inp["w_mv"][0], 256))
    return out


def rope_tabs(pos):
    half = 32
    inv = (10000.0 ** (-np.arange(half, dtype=np.float32) / half)).astype(np.float32)
    ang = pos.astype(np.float32)[:, None] * inv[None, :]
    cos = np.cos(ang).astype(np.float32).T; sin = np.sin(ang).astype(np.float32).T
    return np.ascontiguousarray(np.concatenate([cos, cos], 0)), np.ascontiguousarray(np.concatenate([-sin, sin], 0))


def prep_core(inp, core, stages, shared):
    b, c = core // 4, core % 4
    m = {}
    m.update(shared)
    xp = inp["x_prompt"][b]
    own_pos = np.concatenate([np.arange((4 * mm + c) * 128, (4 * mm + c + 1) * 128) for mm in range(16)])
    xs = inp["x_sample"][2 * core:2 * core + 2].reshape(128, D)
    xo = np.concatenate([xp[own_pos], xs], axis=0)
    m["xT_own"] = np.ascontiguousarray(xo.T.reshape(KC, 128, NOWN).transpose(1, 0, 2))
    pos_all = np.concatenate([own_pos, PAST + np.arange(64), PAST + np.arange(64)])
    m["ropeC_own"], m["ropeS_own"] = rope_tabs(pos_all)
    mk = np.zeros((128, 4 * 512 + 4 * 128 + 64), np.float32)
    kk = np.arange(128)[:, None]; qq = np.arange(128)[None, :]
    for j in range(4):
        kpos = j * 128 + kk; qpos = c * 128 + qq
        mla = np.where((kpos // 64) <= (qpos // 64), 0.0, NEG).astype(np.float32)
        mk[:, j * 512:(j + 1) * 512] = np.tile(mla, (1, 4))
        mk[:, 2048 + j * 128:2048 + (j + 1) * 128] = np.where(kpos < qpos, 0.0, NEG)
    mk[0:64, 2560:2624] = np.where(np.arange(64)[:, None] < np.arange(64)[None, :], 0.0, NEG)
    m["masks"] = mk
    if "PA" in stages:
        m["xT_seq"] = np.ascontiguousarray(xp.T.reshape(KC, 128, SEQ).transpose(1, 0, 2))
        m["ropeC_seq"], m["ropeS_seq"] = rope_tabs(np.arange(SEQ))
    if "PM" in stages:
        m["memT"] = np.ascontiguousarray(inp["mem_prompt"][b].T.reshape(KC, 128, 256).transpose(1, 0, 2))
    if "PB1" in stages:
        sl = slice(2 * core, 2 * core + 2)
        ck = inp["cache_mla_ckv"][0, sl]
        m["c_ckvT"] = np.ascontiguousarray(ck.reshape(2, PAST, 4, 128).transpose(0, 3, 2, 1)).reshape(256, 4 * PAST)
        m["c_ckv"] = np.ascontiguousarray(ck.reshape(2, 32, 128, 512).transpose(0, 2, 1, 3)).reshape(256, 32 * 512)
        m["c_krT"] = np.ascontiguousarray(inp["cache_mla_krope"][0, sl].transpose(0, 2, 1)).reshape(128, PAST)
        m["c_kT"] = np.ascontiguousarray(inp["cache_sb_k"][0, sl].transpose(0, 2, 3, 1)).reshape(2 * 16 * 128, PAST)
        m["c_v"] = np.ascontiguousarray(inp["cache_sb_v"][0, sl].reshape(2, 32, 128, 16, 128).transpose(0, 3, 2, 1, 4)).reshape(2 * 16 * 128, 32 * 128)
    if "PB3" in stages:
        sl = slice(2 * core, 2 * core + 2)
        mkc = inp["cache_mem_k"][0, sl]
        m["c_memkT"] = np.ascontiguousarray(mkc.transpose(3, 0, 2, 1)).reshape(128, 2 * 4 * 256)
        mvc = inp["cache_mem_v"][0, sl].reshape(2, 2, 128, 512)
        m["c_memv"] = np.ascontiguousarray(mvc.transpose(2, 0, 1, 3)).reshape(128, 2 * 2 * 512)
    return m


def prep_shared(inp, stages):
    need = []
    for s in stages:
        need += STAGE_W[s]
    sh = prep_weights(inp, set(need))
    cst = np.zeros((128, 1024), np.float32)
    cst[:, 0:128] = np.eye(128, dtype=np.float32)
    cst[:, 128:256] = (np.arange(128)[:, None] >= np.arange(128)[None, :]).astype(np.float32)
    cst[:, 256:384] = 1.0
    cst[0, 384] = 1.0
    sh["consts"] = cst
    g = np.zeros((128, 256), np.float32)
    def col(v):
        return v.reshape(-1, 128).T
    g[:, 0:32] = col(inp["g_mix"][0]); g[:, 32:64] = col(inp["g_xattn"][0]); g[:, 64:96] = col(inp["g_ffn"][0])
    g[:, 96:128] = col(inp["g_final"]); g[:, 128:160] = col(inp["g_mem"][0]); g[:, 160:168] = col(inp["g_q_lat"][0])
    g[:, 168:172] = col(inp["g_kv_lat"][0]); g[:, 172:236] = col(inp["b_gate"][0])
    sh["gains"] = g
    sh["gkv_row"] = np.ascontiguousarray(np.tile(inp["g_kv_lat"][0][None, :], (128, 1)))
    return sh


_CACHE = {}


def run(inp, stages=ALL_STAGES, cores=tuple(range(8))):
    inp = {kk: np.asarray(v) for kk, v in inp.items()}
    key = tuple(stages)
    if key not in _CACHE:
        _CACHE[key] = build(stages)
    kb = _CACHE[key]
    shared = prep_shared(inp, stages)
    maps = []
    for core in cores:
        mm = prep_core(inp, core, stages, shared)
        maps.append({n: np.ascontiguousarray(mm[n], dtype=np.float32) for n in kb.inputs})
    res = run_bass_kernel_spmd(kb.nc, maps, core_ids=list(range(len(cores))))
    return res.results


def assemble(results, cores=tuple(range(8))):
    y_p = np.zeros((2, SEQ, D), np.float32); y_s = np.zeros((16, 64, D), np.float32)
    p_ckv = np.zeros((1, 2, SEQ, 512), np.float32); p_kr = np.zeros((1, 2, SEQ, 64), np.float32)
    p_k = np.zeros((1, 2, SEQ, 16, 128), np.float32); p_v = np.zeros((1, 2, SEQ, 16, 128), np.float32)
    p_mk = np.zeros((1, 2, 256, 4, 128), np.float32); p_mv = np.zeros((1, 2, 256, 4, 128), np.float32)
    s_ckv = np.zeros((1, 16, 64, 512), np.float32); s_kr = np.zeros((1, 16, 64, 64), np.float32)
    s_k = np.zeros((1, 16, 64, 16, 128), np.float32); s_v = np.zeros((1, 16, 64, 16, 128), np.float32)
    for i, core in enumerate(cores):
        r = results[i]
        b, c = core // 4, core % 4
        own_pos = np.concatenate([np.arange((4 * mm + c) * 128, (4 * mm + c + 1) * 128) for mm in range(16)])
        y = r["o_yT"].transpose(2, 1, 0).reshape(NOWN, D)
        y_p[b, own_pos] = y[:NPO]; y_s[2 * core:2 * core + 2] = y[NPO:].reshape(2, 64, D)
        ck = r["o_ckv"]; p_ckv[0, b, own_pos] = ck[:NPO]; s_ckv[0, 2 * core:2 * core + 2] = ck[NPO:].reshape(2, 64, 512)
        kr = r["o_krT"].T; p_kr[0, b, own_pos] = kr[:NPO]; s_kr[0, 2 * core:2 * core + 2] = kr[NPO:].reshape(2, 64, 64)
        kT = r["o_kT"].transpose(2, 0, 1); p_k[0, b, own_pos] = kT[:NPO]; s_k[0, 2 * core:2 * core + 2] = kT[NPO:].reshape(2, 64, 16, 128)
        v = r["o_v"].reshape(NOWN, 16, 128); p_v[0, b, own_pos] = v[:NPO]; s_v[0, 2 * core:2 * core + 2] = v[NPO:].reshape(2, 64, 16, 128)
        if c == 0:
            p_mk[0, b] = r["o_memk"].reshape(256, 4, 128); p_mv[0, b] = r["o_memv"].reshape(256, 4, 128)
    return (y_p, y_s, p_ckv, p_kr, p_k, p_v, p_mk, p_mv, s_ckv, s_kr, s_k, s_v)


def kernel(**inputs):
    results = run(inputs)
    return assemble(results)
```

```python
import numpy as np
import concourse.bass as bass
import concourse.mybir as mybir
from concourse.bass_utils import run_bass_kernel_spmd

F32 = mybir.dt.float32
BF16 = mybir.dt.bfloat16
AF = mybir.ActivationFunctionType
ALU = mybir.AluOpType
AX = mybir.AxisListType

D = 4096; KC = 32; SEQ = 8192; NOWN = 2176; NPO = 2048
EPS = 1e-6
MLA_SCALE = 192 ** -0.5
SB_SCALE = 128 ** -0.5
MEM_SCALE = 128 ** -0.5
NEG = -30000.0
PAST = 4096; ZK = 4160
SB_LIMIT = 229376
SB_BASE = 16384 + 512

ALL_STAGES = ("PM", "PA", "PB1", "PB2", "PB3")


class Buf:
    def __init__(self, name, multi=False, excl=False):
        self.name = name; self.w = {}; self.r = {}; self.multi = multi; self.excl = excl


class Eng:
    def __init__(self, nc, obj, name, is_pe=False):
        self.obj = obj; self.sem = nc.alloc_semaphore("e_" + name); self.cnt = 0
        self.seen = {}; self.is_pe = is_pe; self.name = name

    def wait_map(self, m):
        for sem, val in m.items():
            if self.is_pe and sem is self.sem:
                continue
            if self.seen.get(sem, 0) >= val:
                continue
            self.obj.wait_ge(sem, val)
            self.seen[sem] = val


class Slot:
    def __init__(self, nc, name):
        self.sem = nc.alloc_semaphore("d_" + name); self.cnt = 0


class K:
    def __init__(self, stages):
        self.stages = stages
        nc = self.nc = bass.Bass("TRN2", target_bir_lowering=False)
        self.pe = Eng(nc, nc.tensor, "pe", True)
        self.act = Eng(nc, nc.scalar, "act")
        self.dve = Eng(nc, nc.vector, "dve")
        self.pool = Eng(nc, nc.gpsimd, "pool")
        self.sp = Eng(nc, nc.sync, "sp")
        self.engs = [self.pe, self.act, self.dve, self.pool, self.sp]
        self.sb_off = SB_BASE
        self.inputs = {}
        self.outputs = {}
        self.slots = []
        self.all_slots = []
        self.nname = 0

    def _deps(self, eng, reads, writes):
        for b in reads:
            eng.wait_map(b.w)
            if b.excl:
                eng.wait_map({s_: v_ for s_, v_ in b.r.items() if s_ is not eng.sem})
        for b in writes:
            if not b.multi:
                eng.wait_map(b.w)
            eng.wait_map(b.r)

    def _record(self, sem, val, reads, writes):
        for b in reads:
            if b.r.get(sem, 0) < val:
                b.r[sem] = val
        for b in writes:
            if b.multi:
                if b.w.get(sem, 0) < val:
                    b.w[sem] = val
            else:
                b.w = {sem: val}; b.r = {}

    def op(self, eng, fn, reads=(), writes=()):
        self._deps(eng, reads, writes)
        ins = fn(eng.obj)
        eng.cnt += 1
        ins.then_inc(eng.sem, 1)
        self._record(eng.sem, eng.cnt, reads, writes)

    def mm(self, mms, reads, writes):
        eng = self.pe
        self._deps(eng, reads, writes)
        ins = None
        for (o, l, r, st, sp_) in mms:
            ins = eng.obj.matmul(o, l, r, start=st, stop=sp_)
        eng.cnt += 1
        ins.then_inc(eng.sem, 1)
        self._record(eng.sem, eng.cnt, reads, writes)

    def slot(self, name=None):
        s = Slot(self.nc, f"{name or 's'}_{len(self.all_slots)}")
        if not (name or "").startswith("wc_"):
            self.slots.append(s)
        self.all_slots.append(s)
        return s

    def dma(self, q, out, in_, reads, writes, slot):
        self._deps(q, reads, writes)
        ins = q.obj.dma_start(out=out, in_=in_)
        slot.cnt += 16
        ins.then_inc(slot.sem, 16)
        self._record(slot.sem, slot.cnt, reads, writes)

    def barrier(self):
        m = {}
        for e in self.engs:
            if e.cnt:
                m[e.sem] = e.cnt
        for s in self.slots:
            if s.cnt:
                m[s.sem] = s.cnt
        for e in self.engs:
            mm = dict(m)
            mm.pop(e.sem, None) if e.is_pe else None
            e.wait_map(mm)

    def sb(self, name, shape, dt, off=None):
        sz = int(np.prod(shape[1:])) * (4 if dt == F32 else 2)
        if off is None:
            off = self.sb_off
            self.sb_off += (sz + 63) // 64 * 64
        assert off + sz <= SB_LIMIT, (name, off, sz)
        self.nname += 1
        return self.nc.alloc_sbuf_tensor_at(f"{name}_{self.nname}", list(shape), dt, offset=off)

    def din(self, name, shape, dt=F32):
        t = self.nc.dram_tensor(name, list(shape), dt, kind="ExternalInput")
        self.inputs[name] = (tuple(shape), dt)
        return t.ap()

    def dout(self, name, shape, dt=F32):
        t = self.nc.dram_tensor(name, list(shape), dt, kind="ExternalOutput")
        self.outputs[name] = (tuple(shape), dt)
        return t.ap()

    def dscr(self, name, shape, dt=BF16):
        return self.nc.dram_tensor(name, list(shape), dt).ap()


def weight_table():
    return {
        "w_cq": (8, 32, 128), "w_ckvF": (4, 32, 128), "w_ckvT": (2, 32, 256), "w_kr": (2, 32, 64),
        "w_sbq": (16, 32, 128), "w_sbk": (16, 32, 128), "w_sbv": (8, 32, 256), "w_gate": (64, 32, 128),
        "w_uqn": (16, 8, 128), "w_uqr": (32, 8, 64), "w_ukT": (1, 64, 128), "w_uvr": (1, 64, 128),
        "w_ba": (32, 16, 128), "w_bb": (32, 16, 128), "w_out": (32, 32, 128),
        "w_mq": (4, 32, 128), "w_mkF": (4, 32, 128), "w_mkT": (2, 32, 256), "w_mvT": (2, 32, 256),
        "w_mo": (32, 4, 128), "w_fg": (86, 32, 128), "w_fu": (86, 32, 128), "w_fd": (32, 86, 128),
    }


STAGE_W = {
    "PM": ["w_mkF", "w_mkT", "w_mvT"],
    "PA": ["w_ckvF", "w_ckvT", "w_kr", "w_sbk", "w_sbv"],
    "PB1": ["w_ckvF", "w_ckvT", "w_kr", "w_sbk", "w_sbv", "w_cq", "w_sbq", "w_uqn", "w_uqr", "w_ukT"],
    "PB2": ["w_uvr"],
    "PB3": ["w_gate", "w_ba", "w_bb", "w_out", "w_mq", "w_mo", "w_fg", "w_fu", "w_fd"],
}


STOP_AT = 0


class _Stop(Exception):
    pass


def cp(n):
    if STOP_AT == n:
        raise _Stop()


def build(stages=ALL_STAGES):
    k = K(stages)
    nc = k.nc
    pe, act, dve, pool, sp = k.pe, k.act, k.dve, k.pool, k.sp
    WT = weight_table()
    need_w = []
    for s in stages:
        for w in STAGE_W[s]:
            if w not in need_w:
                need_w.append(w)

    Wf = {}; Wb = {}; Wbuf = {}
    for w in need_w:
        G, kcw, N = WT[w]
        Wf[w] = k.din(w, [G * 128, kcw * N])
        Wb[w] = k.dscr(w + "_bf", [G * 128, kcw * N])
        Wbuf[w] = Buf(w, multi=True)
    consts = k.din("consts", [128, 1024])
    gains = k.din("gains", [128, 256])
    gkv_row = k.din("gkv_row", [128, 512])
    NMSK = 4 * 512 + 4 * 128 + 64
    masks = k.din("masks", [128, NMSK])
    xT_own = k.din("xT_own", [128, KC, NOWN])
    ropeC_own = k.din("ropeC_own", [64, NOWN]); ropeS_own = k.din("ropeS_own", [64, NOWN])
    if "PA" in stages:
        xT_seq = k.din("xT_seq", [128, KC, SEQ])
        ropeC_seq = k.din("ropeC_seq", [64, SEQ]); ropeS_seq = k.din("ropeS_seq", [64, SEQ])
    if "PM" in stages:
        memT = k.din("memT", [128, KC, 256])
    if "PB1" in stages:
        c_ckvT = k.din("c_ckvT", [2 * 128, 4 * PAST]); c_ckv = k.din("c_ckv", [2 * 128, 32 * 512])
        c_krT = k.din("c_krT", [2 * 64, PAST]); c_kT = k.din("c_kT", [2 * 16 * 128, PAST])
        c_v = k.din("c_v", [2 * 16 * 128, 32 * 128])
    if "PB3" in stages:
        c_memkT = k.din("c_memkT", [128, 2 * 4 * 256]); c_memv = k.din("c_memv", [128, 2 * 2 * 512])

    o_yT = k.dout("o_yT", [128, KC, NOWN])
    o_ckv = k.dout("o_ckv", [NOWN, 512]); o_krT = k.dout("o_krT", [64, NOWN])
    o_kT = k.dout("o_kT", [16, 128, NOWN]); o_v = k.dout("o_v", [NOWN, 2048])
    o_memk = k.dout("o_memk", [256, 512]); o_memv = k.dout("o_memv", [256, 512])

    S_ckvT = k.dscr("S_ckvT", [128, 4, SEQ]); S_ckv = k.dscr("S_ckv", [128, 64, 512]); S_krT = k.dscr("S_krT", [64, SEQ])
    S_kT = k.dscr("S_kT", [16, 128, SEQ]); S_v = k.dscr("S_v", [16, 128, 64, 128])
    Z_ckvT = k.dscr("Z_ckvT", [2, 128, 4, ZK]); Z_ckv = k.dscr("Z_ckv", [2, 128, 33, 512]); Z_krT = k.dscr("Z_krT", [2, 64, ZK])
    Z_kT = k.dscr("Z_kT", [2, 16, 128, ZK]); Z_v = k.dscr("Z_v", [2, 16, 128, 33, 128])
    Q_lat = k.dscr("Q_lat", [128, 4, 16, NOWN]); Q_rope = k.dscr("Q_rope", [64, 16, NOWN]); Q_sb = k.dscr("Q_sb", [128, 16, NOWN])
    O_a = k.dscr("O_a", [128, 16, NOWN]); O_b = k.dscr("O_b", [128, 16, NOWN])
    B_S = Buf("S_scr", True); B_Z = Buf("Z_scr", True); B_Q = Buf("Q_scr", True); B_O = Buf("O_scr", True)
    B_out = Buf("outs", True)

    cst_f = k.sb("cst_f", [128, 1024], F32)
    cst = k.sb("cst", [128, 1024], BF16)
    gn = k.sb("gn", [128, 256], F32)
    gkvr = k.sb("gkvr", [128, 512], F32)
    msk = k.sb("msk", [128, NMSK], BF16)
    MK = k.sb("MK", [128, 4, 256], BF16); MV = k.sb("MV", [128, 2, 512], BF16); MKb = Buf("MK", True)
    kmx = k.sb("kmx", [128, 8], F32); kmxb = Buf("kmx")
    epsb = k.sb("epsb", [128, 1], F32)
    B_c = Buf("consts")
    ident = cst[:, 0:128]; Umat = cst[:, 128:256]; ones = cst[:, 256:384]
    ARENA0 = k.sb_off
    WR_CH = 4; CHB = 8192
    wring = k.sb("wring", [128, WR_CH * CHB // 2], BF16)
    wr_bufs = [Buf(f"wr{i}") for i in range(WR_CH)]
    wr_slots = [k.slot(f"wr{i}") for i in range(WR_CH)]
    wr_pos = [0]
    ARENA = k.sb_off

    banks = [nc.alloc_psum_tensor(f"bank{i}", [128, 512], F32) for i in range(8)]
    bbuf = [Buf(f"bank{i}", excl=True) for i in range(8)]

    blk = nc.Block()
    blk.__enter__()
    try:
        ld = k.slot("ld")
        k.dma(sp, cst_f[:], consts, [], [B_c], ld)
        k.dma(sp, gn[:], gains, [], [B_c], ld)
        k.dma(sp, gkvr[:], gkv_row, [], [B_c], ld)
        k.dma(pool, msk[:], masks, [], [B_c], ld)
        k.op(dve, lambda e: e.tensor_copy(out=cst[:], in_=cst_f[:]), [B_c], [B_c])
        k.op(dve, lambda e: e.memset(kmx[:], 0.0), [], [kmxb])
        k.op(dve, lambda e: e.memset(epsb[:], EPS), [B_c], [B_c])
        cp(1)

        for w in need_w:
            k.dma(pool, Wb[w], Wf[w], [], [Wbuf[w]], k.slot("wc_" + w))
        cp(2)

        def wload(w, g0, ng, kc0=0, nk=None):
            G, kcw, N = WT[w]
            nk = kcw if nk is None else nk
            nbytes = ng * nk * N * 2
            nch = (nbytes + CHB - 1) // CHB
            assert nch <= WR_CH
            if nch > 1:
                wr_pos[0] = (wr_pos[0] + nch - 1) // nch * nch
            if wr_pos[0] + nch > WR_CH:
                wr_pos[0] = 0
            c0 = wr_pos[0]; wr_pos[0] += nch
            bufs = wr_bufs[c0:c0 + nch]
            base = c0 * CHB // 2
            dst = wring[:, base:base + ng * nk * N].rearrange("p (g k n) -> p g k n", g=ng, k=nk, n=N)
            src = Wb[w].rearrange("(g p) (k n) -> p g k n", p=128, n=N)[:, g0:g0 + ng, kc0:kc0 + nk, :]
            k.dma(sp, dst, src, [Wbuf[w]], bufs, wr_slots[c0])
            return dst, bufs

        bank_rr = [0]

        def next_bank(lo=0, hi=8):
            b = lo + (bank_rr[0] % (hi - lo)); bank_rr[0] += 1
            return b

        def gemm_fm(w, groups, actT, abuf, T, evac, M=128, kc0=0, nk=None, gpl=1, blo=0, bhi=8):
            G, kcw, N = WT[w]
            nk_ = kcw if nk is None else nk
            groups = list(groups)
            loads = [groups[i:i + gpl] for i in range(0, len(groups), gpl)]
            pend = []

            def issue(i):
                gs = loads[i]
                pend.append(wload(w, gs[0], len(gs), kc0, nk_))
            for i in range(min(2, len(loads))):
                issue(i)
            for i, gs in enumerate(loads):
                wt, wb = pend.pop(0)
                for gi, g in enumerate(gs):
                    b = next_bank(blo, bhi)
                    mms = [(banks[b][0:M, 0:T], wt[:, gi, kk, 0:M], actT[:, kk, 0:T], kk == 0, kk == nk_ - 1) for kk in range(nk_)]
                    k.mm(mms, wb + [abuf], [bbuf[b]])
                    evac(g, banks[b], bbuf[b])
                if i + 2 < len(loads):
                    issue(i + 2)

        def rstd_from_bank(bank_ap, bb, n, out_ap, obuf):
            k.op(act, lambda e: e.activation(out=out_ap, in_=bank_ap, func=AF.Sqrt, bias=epsb[:, 0:1], scale=1.0 / n), [bb, B_c], [obuf])
            k.op(dve, lambda e: e.reciprocal(out=out_ap, in_=out_ap), [obuf], [obuf])

        def front(x_ap, T, gcol0, hT, hbuf, xs, xsb, xslots, sq, sqb, rrow, rbuf, NP=8):
            npc = KC // NP
            b = next_bank()
            for pc in range(npc):
                s = pc % 2
                k.dma(sp, xs[s][:, 0:NP, 0:T], x_ap[:, pc * NP:(pc + 1) * NP, :], [], [xsb[s]], xslots[s])
                k.op(act, lambda e: e.activation(out=sq[s][:, 0:NP, 0:T], in_=xs[s][:, 0:NP, 0:T], func=AF.Square), [xsb[s]], [sqb[s]])
                mms = [(banks[b][:, 0:T], ones, sq[s][:, j, 0:T], pc == 0 and j == 0, pc == npc - 1 and j == NP - 1) for j in range(NP)]
                k.mm(mms, [sqb[s], B_c], [bbuf[b]])
            rstd_from_bank(banks[b][:, 0:T], bbuf[b], D, rrow[:, 0:T], rbuf)
            for pc in range(npc):
                s = pc % 2
                k.dma(sp, xs[s][:, 0:NP, 0:T], x_ap[:, pc * NP:(pc + 1) * NP, :], [], [xsb[s]], xslots[s])
                for j in range(NP):
                    kc = pc * NP + j
                    eng = dve
                    k.op(eng, lambda e: e.scalar_tensor_tensor(out=hT[:, kc, 0:T], in0=xs[s][:, j, 0:T], scalar=gn[:, gcol0 + kc:gcol0 + kc + 1], in1=rrow[:, 0:T], op0=ALU.mult, op1=ALU.mult), [xsb[s], rbuf, B_c], [hbuf])

        def norm_sb(xt, xb, T, gcol0, hT, hbuf, sq, sqb, rrow, rbuf, NP=8):
            npc = KC // NP
            b = next_bank()
            for pc in range(npc):
                s = pc % 2
                k.op(act, lambda e: e.activation(out=sq[s][:, 0:NP, 0:T], in_=xt[:, pc * NP:(pc + 1) * NP, 0:T], func=AF.Square), [xb], [sqb[s]])
                mms = [(banks[b][:, 0:T], ones, sq[s][:, j, 0:T], pc == 0 and j == 0, pc == npc - 1 and j == NP - 1) for j in range(NP)]
                k.mm(mms, [sqb[s], B_c], [bbuf[b]])
            rstd_from_bank(banks[b][:, 0:T], bbuf[b], D, rrow[:, 0:T], rbuf)
            for kc in range(KC):
                eng = dve
                k.op(eng, lambda e: e.scalar_tensor_tensor(out=hT[:, kc, 0:T], in0=xt[:, kc, 0:T], scalar=gn[:, gcol0 + kc:gcol0 + kc + 1], in1=rrow[:, 0:T], op0=ALU.mult, op1=ALU.mult), [xb, rbuf, B_c], [hbuf])

        def proj_arena(TM):
            k.sb_off = ARENA
            a = {}
            a["xs"] = [k.sb(f"xs{i}", [128, 8, TM], F32) for i in range(2)]
            a["xsb"] = [Buf(f"xs{i}") for i in range(2)]
            a["xslots"] = [k.slot(f"xs{i}") for i in range(2)]
            a["sq"] = [k.sb(f"sq{i}", [128, 8, TM], BF16) for i in range(2)]
            a["sqb"] = [Buf(f"sq{i}") for i in range(2)]
            a["hT"] = k.sb("hT", [128, KC, TM], BF16); a["hb"] = Buf("hT", True)
            a["rrow"] = k.sb("rrow", [128, TM], F32); a["rb"] = Buf("rrow")
            return a

        def mk_stage(name, shape, dt, n):
            return [(k.sb(f"{name}{i}", shape, dt), Buf(f"{name}{i}"), k.slot(f"{name}{i}")) for i in range(n)], [0]

        def nxt(st):
            lst, pos = st
            r = lst[pos[0] % len(lst)]; pos[0] += 1
            return r

        def kv_stage(TM):
            st = {}
            st["bf512"] = mk_stage("stb", [128, 512], BF16, 3)
            st["f512"] = mk_stage("stf", [128, 512], F32, 3)
            st["craw"] = k.sb("craw", [128, 4, TM], F32); st["crawb"] = Buf("craw", True)
            st["csq"] = k.sb("csq", [128, 4, TM], BF16); st["csqb"] = Buf("csq", True)
            st["srow"] = k.sb("srow", [128, TM], F32); st["srb"] = Buf("srow")
            st["cT"] = k.sb("cT", [128, 4, TM], BF16); st["cTb"] = Buf("cT", True); st["cTs"] = k.slot("cTs")
            st["rc"] = k.sb("rc", [64, TM], F32); st["rs"] = k.sb("rs", [64, TM], F32); st["rcb"] = Buf("rc"); st["rsl"] = k.slot("rsl")
            st["krf"] = [k.sb(f"krf{i}", [64, TM], F32) for i in range(2)]; st["krfb"] = [Buf("krf0"), Buf("krf1")]
            st["krs"] = k.slot("krs"); st["krs2"] = k.slot("krs2")
            st["krb16"] = k.sb("krb16", [64, TM], BF16); st["krb16b"] = Buf("krb16")
            st["ctm"] = k.sb("ctm", [128, TM // 128, 512], F32); st["ctmb"] = Buf("ctm", True)
            st["acc"] = k.sb("acc", [128, 8], F32); st["accb"] = Buf("acc", True)
            st["junk"] = k.sb("junk", [128, 256], F32); st["junkb"] = Buf("junk", True)
            st["s1"] = k.sb("s1", [128, 1], F32); st["s1b"] = Buf("s1")
            st["tmpc"] = k.sb("tmpc", [128, 1], F32); st["tmpb"] = Buf("tmpc")
            return st

        def kv_project(a, T, ropeC_ap, ropeS_ap, kst, scr=None, out=None, kmax=None):
            hT, hb = a["hT"], a["hb"]
            ntb = (T + 127) // 128
            TB = min(T, 128)
            def ev_k(h, bank, bb):
                if scr is not None:
                    t, tb_, sl = nxt(kst["bf512"])
                    k.op(act, lambda e: e.activation(out=t[:, 0:T], in_=bank[:, 0:T], func=AF.Copy), [bb], [tb_])
                    for (dst, c0, c1) in scr["kT"](h):
                        k.dma(sp, dst, t[:, c0:c1], [tb_], [scr["buf"]], sl)
                if out is not None:
                    t, tb_, sl = nxt(kst["f512"])
                    k.op(dve, lambda e: e.tensor_copy(out=t[:, 0:T], in_=bank[:, 0:T]), [bb], [tb_])
                    k.dma(sp, out["kT"](h), t[:, 0:T], [tb_], [B_out], sl)
            gemm_fm("w_sbk", range(16), hT, hb, T, ev_k)
            craw = kst["craw"]; crawb = kst["crawb"]; csq = kst["csq"]; csqb = kst["csqb"]
            def ev_c(cc, bank, bb):
                k.op(act, lambda e: e.activation(out=craw[:, cc, 0:T], in_=bank[:, 0:T], func=AF.Copy), [bb], [crawb])
                k.op(act, lambda e: e.activation(out=csq[:, cc, 0:T], in_=bank[:, 0:T], func=AF.Square), [bb], [csqb])
            srow = kst["srow"]; srb = kst["srb"]
            cT = kst["cT"]; cTb = kst["cTb"]; cTs = kst["cTs"]
            if scr is not None:
                gemm_fm("w_ckvF", range(4), hT, hb, T, ev_c)
                b = next_bank()
                k.mm([(banks[b][:, 0:T], ones, csq[:, cc, 0:T], cc == 0, cc == 3) for cc in range(4)], [csqb, B_c], [bbuf[b]])
                rstd_from_bank(banks[b][:, 0:T], bbuf[b], 512, srow[:, 0:T], srb)
                for cc in range(4):
                    k.op(dve, lambda e: e.scalar_tensor_tensor(out=cT[:, cc, 0:T], in0=craw[:, cc, 0:T], scalar=gn[:, 168 + cc:169 + cc], in1=srow[:, 0:T], op0=ALU.mult, op1=ALU.mult), [crawb, srb, B_c], [cTb])
                for (dst, c0, c1) in scr["ckvT"]():
                    k.dma(sp, dst, cT[:, :, c0:c1], [cTb], [scr["buf"]], cTs)
                if kmax is not None:
                    k.op(act, lambda e: e.activation(out=csq[:, :, 0:T], in_=cT[:, :, 0:T], func=AF.Square), [cTb], [csqb])
                    b = next_bank()
                    k.mm([(banks[b][:, 0:T], ones, csq[:, cc, 0:T], cc == 0, cc == 3) for cc in range(4)], [csqb, B_c], [bbuf[b]])
                    k.op(dve, lambda e: e.tensor_reduce(out=kst["tmpc"][:, 0:1], in_=banks[b][:, 0:T], axis=AX.X, op=ALU.max), [bbuf[b]], [kst["tmpb"]])
                    k.op(dve, lambda e: e.tensor_tensor(out=kmx[:, kmax[0]:kmax[0] + 1], in0=kmx[:, kmax[0]:kmax[0] + 1], in1=kst["tmpc"][:, 0:1], op=ALU.max), [kst["tmpb"], kmxb], [kmxb])
            rc = kst["rc"]; rs = kst["rs"]; rcb = kst["rcb"]; rsl = kst["rsl"]
            k.dma(sp, rc[:, 0:T], ropeC_ap, [], [rcb], rsl)
            k.dma(sp, rs[:, 0:T], ropeS_ap, [], [rcb], rsl)
            krf = kst["krf"]; krfb = kst["krfb"]
            def ev_r(g, bank, bb):
                tab = rc if g == 0 else rs
                k.op(dve, lambda e: e.tensor_tensor(out=krf[g][:, 0:T], in0=bank[0:64, 0:T], in1=tab[:, 0:T], op=ALU.mult), [bb, rcb], [krfb[g]])
            gemm_fm("w_kr", range(2), hT, hb, T, ev_r, M=64, gpl=2)
            k.op(dve, lambda e: e.tensor_tensor(out=krf[0][:, 0:T], in0=krf[0][:, 0:T], in1=krf[1][:, 0:T], op=ALU.add), [krfb[0], krfb[1]], [krfb[0]])
            if out is not None:
                k.dma(sp, out["krT"](), krf[0][:, 0:T], [krfb[0]], [B_out], kst["krs"])
            if scr is not None:
                krb16 = kst["krb16"]; krb16b = kst["krb16b"]
                k.op(act, lambda e: e.activation(out=krb16[:, 0:T], in_=krf[0][:, 0:T], func=AF.Copy), [krfb[0]], [krb16b])
                for (dst, c0, c1) in scr["krT"]():
                    k.dma(sp, dst, krb16[:, c0:c1], [krb16b], [scr["buf"]], kst["krs2"])
                if kmax is not None:
                    k.op(act, lambda e: e.activation(out=csq[0:64, 0, 0:T], in_=krb16[:, 0:T], func=AF.Square), [krb16b], [csqb])
                    b = next_bank()
                    k.mm([(banks[b][:, 0:T], ones[0:64, :], csq[0:64, 0, 0:T], True, True)], [csqb, B_c], [bbuf[b]])
                    k.op(dve, lambda e: e.tensor_reduce(out=kst["tmpc"][:, 0:1], in_=banks[b][:, 0:T], axis=AX.X, op=ALU.max), [bbuf[b]], [kst["tmpb"]])
                    k.op(dve, lambda e: e.tensor_tensor(out=kmx[:, kmax[1]:kmax[1] + 1], in0=kmx[:, kmax[1]:kmax[1] + 1], in1=kst["tmpc"][:, 0:1], op=ALU.max), [kst["tmpb"], kmxb], [kmxb])
            ctm = kst["ctm"]; ctmb = kst["ctmb"]; acc = kst["acc"]; accb = kst["accb"]
            k.op(dve, lambda e: e.memset(acc[:], 0.0), [], [accb])

            def tm_group(w, g, evac_tb, wt, wb):
                for tb in range(ntb):
                    b = next_bank()
                    mms = [(banks[b][0:TB, 0:256], hT[:, kk, tb * 128:tb * 128 + TB], wt[:, 0, kk, :], kk == 0, kk == KC - 1) for kk in range(KC)]
                    k.mm(mms, wb + [hb], [bbuf[b]])
                    evac_tb(g, tb, banks[b], bbuf[b])

            def ev_ctm(g, tb, bank, bb):
                k.op(act, lambda e: e.activation(out=ctm[0:TB, tb, g * 256:(g + 1) * 256], in_=bank[0:TB, 0:256], func=AF.Copy), [bb], [ctmb])
                k.op(act, lambda e: e.activation(out=kst["junk"][0:TB, 0:256], in_=bank[0:TB, 0:256], func=AF.Square, accum_out=acc[0:TB, tb * 2 + g:tb * 2 + g + 1]), [bb, accb], [accb, kst["junkb"]])
            tm_jobs = [("w_ckvT", g, 0) for g in range(2)] + [("w_sbv", g, 1) for g in range(8)]
            tm_pend = [wload(tm_jobs[0][0], tm_jobs[0][1], 1)]

            def tm_run(ji, ev):
                if ji + 1 < len(tm_jobs):
                    tm_pend.append(wload(tm_jobs[ji + 1][0], tm_jobs[ji + 1][1], 1))
                wt, wb = tm_pend.pop(0)
                tm_group(tm_jobs[ji][0], tm_jobs[ji][1], ev, wt, wb)
            for g in range(2):
                tm_run(g, ev_ctm)
            s1 = kst["s1"]; s1b = kst["s1b"]
            for tb in range(ntb):
                k.op(dve, lambda e: e.tensor_tensor(out=s1[0:TB, 0:1], in0=acc[0:TB, tb * 2:tb * 2 + 1], in1=acc[0:TB, tb * 2 + 1:tb * 2 + 2], op=ALU.add), [accb], [s1b])
                k.op(act, lambda e: e.activation(out=s1[0:TB, 0:1], in_=s1[0:TB, 0:1], func=AF.Sqrt, bias=epsb[0:TB, 0:1], scale=1.0 / 512), [s1b, B_c], [s1b])
                k.op(dve, lambda e: e.reciprocal(out=s1[0:TB, 0:1], in_=s1[0:TB, 0:1]), [s1b], [s1b])
                t, tb_, sl = nxt(kst["f512"])
                k.op(dve, lambda e: e.scalar_tensor_tensor(out=t[0:TB, 0:512], in0=ctm[0:TB, tb, :], scalar=s1[0:TB, 0:1], in1=gkvr[0:TB, :], op0=ALU.mult, op1=ALU.mult), [ctmb, s1b, B_c], [tb_])
                if out is not None:
                    k.dma(sp, out["ckv"](tb), t[0:TB, 0:512], [tb_], [B_out], sl)
                if scr is not None:
                    t2, t2b, sl2 = nxt(kst["bf512"])
                    k.op(act, lambda e: e.activation(out=t2[0:TB, 0:512], in_=t[0:TB, 0:512], func=AF.Copy), [tb_], [t2b])
                    for (dst, p0, p1) in scr["ckv"](tb):
                        k.dma(sp, dst, t2[p0:p1, 0:512], [t2b], [scr["buf"]], sl2)

            def ev_v(g, tb, bank, bb):
                if out is not None:
                    t, tb_, sl = nxt(kst["f512"])
                    k.op(dve, lambda e: e.tensor_copy(out=t[0:TB, 0:256], in_=bank[0:TB, 0:256]), [bb], [tb_])
                    k.dma(sp, out["v"](tb, g), t[0:TB, 0:256], [tb_], [B_out], sl)
                if scr is not None:
                    t2, t2b, sl2 = nxt(kst["bf512"])
                    k.op(act, lambda e: e.activation(out=t2[0:TB, 0:256], in_=bank[0:TB, 0:256], func=AF.Copy), [bb], [t2b])
                    for (dst, p0, p1) in scr["v"](tb, g):
                        k.dma(sp, dst, t2[p0:p1, 0:256].rearrange("p (h d) -> p h d", h=2), [t2b], [scr["buf"]], sl2)
            for g in range(8):
                tm_run(2 + g, ev_v)

        if "PM" in stages:
            a = proj_arena(256)
            T = 256
            front(memT, T, 128, a["hT"], a["hb"], a["xs"], a["xsb"], a["xslots"], a["sq"], a["sqb"], a["rrow"], a["rb"])
            cp(3)
            stf = mk_stage("mstf", [128, 256], F32, 2)

            def ev_mk(h, bank, bb):
                k.op(act, lambda e: e.activation(out=MK[:, h, :], in_=bank[:, 0:256], func=AF.Copy), [bb], [MKb])
            gemm_fm("w_mkF", range(4), a["hT"], a["hb"], T, ev_mk)
            cp(4)
            for (w, dst, tomv) in (("w_mkT", o_memk, False), ("w_mvT", o_memv, True)):
                for g in range(2):
                    wt, wb = wload(w, g, 1)
                    cp(5)
                    for tb in range(2):
                        b = next_bank()
                        mms = [(banks[b][:, 0:256], a["hT"][:, kk, tb * 128:(tb + 1) * 128], wt[:, 0, kk, :], kk == 0, kk == KC - 1) for kk in range(KC)]
                        k.mm(mms, wb + [a["hb"]], [bbuf[b]])
                        cp(6)
                        t, tb_, sl = nxt(stf)
                        k.op(dve, lambda e: e.tensor_copy(out=t[:, 0:256], in_=banks[b][:, 0:256]), [bbuf[b]], [tb_])
                        cp(7)
                        k.dma(sp, dst[tb * 128:(tb + 1) * 128, g * 256:(g + 1) * 256], t[:, 0:256], [tb_], [B_out], sl)
                        cp(8)
                        if tomv:
                            cp(11)
                            k.op(act, lambda e: e.activation(out=MV[:, tb, g * 256:(g + 1) * 256], in_=t[:, 0:256], func=AF.Copy), [tb_], [MKb])
                            cp(12)
                if not tomv:
                    cp(10)
            cp(9)
            k.barrier()

        if "PA" in stages:
            a = proj_arena(512)
            kst = kv_stage(512)
            for t in range(16):
                T = 512; t0 = t * 512
                front(xT_seq[:, :, t0:t0 + T], T, 0, a["hT"], a["hb"], a["xs"], a["xsb"], a["xslots"], a["sq"], a["sqb"], a["rrow"], a["rb"])
                scr = {
                    "buf": B_S,
                    "kT": lambda h: [(S_kT[h, :, t0:t0 + 512], 0, 512)],
                    "ckvT": lambda: [(S_ckvT[:, :, t0:t0 + 512], 0, 512)],
                    "krT": lambda: [(S_krT[:, t0:t0 + 512], 0, 512)],
                    "ckv": lambda tb: [(S_ckv[:, t * 4 + tb, :], 0, 128)],
                    "v": lambda tb, g: [(S_v[2 * g:2 * g + 2, :, t * 4 + tb, :].rearrange("h p d -> p h d"), 0, 128)],
                }
                kv_project(a, T, ropeC_seq[:, t0:t0 + T], ropeS_seq[:, t0:t0 + T], kst, scr=scr, out=None, kmax=(0, 1))
            k.barrier()

        if "PB1" in stages:
            TM = 256
            own_tiles1 = [(i * 256, 256) for i in range(8)] + [(2048, 128)]
            a = proj_arena(TM)
            kst = kv_stage(TM)
            zs = k.slot("zcast")
            for s in range(2):
                k.dma(pool, Z_ckvT[s, :, :, 0:PAST], c_ckvT[s * 128:(s + 1) * 128, :].rearrange("p (c n) -> p c n", c=4), [], [B_Z], zs)
                k.dma(pool, Z_ckv[s, :, 0:32, :], c_ckv[s * 128:(s + 1) * 128, :].rearrange("p (j n) -> p j n", j=32), [], [B_Z], zs)
                k.dma(pool, Z_krT[s, :, 0:PAST], c_krT[s * 64:(s + 1) * 64, :], [], [B_Z], zs)
                k.dma(pool, Z_kT[s, :, :, 0:PAST], c_kT[s * 2048:(s + 1) * 2048, :].rearrange("(h p) n -> h p n", h=16), [], [B_Z], zs)
                k.dma(pool, Z_v[s, :, :, 0:32, :], c_v[s * 2048:(s + 1) * 2048, :].rearrange("(h p) (j d) -> h p j d", h=16, d=128), [], [B_Z], zs)
            cqraw = k.sb("cqraw", [128, 8, TM], F32); cqrawb = Buf("cqraw", True)
            cqsq = k.sb("cqsq", [128, 8, TM], BF16); cqsqb = Buf("cqsq", True)
            cqn = k.sb("cqn", [128, 8, TM], BF16); cqnb = Buf("cqn", True)
            qn = k.sb("qn", [128, 16, TM], BF16); qnb = Buf("qn", True)
            qst = mk_stage("qst", [128, 4, TM], BF16, 2)
            sbqst = mk_stage("sbqst", [128, TM], BF16, 3)
            qr0 = k.sb("qr0", [64, TM], F32); qr1 = k.sb("qr1", [64, TM], F32); qrb = [Buf("qr0"), Buf("qr1")]
            qrst = mk_stage("qrst", [64, TM], BF16, 2)
            wuk = k.sb("wuk", [128, 64, 128], BF16); wukb = Buf("wuk"); wuks = k.slot("wuk")
            k.dma(sp, wuk[:], Wb["w_ukT"].rearrange("p (k n) -> p k n", n=128), [Wbuf["w_ukT"]], [wukb], wuks)
            for (t0, T) in own_tiles1:
                front(xT_own[:, :, t0:t0 + T], T, 0, a["hT"], a["hb"], a["xs"], a["xsb"], a["xslots"], a["sq"], a["sqb"], a["rrow"], a["rb"])
                is_s = (T == 128)
                out = {
                    "kT": lambda h: o_kT[h, :, t0:t0 + T],
                    "krT": lambda: o_krT[:, t0:t0 + T],
                    "ckv": lambda tb: o_ckv[t0 + tb * 128:t0 + tb * 128 + 128, :],
                    "v": lambda tb, g: o_v[t0 + tb * 128:t0 + tb * 128 + 128, g * 256:(g + 1) * 256],
                }
                scr = None
                if is_s:
                    scr = {
                        "buf": B_Z,
                        "kT": lambda h: [(Z_kT[s, h, :, PAST:ZK], s * 64, s * 64 + 64) for s in range(2)],
                        "ckvT": lambda: [(Z_ckvT[s, :, :, PAST:ZK], s * 64, s * 64 + 64) for s in range(2)],
                        "krT": lambda: [(Z_krT[s, :, PAST:ZK], s * 64, s * 64 + 64) for s in range(2)],
                        "ckv": lambda tb: [(Z_ckv[s, 0:64, 32, :], s * 64, s * 64 + 64) for s in range(2)],
                        "v": lambda tb, g: [(Z_v[s, 2 * g:2 * g + 2, 0:64, 32, :].rearrange("h p d -> p h d"), s * 64, s * 64 + 64) for s in range(2)],
                    }
                kv_project(a, T, ropeC_own[:, t0:t0 + T], ropeS_own[:, t0:t0 + T], kst, scr=scr, out=out, kmax=None)
                hT, hb = a["hT"], a["hb"]

                def ev_cq(g, bank, bb):
                    k.op(act, lambda e: e.activation(out=cqraw[:, g, 0:T], in_=bank[:, 0:T], func=AF.Copy), [bb], [cqrawb])
                    k.op(act, lambda e: e.activation(out=cqsq[:, g, 0:T], in_=bank[:, 0:T], func=AF.Square), [bb], [cqsqb])
                gemm_fm("w_cq", range(8), hT, hb, T, ev_cq)
                b = next_bank()
                k.mm([(banks[b][:, 0:T], ones, cqsq[:, g, 0:T], g == 0, g == 7) for g in range(8)], [cqsqb, B_c], [bbuf[b]])
                srow = kst["srow"]; srb = kst["srb"]
                rstd_from_bank(banks[b][:, 0:T], bbuf[b], 1024, srow[:, 0:T], srb)
                for g in range(8):
                    eng = dve
                    k.op(eng, lambda e: e.scalar_tensor_tensor(out=cqn[:, g, 0:T], in0=cqraw[:, g, 0:T], scalar=gn[:, 160 + g:161 + g], in1=srow[:, 0:T], op0=ALU.mult, op1=ALU.mult), [cqrawb, srb, B_c], [cqnb])

                def ev_sbq(h, bank, bb):
                    t, tb_, sl = nxt(sbqst)
                    k.op(act, lambda e: e.activation(out=t[:, 0:T], in_=bank[:, 0:T], func=AF.Copy), [bb], [tb_])
                    k.dma(sp, Q_sb[:, h, t0:t0 + T], t[:, 0:T], [tb_], [B_Q], sl)
                gemm_fm("w_sbq", range(16), hT, hb, T, ev_sbq)

                def ev_qn(h, bank, bb):
                    k.op(act, lambda e: e.activation(out=qn[:, h, 0:T], in_=bank[:, 0:T], func=AF.Copy), [bb], [qnb])
                gemm_fm("w_uqn", range(16), cqn, cqnb, T, ev_qn, gpl=4)
                rc = kst["rc"]; rs = kst["rs"]; rcb = kst["rcb"]

                def ev_qr(g, bank, bb):
                    h, z = g // 2, g % 2
                    if z == 0:
                        k.op(dve, lambda e: e.tensor_tensor(out=qr0[:, 0:T], in0=bank[0:64, 0:T], in1=rc[:, 0:T], op=ALU.mult), [bb, rcb], [qrb[0]])
                    else:
                        k.op(dve, lambda e: e.tensor_tensor(out=qr1[:, 0:T], in0=bank[0:64, 0:T], in1=rs[:, 0:T], op=ALU.mult), [bb, rcb], [qrb[1]])
                        t, tb_, sl = nxt(qrst)
                        k.op(pool, lambda e: e.tensor_tensor(out=t[:, 0:T], in0=qr0[:, 0:T], in1=qr1[:, 0:T], op=ALU.add), [qrb[0], qrb[1]], [tb_])
                        k.dma(sp, Q_rope[:, h, t0:t0 + T], t[:, 0:T], [tb_], [B_Q], sl)
                gemm_fm("w_uqr", range(32), cqn, cqnb, T, ev_qr, M=64, gpl=8)
                for h in range(16):
                    t, tb_, sl = nxt(qst)
                    for cc in range(4):
                        b = next_bank()
                        k.mm([(banks[b][:, 0:T], wuk[:, h * 4 + cc, :], qn[:, h, 0:T], True, True)], [wukb, qnb], [bbuf[b]])
                        if cc % 2 == 0:
                            k.op(act, lambda e: e.activation(out=t[:, cc, 0:T], in_=banks[b][:, 0:T], func=AF.Copy), [bbuf[b]], [tb_])
                        else:
                            k.op(dve, lambda e: e.tensor_copy(out=t[:, cc, 0:T], in_=banks[b][:, 0:T]), [bbuf[b]], [tb_])
                    k.dma(sp, Q_lat[:, :, h, t0:t0 + T], t[:, :, 0:T], [tb_], [B_Q], sl)
            k.barrier()

        if "PB2" in stages:
            k.sb_off = ARENA0
            wuv = k.sb("wuv", [128, 64, 128], BF16); wuvb = Buf("wuv"); wuvs = k.slot("wuv")
            k.dma(sp, wuv[:], Wb["w_uvr"].rearrange("p (k n) -> p k n", n=128), [Wbuf["w_uvr"]], [wuvb], wuvs)
            qlat = k.sb("qlat", [128, 4, 2048], BF16); qrope = k.sb("qrope", [64, 2048], BF16); sbq = k.sb("sbq", [128, 2048], BF16)
            qb_ = Buf("qtiles"); qsl = k.slot("qtiles")
            qsq = k.sb("qsq", [128, 4, 512], BF16); qsqb = Buf("qsq")
            rrow = k.sb("r_row", [128, 512], F32); rrb = Buf("r_row")
            rbf = k.sb("r_bf", [1, 512], BF16); rbfb = Buf("r_bf")
            KT = [dict(ckvT=k.sb(f"ktc{i}", [128, 4, 512], BF16), krT=k.sb(f"ktr{i}", [64, 512], BF16), ckv=k.sb(f"ktv{i}", [128, 4, 512], BF16), b=Buf(f"kt{i}"), s=k.slot(f"kt{i}")) for i in range(2)]
            PT = [(k.sb(f"PT{i}", [128, 512], BF16), Buf(f"PT{i}")) for i in range(2)]
            linv = k.sb("linv", [128, 512], F32); linvb = Buf("linv")
            olat = k.sb("olat", [128, 4, 512], BF16); olatb = Buf("olat", True)
            oast = k.sb("oast", [128, 2048], BF16); oastb = Buf("oast", True); oasl = k.slot("oast")
            obst = k.sb("obst", [128, 2048], BF16); obstb = Buf("obst", True); obsl = k.slot("obst")
            NKMAX = 8704
            SK = [dict(kT=k.sb(f"skT{i}", [128, NKMAX], BF16), v=k.sb(f"sv{i}", [128, 68, 128], BF16), b=Buf(f"sk{i}"), s=k.slot(f"sk{i}")) for i in range(2)]
            ebuf = [(k.sb(f"e{i}", [128, 512], F32), Buf(f"e{i}")) for i in range(2)]
            spb = [(k.sb(f"sp{i}", [128, 512], BF16), Buf(f"sp{i}")) for i in range(2)]
            e2b = [(k.sb(f"e2{i}", [128, 512], F32), Buf(f"e2{i}")) for i in range(2)]
            wTb = [(k.sb(f"wT{i}", [128, 512], BF16), Buf(f"wT{i}")) for i in range(2)]
            carry = [(k.sb(f"carry{i}", [1, 128], BF16), Buf(f"carry{i}")) for i in range(2)]
            km2 = k.sb("km2", [128, 2], F32); km2b = Buf("km2")
            ztmp = k.sb("ztmp", [128, 4, 512], BF16); ztmpb = Buf("ztmp"); ztsl = k.slot("ztmp")

            qsets = []
            for m in range(16):
                tiles = [dict(k0=kt * 512, nb=4, bs=128, diag=(kt == m), jb0=kt * 4) for kt in range(m + 1)]
                qsets.append(dict(t0=m * 128, NQ=128, src="S", s=None, tiles=tiles))
            for s in range(2):
                tiles = [dict(k0=kt * 512, nb=4, bs=128, diag=False, jb0=kt * 4) for kt in range(8)]
                tiles.append(dict(k0=PAST, nb=1, bs=64, diag=True, jb0=32))
                qsets.append(dict(t0=2048 + s * 64, NQ=64, src="Z", s=s, tiles=tiles))

            def kmax_pass(src_ckvT, src_krT, n, ccol, rcol):
                k.dma(sp, ztmp[:, :, 0:n], src_ckvT, [B_Z], [ztmpb], ztsl)
                k.op(act, lambda e: e.activation(out=qsq[:, :, 0:n], in_=ztmp[:, :, 0:n], func=AF.Square), [ztmpb], [qsqb])
                b = next_bank()
                k.mm([(banks[b][:, 0:n], ones, qsq[:, cc, 0:n], cc == 0, cc == 3) for cc in range(4)], [qsqb, B_c], [bbuf[b]])
                k.op(dve, lambda e: e.tensor_reduce(out=km2[:, 0:1], in_=banks[b][:, 0:n], axis=AX.X, op=ALU.max), [bbuf[b]], [km2b])
                k.op(dve, lambda e: e.tensor_tensor(out=kmx[:, ccol:ccol + 1], in0=kmx[:, ccol:ccol + 1], in1=km2[:, 0:1], op=ALU.max), [km2b, kmxb], [kmxb])
                k.dma(sp, ztmp[0:64, 0, 0:n], src_krT, [B_Z], [ztmpb], ztsl)
                k.op(act, lambda e: e.activation(out=qsq[0:64, 0, 0:n], in_=ztmp[0:64, 0, 0:n], func=AF.Square), [ztmpb], [qsqb])
                b = next_bank()
                k.mm([(banks[b][:, 0:n], ones[0:64, :], qsq[0:64, 0, 0:n], True, True)], [qsqb, B_c], [bbuf[b]])
                k.op(dve, lambda e: e.tensor_reduce(out=km2[:, 0:1], in_=banks[b][:, 0:n], axis=AX.X, op=ALU.max), [bbuf[b]], [km2b])
                k.op(dve, lambda e: e.tensor_tensor(out=kmx[:, rcol:rcol + 1], in0=kmx[:, rcol:rcol + 1], in1=km2[:, 0:1], op=ALU.max), [km2b, kmxb], [kmxb])
            for s in range(2):
                for kt in range(8):
                    kmax_pass(Z_ckvT[s, :, :, kt * 512:(kt + 1) * 512], Z_krT[s, :, kt * 512:(kt + 1) * 512], 512, 2 + 2 * s, 3 + 2 * s)
                kmax_pass(Z_ckvT[s, :, :, PAST:ZK], Z_krT[s, :, PAST:ZK], 64, 2 + 2 * s, 3 + 2 * s)

            PT3 = PT + [(k.sb("PT2", [128, 512], BF16), Buf("PT2"))]
            rbfa = k.sb("r_bfa", [1, 2048], BF16); rbfab = Buf("r_bfa")
            e3 = ebuf + [(k.sb("e_2", [128, 512], F32), Buf("e_2"))]
            sp3 = spb + [(k.sb("sp_2", [128, 512], BF16), Buf("sp_2"))]
            w3 = wTb + [(k.sb("wT_2", [128, 512], BF16), Buf("wT_2"))]
            kt_rr = [0]; sk_rr = [0]
            for qs in qsets:
                t0, NQ, src, s = qs["t0"], qs["NQ"], qs["src"], qs["s"]
                HG = 512 // NQ; NG = 16 // HG
                scr_buf = B_S if src == "S" else B_Z
                for cc in range(4):
                    k.dma(sp, qlat[:, cc, 0:16 * NQ].rearrange("p (h q) -> p h q", q=NQ), Q_lat[:, cc, :, t0:t0 + NQ], [B_Q], [qb_], qsl)
                k.dma(sp, qrope[:, 0:16 * NQ].rearrange("p (h q) -> p h q", q=NQ), Q_rope[:, :, t0:t0 + NQ], [B_Q], [qb_], qsl)
                k.dma(sp, sbq[:, 0:16 * NQ].rearrange("p (h q) -> p h q", q=NQ), Q_sb[:, :, t0:t0 + NQ], [B_Q], [qb_], qsl)
                ccol, rcol = (0, 1) if src == "S" else (2 + 2 * s, 3 + 2 * s)
                k.op(dve, lambda e: e.tensor_tensor(out=km2[:, 1:2], in0=kmx[:, ccol:ccol + 1], in1=kmx[:, rcol:rcol + 1], op=ALU.add), [kmxb], [km2b])
                for hg in range(NG):
                    c0q = hg * 512
                    k.op(act, lambda e: e.activation(out=qsq[:, :, :], in_=qlat[:, :, c0q:c0q + 512], func=AF.Square), [qb_], [qsqb])
                    b = 7
                    k.mm([(banks[b][:, :], ones, qsq[:, cc, :], cc == 0, False) for cc in range(4)], [qsqb, B_c], [bbuf[b]])
                    k.op(act, lambda e: e.activation(out=qsq[0:64, 0, :], in_=qrope[:, c0q:c0q + 512], func=AF.Square), [qb_], [qsqb])
                    k.mm([(banks[b][:, :], ones[0:64, :], qsq[0:64, 0, :], False, True)], [qsqb, B_c], [bbuf[b]])
                    k.op(act, lambda e: e.activation(out=rrow[:, :], in_=banks[b][:, :], func=AF.Sqrt, scale=km2[:, 1:2]), [bbuf[b], km2b], [rrb])
                    k.op(dve, lambda e: e.tensor_scalar(out=rbfa[0:1, c0q:c0q + 512], in0=rrow[0:1, :], scalar1=-1.02, scalar2=None, op0=ALU.mult), [rrb], [rbfab])
                for hg in range(NG):
                    c0q = hg * 512
                    qrv = qrope[:, c0q:c0q + 512]
                    blocks = []
                    ntl = len(qs["tiles"])
                    for ti, tl in enumerate(qs["tiles"]):
                        for j in range(tl["nb"]):
                            blocks.append((ti, tl, j))
                    pend = None
                    K_ = None
                    for bi, (ti, tl, j) in enumerate(blocks):
                        nb, bs, k0, jb0 = tl["nb"], tl["bs"], tl["k0"], tl["jb0"]
                        nk = nb * bs
                        if j == 0:
                            K_ = KT[kt_rr[0] % 2]; kt_rr[0] += 1
                            if src == "S":
                                k.dma(sp, K_["ckvT"][:, :, 0:nk], S_ckvT[:, :, k0:k0 + nk], [scr_buf], [K_["b"]], K_["s"])
                                k.dma(sp, K_["krT"][:, 0:nk], S_krT[:, k0:k0 + nk], [scr_buf], [K_["b"]], K_["s"])
                                k.dma(sp, K_["ckv"][0:bs, 0:nb, :], S_ckv[0:bs, jb0:jb0 + nb, :], [scr_buf], [K_["b"]], K_["s"])
                            else:
                                k.dma(sp, K_["ckvT"][:, :, 0:nk], Z_ckvT[s, :, :, k0:k0 + nk], [scr_buf], [K_["b"]], K_["s"])
                                k.dma(sp, K_["krT"][:, 0:nk], Z_krT[s, :, k0:k0 + nk], [scr_buf], [K_["b"]], K_["s"])
                                k.dma(sp, K_["ckv"][0:bs, 0:nb, :], Z_ckv[s, 0:bs, jb0:jb0 + nb, :], [scr_buf], [K_["b"]], K_["s"])
                        bS = bi % 2
                        use_mask = tl["diag"] and src == "S"
                        mms = [(banks[bS][0:bs, :], K_["ckvT"][:, cc, j * bs:(j + 1) * bs], qlat[:, cc, c0q:c0q + 512], cc == 0, False) for cc in range(4)]
                        mms.append((banks[bS][0:bs, :], K_["krT"][:, j * bs:(j + 1) * bs], qrv, False, False))
                        mms.append((banks[bS][0:bs, :], ones[0:1, 0:bs], rbfa[0:1, c0q:c0q + 512], False, not use_mask))
                        if use_mask:
                            mms.append((banks[bS][0:bs, :], ident, msk[:, j * 512:(j + 1) * 512], False, True))
                        k.mm(mms, [K_["b"], qb_, rbfab, B_c], [bbuf[bS]])
                        pt, ptb = PT3[bi % 3]
                        k.op(act, lambda e: e.activation(out=pt[0:bs, :], in_=banks[bS][0:bs, :], func=AF.Exp, scale=MLA_SCALE), [bbuf[bS]], [ptb])

                        def pv(p_):
                            (pbi, pK, pbs, pj, ppt, pptb) = p_
                            first = (pbi == 0); last = (pbi == len(blocks) - 1)
                            mms2 = [(banks[2 + cc][:, :], pK["ckv"][0:pbs, pj, cc * 128:(cc + 1) * 128], ppt[0:pbs, :], first, last) for cc in range(4)]
                            mms2.append((banks[6][:, :], ones[0:pbs, :], ppt[0:pbs, :], first, last))
                            k.mm(mms2, [pK["b"], pptb, B_c], [bbuf[2], bbuf[3], bbuf[4], bbuf[5], bbuf[6]])
                        if pend is not None:
                            pv(pend)
                        pend = (bi, K_, bs, j, pt, ptb)
                    pv(pend)
                    k.op(dve, lambda e: e.reciprocal(out=linv[:, :], in_=banks[6][:, :]), [bbuf[6]], [linvb])
                    for cc in range(4):
                        k.op(dve, lambda e: e.tensor_tensor(out=olat[:, cc, :], in0=banks[2 + cc][:, :], in1=linv[:, :], op=ALU.mult), [bbuf[2 + cc], linvb], [olatb])
                    b = 7
                    mms = []
                    for hh in range(HG):
                        for cc in range(4):
                            mms.append((banks[b][:, hh * NQ:(hh + 1) * NQ], wuv[:, (hg * HG + hh) * 4 + cc, :], olat[:, cc, hh * NQ:(hh + 1) * NQ], cc == 0, cc == 3))
                    k.mm(mms, [wuvb, olatb], [bbuf[b]])
                    k.op(act, lambda e: e.activation(out=oast[:, c0q:c0q + 512], in_=banks[b][:, :], func=AF.Copy), [bbuf[b]], [oastb])
                k.dma(sp, O_a[:, :, t0:t0 + NQ], oast[:, 0:16 * NQ].rearrange("p (h q) -> p h q", q=NQ), [oastb], [B_O], oasl)
                nkeys = sum(tl["nb"] * tl["bs"] for tl in qs["tiles"])
                ntl = len(qs["tiles"])
                steps = [(h, ti) for h in range(16) for ti in range(ntl - 1, -1, -1)]
                NS = len(steps)
                skof = {}

                def sb_load(h):
                    S_ = SK[sk_rr[0] % 2]; sk_rr[0] += 1
                    if src == "S":
                        k.dma(sp, S_["kT"][:, 0:nkeys], S_kT[h, :, 0:nkeys], [scr_buf], [S_["b"]], S_["s"])
                        k.dma(sp, S_["v"][:, 0:nkeys // 128, :], S_v[h, :, 0:nkeys // 128, :], [scr_buf], [S_["b"]], S_["s"])
                    else:
                        k.dma(sp, S_["kT"][:, 0:nkeys], Z_kT[s, h, :, 0:nkeys], [scr_buf], [S_["b"]], S_["s"])
                        k.dma(sp, S_["v"][:, 0:33, :], Z_v[s, h, :, 0:33, :], [scr_buf], [S_["b"]], S_["s"])
                    skof[h] = S_
                first_idx = {h: h * ntl for h in range(16)}

                def stage1(i):
                    h, ti = steps[i]
                    S_ = skof[h]
                    tl = qs["tiles"][ti]
                    nb, bs, k0 = tl["nb"], tl["bs"], tl["k0"]
                    W_ = nb * NQ
                    qh = sbq[:, h * NQ:(h + 1) * NQ]
                    bA = i % 2
                    mms = []
                    for j in range(nb):
                        mms.append((banks[bA][0:bs, j * NQ:(j + 1) * NQ], S_["kT"][:, k0 + j * bs:k0 + (j + 1) * bs], qh, True, not tl["diag"]))
                        if tl["diag"]:
                            if src == "S":
                                mms.append((banks[bA][0:bs, j * NQ:(j + 1) * NQ], ident, msk[:, 2048 + j * 128:2048 + (j + 1) * 128], False, True))
                            else:
                                mms.append((banks[bA][0:bs, j * NQ:(j + 1) * NQ], ident[0:64, 0:64], msk[0:64, 2560:2624], False, True))
                    k.mm(mms, [S_["b"], qb_, B_c], [bbuf[bA]])
                    e_, eb = e3[i % 3]; sp_, spb_ = sp3[i % 3]
                    k.op(act, lambda e: e.activation(out=e_[0:bs, 0:W_], in_=banks[bA][0:bs, 0:W_], func=AF.Exp, scale=SB_SCALE), [bbuf[bA]], [eb])
                    k.op(act, lambda e: e.activation(out=sp_[0:bs, 0:W_], in_=e_[0:bs, 0:W_], func=AF.Ln, bias=1.0, scale=1.0), [eb], [spb_])

                def stage2(i):
                    h, ti = steps[i]
                    tl = qs["tiles"][ti]
                    nb, bs = tl["nb"], tl["bs"]
                    W_ = nb * NQ
                    e_, eb = e3[i % 3]; sp_, spb_ = sp3[i % 3]; e2_, e2b_ = e2b[i % 2]; w_, wb_ = w3[i % 3]
                    bB = 2 + (i % 2)
                    mms = [(banks[bB][0:bs, 0:W_], Umat[0:bs, 0:bs], sp_[0:bs, 0:W_], True, False)]
                    for sh in range(1, nb):
                        mms.append((banks[bB][0:bs, 0:(nb - sh) * NQ], ones[0:bs, 0:bs], sp_[0:bs, sh * NQ:W_], False, False))
                    rd = [spb_, B_c]
                    if ti != ntl - 1:
                        cp_ = carry[(i - 1) % 2]
                        for j in range(nb):
                            mms.append((banks[bB][0:bs, j * NQ:(j + 1) * NQ], ones[0:1, 0:bs], cp_[0][0:1, 0:NQ], False, False))
                        rd.append(cp_[1])
                    mms[-1] = mms[-1][:4] + (True,)
                    k.mm(mms, rd, [bbuf[bB]])
                    if ti > 0:
                        cn = carry[i % 2]
                        k.op(dve, lambda e: e.tensor_copy(out=cn[0][0:1, 0:NQ], in_=banks[bB][0:1, 0:NQ]), [bbuf[bB]], [cn[1]])
                    k.op(act, lambda e: e.activation(out=e2_[0:bs, 0:W_], in_=banks[bB][0:bs, 0:W_], func=AF.Exp, scale=-1.0), [bbuf[bB]], [e2b_])
                    k.op(pool, lambda e: e.tensor_tensor(out=w_[0:bs, 0:W_], in0=e_[0:bs, 0:W_], in1=e2_[0:bs, 0:W_], op=ALU.mult), [eb, e2b_], [wb_])

                def stage3(i):
                    h, ti = steps[i]
                    S_ = skof[h]
                    tl = qs["tiles"][ti]
                    nb, bs, jb0 = tl["nb"], tl["bs"], tl["jb0"]
                    w_, wb_ = w3[i % 3]
                    bC = 4 + (h % 2)
                    mms = []
                    for j in range(nb):
                        mms.append((banks[bC][:, 0:NQ], S_["v"][0:bs, jb0 + j, :], w_[0:bs, j * NQ:(j + 1) * NQ], ti == ntl - 1 and j == 0, ti == 0 and j == nb - 1))
                    k.mm(mms, [S_["b"], wb_], [bbuf[bC]])
                    if ti == 0:
                        k.op(act, lambda e: e.activation(out=obst[:, h * NQ:(h + 1) * NQ], in_=banks[bC][:, 0:NQ], func=AF.Copy), [bbuf[bC]], [obstb])
                for i in range(NS + 2):
                    if 2 <= i <= NS + 1:
                        stage3(i - 2)
                    if 1 <= i <= NS:
                        stage2(i - 1)
                    if i < NS:
                        h_, ti_ = steps[i]
                        if h_ not in skof:
                            sb_load(h_)
                        if h_ + 1 < 16 and (h_ + 1) not in skof and i >= first_idx[h_] + 1:
                            sb_load(h_ + 1)
                        stage1(i)
                k.dma(sp, O_b[:, :, t0:t0 + NQ], obst[:, 0:16 * NQ].rearrange("p (h q) -> p h q", q=NQ), [obstb], [B_O], obsl)

            k.barrier()

        own_tiles = [(0, 512), (512, 512), (1024, 512), (1536, 512), (2048, 128)]
        if "PB3" in stages:
            A0 = ARENA
            RA = Buf("p3_A", True)
            hT = k.sb("p3_hT", [128, KC, 512], BF16, off=A0)
            oa = k.sb("p3_oa", [128, 16, 512], BF16, off=A0 + 32768); ob = k.sb("p3_ob", [128, 16, 512], BF16, off=A0 + 49152)
            x1 = k.sb("p3_x1", [128, KC, 512], F32, off=A0)
            oasl2 = k.slot("p3_o"); x1sl = k.slot("p3_x1")
            mg = k.sb("p3_mg", [128, KC, 512], BF16, off=A0 + 65536); RB = Buf("p3_B", True)
            h2 = mg
            R0 = A0 + 98304
            RC = Buf("p3_C", True)
            ff = k.sb("p3_ff", [128, 22, 512], BF16, off=R0)
            xs = [k.sb(f"p3_xs{i}", [128, 4, 512], F32, off=R0 + i * 8192) for i in range(2)]
            sqf = [k.sb(f"p3_sqf{i}", [128, 4, 512], BF16, off=R0 + 16384 + i * 4096) for i in range(2)]
            xslots = [k.slot(f"p3xs{i}") for i in range(2)]
            mqT = k.sb("p3_mqT", [128, 4, 512], BF16, off=R0); atT = k.sb("p3_atT", [128, 4, 512], BF16, off=R0 + 4096)
            PTt = k.sb("p3_PT", [128, 2, 4, 512], BF16, off=R0 + 8192)
            Pn = [k.sb(f"p3_P{i}", [128, 256], BF16, off=R0 + 16384 + i * 512) for i in range(2)]
            ZMK = k.sb("p3_ZMK", [128, 2, 4, 256], BF16, off=R0 + 20480); ZMV = k.sb("p3_ZMV", [128, 2, 2, 512], BF16, off=R0 + 24576)
            k.sb_off = R0 + 32768
            sq = [k.sb(f"p3_sq{i}", [128, 4, 512], BF16) for i in range(2)]; sqb = [Buf("p3sq0"), Buf("p3sq1")]
            rrow = k.sb("p3_rrow", [128, 512], F32); rb = Buf("p3_rrow")
            ga = [(k.sb(f"p3_ga{i}", [128, 512], F32), Buf(f"p3_ga{i}")) for i in range(2)]
            tt = [(k.sb(f"p3_tt{i}", [128, 512], F32), Buf(f"p3_tt{i}")) for i in range(2)]
            st4 = [(k.sb(f"p3_st{i}", [128, 4], F32), Buf(f"p3_st{i}")) for i in range(2)]
            zms = k.slot("zm")
            yst = mk_stage("p3_y", [128, 512], F32, 3)
            bank_bf = [banks[i].bitcast(BF16) for i in range(8)]
            rr3 = [0]
            for (t0, T) in own_tiles:
                front(xT_own[:, :, t0:t0 + T], T, 0, hT, RA, xs, [RC, RC], xslots, sqf, [RC, RC], rrow, rb, NP=4)
                k.dma(sp, oa[:, :, 0:T], O_a[:, :, t0:t0 + T], [B_O], [RA], oasl2)
                k.dma(sp, ob[:, :, 0:T], O_b[:, :, t0:t0 + T], [B_O], [RA], oasl2)
                for j in range(32):
                    res = []
                    for (gidx, wbn, osrc, bcol) in ((j, "w_ba", oa, 172 + j), (32 + j, "w_bb", ob, 204 + j)):
                        wt, wb = wload("w_gate", gidx, 1)
                        bG = next_bank()
                        k.mm([(banks[bG][:, 0:T], wt[:, 0, kk, :], hT[:, kk, 0:T], kk == 0, kk == KC - 1) for kk in range(KC)], wb + [RA], [bbuf[bG]])
                        g_, gb_ = ga[rr3[0] % 2]
                        k.op(act, lambda e: e.activation(out=g_[:, 0:T], in_=banks[bG][:, 0:T], func=AF.Sigmoid, bias=gn[:, bcol:bcol + 1], scale=1.0), [bbuf[bG], B_c], [gb_])
                        wt2, wb2 = wload(wbn, j, 1)
                        bB = next_bank()
                        k.mm([(banks[bB][:, 0:T], wt2[:, 0, kk, :], osrc[:, kk, 0:T], kk == 0, kk == 15) for kk in range(16)], wb2 + [RA], [bbuf[bB]])
                        t_, tb_ = tt[rr3[0] % 2]; rr3[0] += 1
                        k.op(dve, lambda e: e.tensor_tensor(out=t_[:, 0:T], in0=banks[bB][:, 0:T], in1=g_[:, 0:T], op=ALU.mult), [bbuf[bB], gb_], [tb_])
                        res.append((t_, tb_))
                    k.op(pool, lambda e: e.tensor_tensor(out=mg[:, j, 0:T], in0=res[0][0][:, 0:T], in1=res[1][0][:, 0:T], op=ALU.add), [res[0][1], res[1][1]], [RB])
                for pc in range(4):
                    k.dma(sp, x1[:, pc * 8:(pc + 1) * 8, 0:T], xT_own[:, pc * 8:(pc + 1) * 8, t0:t0 + T], [], [RA], x1sl)

                def ev_res(j, bank, bb):
                    k.op(dve, lambda e: e.tensor_tensor(out=x1[:, j, 0:T], in0=bank[:, 0:T], in1=x1[:, j, 0:T], op=ALU.add), [bb, RA], [RA])
                gemm_fm("w_out", range(32), mg, RB, T, ev_res)
                norm_sb(x1, RA, T, 32, h2, RB, sq, sqb, rrow, rb, NP=4)

                def ev_mq(h, bank, bb):
                    k.op(act, lambda e: e.activation(out=mqT[:, h, 0:T], in_=bank[:, 0:T], func=AF.Copy), [bb], [RC])
                gemm_fm("w_mq", range(4), h2, RB, T, ev_mq)
                if T == 512:
                    msets = [(tb * 128, 128, None) for tb in range(4)]
                else:
                    k.dma(pool, ZMK[:], c_memkT.rearrange("p (s h m) -> p s h m", s=2, h=4), [], [RC], zms)
                    k.dma(pool, ZMV[:], c_memv.rearrange("p (s j n) -> p s j n", s=2, j=2), [], [RC], zms)
                    msets = [(s * 64, 64, s) for s in range(2)]
                for (c0, nq, s) in msets:
                    for h in range(4):
                        bS = next_bank(0, 4)
                        krhs = MK[:, h, :] if s is None else ZMK[:, s, h, :]
                        k.mm([(banks[bS][0:nq, 0:256], mqT[:, h, c0:c0 + nq], krhs, True, True)], [RC, MKb], [bbuf[bS]])
                        s4, s4b = st4[rr3[0] % 2]; p_ = Pn[rr3[0] % 2]; rr3[0] += 1
                        k.op(dve, lambda e: e.memset(s4[:, :], 0.0), [], [s4b])
                        k.op(dve, lambda e: e.tensor_reduce(out=s4[0:nq, 0:1], in_=banks[bS][0:nq, 0:256], axis=AX.X, op=ALU.max), [bbuf[bS], s4b], [s4b])
                        k.op(dve, lambda e: e.tensor_scalar(out=s4[0:nq, 1:2], in0=s4[0:nq, 0:1], scalar1=-MEM_SCALE, scalar2=None, op0=ALU.mult), [s4b], [s4b])
                        k.op(act, lambda e: e.activation(out=p_[0:nq, :], in_=banks[bS][0:nq, 0:256], func=AF.Exp, bias=s4[0:nq, 1:2], scale=MEM_SCALE, accum_out=s4[0:nq, 2:3]), [bbuf[bS], s4b, RC], [RC, s4b])
                        k.op(dve, lambda e: e.reciprocal(out=s4[0:nq, 3:4], in_=s4[0:nq, 2:3]), [s4b], [s4b])
                        k.op(dve, lambda e: e.tensor_scalar(out=p_[0:nq, :], in0=p_[0:nq, :], scalar1=s4[0:nq, 3:4], scalar2=None, op0=ALU.mult), [s4b, RC], [RC])
                        for jb in range(2):
                            bT = next_bank(4, 8)
                            k.op(pe, lambda e: e.transpose(out=bank_bf[bT][:, 0:nq], in_=p_[0:nq, jb * 128:(jb + 1) * 128], identity=ident[0:nq, 0:nq]), [RC, B_c], [bbuf[bT]])
                            k.op(act, lambda e: e.activation(out=PTt[:, jb, h, c0:c0 + nq], in_=bank_bf[bT][:, 0:nq], func=AF.Copy), [bbuf[bT]], [RC])
                for (c0, nq, s) in (msets if T != 512 else [(0, 512, None)]):
                    for h in range(4):
                        b = next_bank()
                        mms = []
                        for jb in range(2):
                            vl = MV[:, jb, h * 128:(h + 1) * 128] if s is None else ZMV[:, s, jb, h * 128:(h + 1) * 128]
                            mms.append((banks[b][:, 0:nq], vl, PTt[:, jb, h, c0:c0 + nq], jb == 0, jb == 1))
                        k.mm(mms, [RC, MKb], [bbuf[b]])
                        k.op(act, lambda e: e.activation(out=atT[:, h, c0:c0 + nq], in_=banks[b][:, 0:nq], func=AF.Copy), [bbuf[b]], [RC])
                gemm_fm("w_mo", range(32), atT, RC, T, ev_res, gpl=8)
                norm_sb(x1, RA, T, 64, h2, RB, sq, sqb, rrow, rb, NP=4)
                f0 = 0
                for nq_ in (22, 22, 21, 21):
                    for fl in range(nq_):
                        f = f0 + fl
                        wt, wb = wload("w_fg", f, 1)
                        bG = next_bank()
                        k.mm([(banks[bG][:, 0:T], wt[:, 0, kk, :], h2[:, kk, 0:T], kk == 0, kk == KC - 1) for kk in range(KC)], wb + [RB], [bbuf[bG]])
                        g_, gb_ = ga[rr3[0] % 2]; rr3[0] += 1
                        k.op(act, lambda e: e.activation(out=g_[:, 0:T], in_=banks[bG][:, 0:T], func=AF.Silu), [bbuf[bG]], [gb_])
                        wt2, wb2 = wload("w_fu", f, 1)
                        bU = next_bank()
                        k.mm([(banks[bU][:, 0:T], wt2[:, 0, kk, :], h2[:, kk, 0:T], kk == 0, kk == KC - 1) for kk in range(KC)], wb2 + [RB], [bbuf[bU]])
                        k.op(dve, lambda e: e.tensor_tensor(out=ff[:, fl, 0:T], in0=banks[bU][:, 0:T], in1=g_[:, 0:T], op=ALU.mult), [bbuf[bU], gb_], [RC])
                    gemm_fm("w_fd", range(32), ff, RC, T, ev_res, kc0=f0, nk=nq_)
                    f0 += nq_
                b = next_bank()
                for pc in range(8):
                    s_ = pc % 2
                    k.op(act, lambda e: e.activation(out=sq[s_][:, :, 0:T], in_=x1[:, pc * 4:(pc + 1) * 4, 0:T], func=AF.Square), [RA], [sqb[s_]])
                    k.mm([(banks[b][:, 0:T], ones, sq[s_][:, j, 0:T], pc == 0 and j == 0, pc == 7 and j == 3) for j in range(4)], [sqb[s_], B_c], [bbuf[b]])
                rstd_from_bank(banks[b][:, 0:T], bbuf[b], D, rrow[:, 0:T], rb)
                for kc in range(KC):
                    t, tb_, sl = nxt(yst)
                    eng = dve
                    k.op(eng, lambda e: e.scalar_tensor_tensor(out=t[:, 0:T], in0=x1[:, kc, 0:T], scalar=gn[:, 96 + kc:97 + kc], in1=rrow[:, 0:T], op0=ALU.mult, op1=ALU.mult), [RA, rb, B_c], [tb_])
                    k.dma(sp, o_yT[:, kc, t0:t0 + T], t[:, 0:T], [tb_], [B_out], sl)

    except _Stop:
        pass
    k.slots = k.all_slots
    k.barrier()
    blk.__exit__(None, None, None)
    return k


def _fm(w, N=128):
    Kd, NC = w.shape
    kc = Kd // 128; G = NC // N
    return np.ascontiguousarray(w.reshape(kc, 128, G, N).transpose(2, 1, 0, 3)).reshape(G * 128, kc * N)


def prep_weights(inp, need):
    w_in = inp["w_in"][0]
    out = {}
    def put(name, arr):
        if name in need:
            out[name] = arr
    if any(n in need for n in ("w_cq", "w_ckvF", "w_ckvT", "w_kr", "w_sbq", "w_sbk", "w_sbv", "w_gate")):
        put("w_cq", _fm(w_in[:, 0:1024])); put("w_ckvF", _fm(w_in[:, 1024:1536])); put("w_ckvT", _fm(w_in[:, 1024:1536], 256))
        kr = w_in[:, 1536:1600]
        krz = np.concatenate([kr[:, 32:64], kr[:, 0:32]], axis=1)
        put("w_kr", _fm(np.concatenate([kr, krz], axis=1), 64))
        put("w_sbq", _fm(w_in[:, 1600:3648])); put("w_sbk", _fm(w_in[:, 3648:5696])); put("w_sbv", _fm(w_in[:, 5696:7744], 256))
        put("w_gate", _fm(w_in[:, 7744:15936]))
    if "w_uqn" in need:
        wuq = inp["w_uq"][0]
        put("w_uqn", _fm(np.ascontiguousarray(wuq[:, :, 0:128]).reshape(1024, 2048)))
        r = wuq[:, :, 128:192]
        rz = np.concatenate([r[:, :, 32:64], r[:, :, 0:32]], axis=2)
        put("w_uqr", _fm(np.ascontiguousarray(np.stack([r, rz], axis=2)).reshape(1024, 16 * 2 * 64), 64))
    if "w_ukT" in need:
        wuk = inp["w_uk"][0]
        put("w_ukT", np.ascontiguousarray(wuk.reshape(4, 128, 16, 128).transpose(3, 2, 0, 1)).reshape(128, 64 * 128))
    if "w_uvr" in need:
        wuv = inp["w_uv"][0]
        put("w_uvr", np.ascontiguousarray(wuv.reshape(4, 128, 16, 128).transpose(1, 2, 0, 3)).reshape(128, 64 * 128))
    for nm, key in (("w_ba", "w_branch_a"), ("w_bb", "w_branch_b"), ("w_out", "w_out"), ("w_mq", "w_mq"), ("w_mkF", "w_mk"),
                    ("w_mo", "w_mo"), ("w_fg", "w_gate"), ("w_fu", "w_up"), ("w_fd", "w_down")):
        if nm in need:
            put(nm, _fm(inp[key][0]))
    if "w_mkT" in need:
        put("w_mkT", _fm(inp["w_mk"][0], 256)); put("w_mvT", _fm(inp["w_mv"][0], 256))
    return out


def rope_tabs(pos):
    half = 32
    inv = (10000.0 ** (-np.arange(half, dtype=np.float32) / half)).astype(np.float32)
    ang = pos.astype(np.float32)[:, None] * inv[None, :]
    cos = np.cos(ang).astype(np.float32).T; sin = np.sin(ang).astype(np.float32).T
    return np.ascontiguousarray(np.concatenate([cos, cos], 0)), np.ascontiguousarray(np.concatenate([-sin, sin], 0))


def prep_core(inp, core, stages, shared):
    b, c = core // 4, core % 4
    m = {}
    m.update(shared)
    xp = inp["x_prompt"][b]
    own_pos = np.concatenate([np.arange((4 * mm + c) * 128, (4 * mm + c + 1) * 128) for mm in range(16)])
    xs = inp["x_sample"][2 * core:2 * core + 2].reshape(128, D)
    xo = np.concatenate([xp[own_pos], xs], axis=0)
    m["xT_own"] = np.ascontiguousarray(xo.T.reshape(KC, 128, NOWN).transpose(1, 0, 2))
    pos_all = np.concatenate([own_pos, PAST + np.arange(64), PAST + np.arange(64)])
    m["ropeC_own"], m["ropeS_own"] = rope_tabs(pos_all)
    mk = np.zeros((128, 4 * 512 + 4 * 128 + 64), np.float32)
    kk = np.arange(128)[:, None]; qq = np.arange(128)[None, :]
    for j in range(4):
        kpos = j * 128 + kk; qpos = c * 128 + qq
        mla = np.where((kpos // 64) <= (qpos // 64), 0.0, NEG).astype(np.float32)
        mk[:, j * 512:(j + 1) * 512] = np.tile(mla, (1, 4))
        mk[:, 2048 + j * 128:2048 + (j + 1) * 128] = np.where(kpos < qpos, 0.0, NEG)
    mk[0:64, 2560:2624] = np.where(np.arange(64)[:, None] < np.arange(64)[None, :], 0.0, NEG)
    m["masks"] = mk
    if "PA" in stages:
        m["xT_seq"] = np.ascontiguousarray(xp.T.reshape(KC, 128, SEQ).transpose(1, 0, 2))
        m["ropeC_seq"], m["ropeS_seq"] = rope_tabs(np.arange(SEQ))
    if "PM" in stages:
        m["memT"] = np.ascontiguousarray(inp["mem_prompt"][b].T.reshape(KC, 128, 256).transpose(1, 0, 2))
    if "PB1" in stages:
        sl = slice(2 * core, 2 * core + 2)
        ck = inp["cache_mla_ckv"][0, sl]
        m["c_ckvT"] = np.ascontiguousarray(ck.reshape(2, PAST, 4, 128).transpose(0, 3, 2, 1)).reshape(256, 4 * PAST)
        m["c_ckv"] = np.ascontiguousarray(ck.reshape(2, 32, 128, 512).transpose(0, 2, 1, 3)).reshape(256, 32 * 512)
        m["c_krT"] = np.ascontiguousarray(inp["cache_mla_krope"][0, sl].transpose(0, 2, 1)).reshape(128, PAST)
        m["c_kT"] = np.ascontiguousarray(inp["cache_sb_k"][0, sl].transpose(0, 2, 3, 1)).reshape(2 * 16 * 128, PAST)
        m["c_v"] = np.ascontiguousarray(inp["cache_sb_v"][0, sl].reshape(2, 32, 128, 16, 128).transpose(0, 3, 2, 1, 4)).reshape(2 * 16 * 128, 32 * 128)
    if "PB3" in stages:
        sl = slice(2 * core, 2 * core + 2)
        mkc = inp["cache_mem_k"][0, sl]
        m["c_memkT"] = np.ascontiguousarray(mkc.transpose(3, 0, 2, 1)).reshape(128, 2 * 4 * 256)
        mvc = inp["cache_mem_v"][0, sl].reshape(2, 2, 128, 512)
        m["c_memv"] = np.ascontiguousarray(mvc.transpose(2, 0, 1, 3)).reshape(128, 2 * 2 * 512)
    return m


def prep_shared(inp, stages):
    need = []
    for s in stages:
        need += STAGE_W[s]
    sh = prep_weights(inp, set(need))
    cst = np.zeros((128, 1024), np.float32)
    cst[:, 0:128] = np.eye(128, dtype=np.float32)
    cst[:, 128:256] = (np.arange(128)[:, None] >= np.arange(128)[None, :]).astype(np.float32)
    cst[:, 256:384] = 1.0
    cst[0, 384] = 1.0
    sh["consts"] = cst
    g = np.zeros((128, 256), np.float32)
    def col(v):
        return v.reshape(-1, 128).T
    g[:, 0:32] = col(inp["g_mix"][0]); g[:, 32:64] = col(inp["g_xattn"][0]); g[:, 64:96] = col(inp["g_ffn"][0])
    g[:, 96:128] = col(inp["g_final"]); g[:, 128:160] = col(inp["g_mem"][0]); g[:, 160:168] = col(inp["g_q_lat"][0])
    g[:, 168:172] = col(inp["g_kv_lat"][0]); g[:, 172:236] = col(inp["b_gate"][0])
    sh["gains"] = g
    sh["gkv_row"] = np.ascontiguousarray(np.tile(inp["g_kv_lat"][0][None, :], (128, 1)))
    return sh


_CACHE = {}


def run(inp, stages=ALL_STAGES, cores=tuple(range(8))):
    inp = {kk: np.asarray(v) for kk, v in inp.items()}
    key = tuple(stages)
    if key not in _CACHE:
        _CACHE[key] = build(stages)
    kb = _CACHE[key]
    shared = prep_shared(inp, stages)
    maps = []
    for core in cores:
        mm = prep_core(inp, core, stages, shared)
        maps.append({n: np.ascontiguousarray(mm[n], dtype=np.float32) for n in kb.inputs})
    res = run_bass_kernel_spmd(kb.nc, maps, core_ids=list(range(len(cores))))
    return res.results


def assemble(results, cores=tuple(range(8))):
    y_p = np.zeros((2, SEQ, D), np.float32); y_s = np.zeros((16, 64, D), np.float32)
    p_ckv = np.zeros((1, 2, SEQ, 512), np.float32); p_kr = np.zeros((1, 2, SEQ, 64), np.float32)
    p_k = np.zeros((1, 2, SEQ, 16, 128), np.float32); p_v = np.zeros((1, 2, SEQ, 16, 128), np.float32)
    p_mk = np.zeros((1, 2, 256, 4, 128), np.float32); p_mv = np.zeros((1, 2, 256, 4, 128), np.float32)
    s_ckv = np.zeros((1, 16, 64, 512), np.float32); s_kr = np.zeros((1, 16, 64, 64), np.float32)
    s_k = np.zeros((1, 16, 64, 16, 128), np.float32); s_v = np.zeros((1, 16, 64, 16, 128), np.float32)
    for i, core in enumerate(cores):
        r = results[i]
        b, c = core // 4, core % 4
        own_pos = np.concatenate([np.arange((4 * mm + c) * 128, (4 * mm + c + 1) * 128) for mm in range(16)])
        y = r["o_yT"].transpose(2, 1, 0).reshape(NOWN, D)
        y_p[b, own_pos] = y[:NPO]; y_s[2 * core:2 * core + 2] = y[NPO:].reshape(2, 64, D)
        ck = r["o_ckv"]; p_ckv[0, b, own_pos] = ck[:NPO]; s_ckv[0, 2 * core:2 * core + 2] = ck[NPO:].reshape(2, 64, 512)
        kr = r["o_krT"].T; p_kr[0, b, own_pos] = kr[:NPO]; s_kr[0, 2 * core:2 * core + 2] = kr[NPO:].reshape(2, 64, 64)
        kT = r["o_kT"].transpose(2, 0, 1); p_k[0, b, own_pos] = kT[:NPO]; s_k[0, 2 * core:2 * core + 2] = kT[NPO:].reshape(2, 64, 16, 128)
        v = r["o_v"].reshape(NOWN, 16, 128); p_v[0, b, own_pos] = v[:NPO]; s_v[0, 2 * core:2 * core + 2] = v[NPO:].reshape(2, 64, 16, 128)
        if c == 0:
            p_mk[0, b] = r["o_memk"].reshape(256, 4, 128); p_mv[0, b] = r["o_memv"].reshape(256, 4, 128)
    return (y_p, y_s, p_ckv, p_kr, p_k, p_v, p_mk, p_mv, s_ckv, s_kr, s_k, s_v)


def kernel(**inputs):
    results = run(inputs)
    return assemble(results)
```

```python
import numpy as np
import concourse.bass as bass
import concourse.mybir as mybir
from concourse.bass_utils import run_bass_kernel_spmd

F32 = mybir.dt.float32
BF16 = mybir.dt.bfloat16
AF = mybir.ActivationFunctionType
ALU = mybir.AluOpType
AX = mybir.AxisListType

D = 4096; KC = 32; SEQ = 8192; NOWN = 2176; NPO = 2048
EPS = 1e-6
MLA_SCALE = 192 ** -0.5
SB_SCALE = 128 ** -0.5
MEM_SCALE = 128 ** -0.5
NEG = -30000.0
PAST = 4096; ZK = 4160
SB_LIMIT = 229376
SB_BASE = 16384 + 512

ALL_STAGES = ("PM", "PA", "PB1", "PB2", "PB3")


class Buf:
    def __init__(self, name, multi=False, excl=False):
        self.name = name; self.w = {}; self.r = {}; self.multi = multi; self.excl = excl


class Eng:
    def __init__(self, nc, obj, name, is_pe=False):
        self.obj = obj; self.sem = nc.alloc_semaphore("e_" + name); self.cnt = 0
        self.seen = {}; self.is_pe = is_pe; self.name = name

    def wait_map(self, m):
        for sem, val in m.items():
            if self.is_pe and sem is self.sem:
                continue
            if self.seen.get(sem, 0) >= val:
                continue
            self.obj.wait_ge(sem, val)
            self.seen[sem] = val


class Slot:
    def __init__(self, nc, name):
        self.sem = nc.alloc_semaphore("d_" + name); self.cnt = 0


class K:
    def __init__(self, stages):
        self.stages = stages
        nc = self.nc = bass.Bass("TRN2", target_bir_lowering=False)
        self.pe = Eng(nc, nc.tensor, "pe", True)
        self.act = Eng(nc, nc.scalar, "act")
        self.dve = Eng(nc, nc.vector, "dve")
        self.pool = Eng(nc, nc.gpsimd, "pool")
        self.sp = Eng(nc, nc.sync, "sp")
        self.engs = [self.pe, self.act, self.dve, self.pool, self.sp]
        self.sb_off = SB_BASE
        self.inputs = {}
        self.outputs = {}
        self.slots = []
        self.all_slots = []
        self.nname = 0

    def _deps(self, eng, reads, writes):
        for b in reads:
            eng.wait_map(b.w)
            if b.excl:
                eng.wait_map({s_: v_ for s_, v_ in b.r.items() if s_ is not eng.sem})
        for b in writes:
            if not b.multi:
                eng.wait_map(b.w)
            eng.wait_map(b.r)

    def _record(self, sem, val, reads, writes):
        for b in reads:
            if b.r.get(sem, 0) < val:
                b.r[sem] = val
        for b in writes:
            if b.multi:
                if b.w.get(sem, 0) < val:
                    b.w[sem] = val
            else:
                b.w = {sem: val}; b.r = {}

    def op(self, eng, fn, reads=(), writes=()):
        self._deps(eng, reads, writes)
        ins = fn(eng.obj)
        eng.cnt += 1
        ins.then_inc(eng.sem, 1)
        self._record(eng.sem, eng.cnt, reads, writes)

    def mm(self, mms, reads, writes):
        eng = self.pe
        self._deps(eng, reads, writes)
        ins = None
        for (o, l, r, st, sp_) in mms:
            ins = eng.obj.matmul(o, l, r, start=st, stop=sp_)
        eng.cnt += 1
        ins.then_inc(eng.sem, 1)
        self._record(eng.sem, eng.cnt, reads, writes)

    def slot(self, name=None):
        s = Slot(self.nc, f"{name or 's'}_{len(self.all_slots)}")
        if not (name or "").startswith("wc_"):
            self.slots.append(s)
        self.all_slots.append(s)
        return s

    def dma(self, q, out, in_, reads, writes, slot):
        self._deps(q, reads, writes)
        ins = q.obj.dma_start(out=out, in_=in_)
        slot.cnt += 16
        ins.then_inc(slot.sem, 16)
        self._record(slot.sem, slot.cnt, reads, writes)

    def barrier(self):
        m = {}
        for e in self.engs:
            if e.cnt:
                m[e.sem] = e.cnt
        for s in self.slots:
            if s.cnt:
                m[s.sem] = s.cnt
        for e in self.engs:
            mm = dict(m)
            mm.pop(e.sem, None) if e.is_pe else None
            e.wait_map(mm)

    def sb(self, name, shape, dt, off=None):
        sz = int(np.prod(shape[1:])) * (4 if dt == F32 else 2)
        if off is None:
            off = self.sb_off
            self.sb_off += (sz + 63) // 64 * 64
        assert off + sz <= SB_LIMIT, (name, off, sz)
        self.nname += 1
        return self.nc.alloc_sbuf_tensor_at(f"{name}_{self.nname}", list(shape), dt, offset=off)

    def din(self, name, shape, dt=F32):
        t = self.nc.dram_tensor(name, list(shape), dt, kind="ExternalInput")
        self.inputs[name] = (tuple(shape), dt)
        return t.ap()

    def dout(self, name, shape, dt=F32):
        t = self.nc.dram_tensor(name, list(shape), dt, kind="ExternalOutput")
        self.outputs[name] = (tuple(shape), dt)
        return t.ap()

    def dscr(self, name, shape, dt=BF16):
        return self.nc.dram_tensor(name, list(shape), dt).ap()


def weight_table():
    return {
        "w_cq": (8, 32, 128), "w_ckvF": (4, 32, 128), "w_ckvT": (2, 32, 256), "w_kr": (2, 32, 64),
        "w_sbq": (16, 32, 128), "w_sbk": (16, 32, 128), "w_sbv": (8, 32, 256), "w_gate": (64, 32, 128),
        "w_uqn": (16, 8, 128), "w_uqr": (32, 8, 64), "w_ukT": (1, 64, 128), "w_uvr": (1, 64, 128),
        "w_ba": (32, 16, 128), "w_bb": (32, 16, 128), "w_out": (32, 32, 128),
        "w_mq": (4, 32, 128), "w_mkF": (4, 32, 128), "w_mkT": (2, 32, 256), "w_mvT": (2, 32, 256),
        "w_mo": (32, 4, 128), "w_fg": (86, 32, 128), "w_fu": (86, 32, 128), "w_fd": (32, 86, 128),
    }


STAGE_W = {
    "PM": ["w_mkF", "w_mkT", "w_mvT"],
    "PA": ["w_ckvF", "w_ckvT", "w_kr", "w_sbk", "w_sbv"],
    "PB1": ["w_ckvF", "w_ckvT", "w_kr", "w_sbk", "w_sbv", "w_cq", "w_sbq", "w_uqn", "w_uqr", "w_ukT"],
    "PB2": ["w_uvr"],
    "PB3": ["w_gate", "w_ba", "w_bb", "w_out", "w_mq", "w_mo", "w_fg", "w_fu", "w_fd"],
}


STOP_AT = 0


class _Stop(Exception):
    pass


def cp(n):
    if STOP_AT == n:
        raise _Stop()


def build(stages=ALL_STAGES):
    k = K(stages)
    nc = k.nc
    pe, act, dve, pool, sp = k.pe, k.act, k.dve, k.pool, k.sp
    WT = weight_table()
    need_w = []
    for s in stages:
        for w in STAGE_W[s]:
            if w not in need_w:
                need_w.append(w)

    Wf = {}; Wb = {}; Wbuf = {}
    for w in need_w:
        G, kcw, N = WT[w]
        Wf[w] = k.din(w, [G * 128, kcw * N])
        Wb[w] = k.dscr(w + "_bf", [G * 128, kcw * N])
        Wbuf[w] = Buf(w, multi=True)
    consts = k.din("consts", [128, 1024])
    gains = k.din("gains", [128, 256])
    gkv_row = k.din("gkv_row", [128, 512])
    NMSK = 4 * 512 + 4 * 128 + 64
    masks = k.din("masks", [128, NMSK])
    xT_own = k.din("xT_own", [128, KC, NOWN])
    ropeC_own = k.din("ropeC_own", [64, NOWN]); ropeS_own = k.din("ropeS_own", [64, NOWN])
    if "PA" in stages:
        xT_seq = k.din("xT_seq", [128, KC, SEQ])
        ropeC_seq = k.din("ropeC_seq", [64, SEQ]); ropeS_seq = k.din("ropeS_seq", [64, SEQ])
    if "PM" in stages:
        memT = k.din("memT", [128, KC, 256])
    if "PB1" in stages:
        c_ckvT = k.din("c_ckvT", [2 * 128, 4 * PAST]); c_ckv = k.din("c_ckv", [2 * 128, 32 * 512])
        c_krT = k.din("c_krT", [2 * 64, PAST]); c_kT = k.din("c_kT", [2 * 16 * 128, PAST])
        c_v = k.din("c_v", [2 * 16 * 128, 32 * 128])
    if "PB3" in stages:
        c_memkT = k.din("c_memkT", [128, 2 * 4 * 256]); c_memv = k.din("c_memv", [128, 2 * 2 * 512])

    o_yT = k.dout("o_yT", [128, KC, NOWN])
    o_ckv = k.dout("o_ckv", [NOWN, 512]); o_krT = k.dout("o_krT", [64, NOWN])
    o_kT = k.dout("o_kT", [16, 128, NOWN]); o_v = k.dout("o_v", [NOWN, 2048])
    o_memk = k.dout("o_memk", [256, 512]); o_memv = k.dout("o_memv", [256, 512])

    S_ckvT = k.dscr("S_ckvT", [128, 4, SEQ]); S_ckv = k.dscr("S_ckv", [128, 64, 512]); S_krT = k.dscr("S_krT", [64, SEQ])
    S_kT = k.dscr("S_kT", [16, 128, SEQ]); S_v = k.dscr("S_v", [16, 128, 64, 128])
    Z_ckvT = k.dscr("Z_ckvT", [2, 128, 4, ZK]); Z_ckv = k.dscr("Z_ckv", [2, 128, 33, 512]); Z_krT = k.dscr("Z_krT", [2, 64, ZK])
    Z_kT = k.dscr("Z_kT", [2, 16, 128, ZK]); Z_v = k.dscr("Z_v", [2, 16, 128, 33, 128])
    Q_lat = k.dscr("Q_lat", [128, 4, 16, NOWN]); Q_rope = k.dscr("Q_rope", [64, 16, NOWN]); Q_sb = k.dscr("Q_sb", [128, 16, NOWN])
    O_a = k.dscr("O_a", [128, 16, NOWN]); O_b = k.dscr("O_b", [128, 16, NOWN])
    B_S = Buf("S_scr", True); B_Z = Buf("Z_scr", True); B_Q = Buf("Q_scr", True); B_O = Buf("O_scr", True)
    B_out = Buf("outs", True)

    cst_f = k.sb("cst_f", [128, 1024], F32)
    cst = k.sb("cst", [128, 1024], BF16)
    gn = k.sb("gn", [128, 256], F32)
    gkvr = k.sb("gkvr", [128, 512], F32)
    msk = k.sb("msk", [128, NMSK], BF16)
    MK = k.sb("MK", [128, 4, 256], BF16); MV = k.sb("MV", [128, 2, 512], BF16); MKb = Buf("MK", True)
    kmx = k.sb("kmx", [128, 8], F32); kmxb = Buf("kmx")
    epsb = k.sb("epsb", [128, 1], F32)
    B_c = Buf("consts")
    ident = cst[:, 0:128]; Umat = cst[:, 128:256]; ones = cst[:, 256:384]
    ARENA0 = k.sb_off
    WR_CH = 4; CHB = 8192
    wring = k.sb("wring", [128, WR_CH * CHB // 2], BF16)
    wr_bufs = [Buf(f"wr{i}") for i in range(WR_CH)]
    wr_slots = [k.slot(f"wr{i}") for i in range(WR_CH)]
    wr_pos = [0]
    ARENA = k.sb_off

    banks = [nc.alloc_psum_tensor(f"bank{i}", [128, 512], F32) for i in range(8)]
    bbuf = [Buf(f"bank{i}", excl=True) for i in range(8)]

    blk = nc.Block()
    blk.__enter__()
    try:
        ld = k.slot("ld")
        k.dma(sp, cst_f[:], consts, [], [B_c], ld)
        k.dma(sp, gn[:], gains, [], [B_c], ld)
        k.dma(sp, gkvr[:], gkv_row, [], [B_c], ld)
        k.dma(pool, msk[:], masks, [], [B_c], ld)
        k.op(dve, lambda e: e.tensor_copy(out=cst[:], in_=cst_f[:]), [B_c], [B_c])
        k.op(dve, lambda e: e.memset(kmx[:], 0.0), [], [kmxb])
        k.op(dve, lambda e: e.memset(epsb[:], EPS), [B_c], [B_c])
        cp(1)

        for w in need_w:
            k.dma(pool, Wb[w], Wf[w], [], [Wbuf[w]], k.slot("wc_" + w))
        cp(2)

        def wload(w, g0, ng, kc0=0, nk=None):
            G, kcw, N = WT[w]
            nk = kcw if nk is None else nk
            nbytes = ng * nk * N * 2
            nch = (nbytes + CHB - 1) // CHB
            assert nch <= WR_CH
            if nch > 1:
                wr_pos[0] = (wr_pos[0] + nch - 1) // nch * nch
            if wr_pos[0] + nch > WR_CH:
                wr_pos[0] = 0
            c0 = wr_pos[0]; wr_pos[0] += nch
            bufs = wr_bufs[c0:c0 + nch]
            base = c0 * CHB // 2
            dst = wring[:, base:base + ng * nk * N].rearrange("p (g k n) -> p g k n", g=ng, k=nk, n=N)
            src = Wb[w].rearrange("(g p) (k n) -> p g k n", p=128, n=N)[:, g0:g0 + ng, kc0:kc0 + nk, :]
            k.dma(sp, dst, src, [Wbuf[w]], bufs, wr_slots[c0])
            return dst, bufs

        bank_rr = [0]

        def next_bank(lo=0, hi=8):
            b = lo + (bank_rr[0] % (hi - lo)); bank_rr[0] += 1
            return b

        def gemm_fm(w, groups, actT, abuf, T, evac, M=128, kc0=0, nk=None, gpl=1, blo=0, bhi=8):
            G, kcw, N = WT[w]
            nk_ = kcw if nk is None else nk
            groups = list(groups)
            loads = [groups[i:i + gpl] for i in range(0, len(groups), gpl)]
            pend = []

            def issue(i):
                gs = loads[i]
                pend.append(wload(w, gs[0], len(gs), kc0, nk_))
            for i in range(min(2, len(loads))):
                issue(i)
            for i, gs in enumerate(loads):
                wt, wb = pend.pop(0)
                for gi, g in enumerate(gs):
                    b = next_bank(blo, bhi)
                    mms = [(banks[b][0:M, 0:T], wt[:, gi, kk, 0:M], actT[:, kk, 0:T], kk == 0, kk == nk_ - 1) for kk in range(nk_)]
                    k.mm(mms, wb + [abuf], [bbuf[b]])
                    evac(g, banks[b], bbuf[b])
                if i + 2 < len(loads):
                    issue(i + 2)

        def rstd_from_bank(bank_ap, bb, n, out_ap, obuf):
            k.op(act, lambda e: e.activation(out=out_ap, in_=bank_ap, func=AF.Sqrt, bias=epsb[:, 0:1], scale=1.0 / n), [bb, B_c], [obuf])
            k.op(dve, lambda e: e.reciprocal(out=out_ap, in_=out_ap), [obuf], [obuf])

        def front(x_ap, T, gcol0, hT, hbuf, xs, xsb, xslots, sq, sqb, rrow, rbuf, NP=8):
            npc = KC // NP
            b = next_bank()
            for pc in range(npc):
                s = pc % 2
                k.dma(sp, xs[s][:, 0:NP, 0:T], x_ap[:, pc * NP:(pc + 1) * NP, :], [], [xsb[s]], xslots[s])
                k.op(act, lambda e: e.activation(out=sq[s][:, 0:NP, 0:T], in_=xs[s][:, 0:NP, 0:T], func=AF.Square), [xsb[s]], [sqb[s]])
                mms = [(banks[b][:, 0:T], ones, sq[s][:, j, 0:T], pc == 0 and j == 0, pc == npc - 1 and j == NP - 1) for j in range(NP)]
                k.mm(mms, [sqb[s], B_c], [bbuf[b]])
            rstd_from_bank(banks[b][:, 0:T], bbuf[b], D, rrow[:, 0:T], rbuf)
            for pc in range(npc):
                s = pc % 2
                k.dma(sp, xs[s][:, 0:NP, 0:T], x_ap[:, pc * NP:(pc + 1) * NP, :], [], [xsb[s]], xslots[s])
                for j in range(NP):
                    kc = pc * NP + j
                    eng = dve
                    k.op(eng, lambda e: e.scalar_tensor_tensor(out=hT[:, kc, 0:T], in0=xs[s][:, j, 0:T], scalar=gn[:, gcol0 + kc:gcol0 + kc + 1], in1=rrow[:, 0:T], op0=ALU.mult, op1=ALU.mult), [xsb[s], rbuf, B_c], [hbuf])

        def norm_sb(xt, xb, T, gcol0, hT, hbuf, sq, sqb, rrow, rbuf, NP=8):
            npc = KC // NP
            b = next_bank()
            for pc in range(npc):
                s = pc % 2
                k.op(act, lambda e: e.activation(out=sq[s][:, 0:NP, 0:T], in_=xt[:, pc * NP:(pc + 1) * NP, 0:T], func=AF.Square), [xb], [sqb[s]])
                mms = [(banks[b][:, 0:T], ones, sq[s][:, j, 0:T], pc == 0 and j == 0, pc == npc - 1 and j == NP - 1) for j in range(NP)]
                k.mm(mms, [sqb[s], B_c], [bbuf[b]])
            rstd_from_bank(banks[b][:, 0:T], bbuf[b], D, rrow[:, 0:T], rbuf)
            for kc in range(KC):
                eng = dve
                k.op(eng, lambda e: e.scalar_tensor_tensor(out=hT[:, kc, 0:T], in0=xt[:, kc, 0:T], scalar=gn[:, gcol0 + kc:gcol0 + kc + 1], in1=rrow[:, 0:T], op0=ALU.mult, op1=ALU.mult), [xb, rbuf, B_c], [hbuf])

        def proj_arena(TM):
            k.sb_off = ARENA
            a = {}
            a["xs"] = [k.sb(f"xs{i}", [128, 8, TM], F32) for i in range(2)]
            a["xsb"] = [Buf(f"xs{i}") for i in range(2)]
            a["xslots"] = [k.slot(f"xs{i}") for i in range(2)]
            a["sq"] = [k.sb(f"sq{i}", [128, 8, TM], BF16) for i in range(2)]
            a["sqb"] = [Buf(f"sq{i}") for i in range(2)]
            a["hT"] = k.sb("hT", [128, KC, TM], BF16); a["hb"] = Buf("hT", True)
            a["rrow"] = k.sb("rrow", [128, TM], F32); a["rb"] = Buf("rrow")
            return a

        def mk_stage(name, shape, dt, n):
            return [(k.sb(f"{name}{i}", shape, dt), Buf(f"{name}{i}"), k.slot(f"{name}{i}")) for i in range(n)], [0]

        def nxt(st):
            lst, pos = st
            r = lst[pos[0] % len(lst)]; pos[0] += 1
            return r

        def kv_stage(TM):
            st = {}
            st["bf512"] = mk_stage("stb", [128, 512], BF16, 3)
            st["f512"] = mk_stage("stf", [128, 512], F32, 3)
            st["craw"] = k.sb("craw", [128, 4, TM], F32); st["crawb"] = Buf("craw", True)
            st["csq"] = k.sb("csq", [128, 4, TM], BF16); st["csqb"] = Buf("csq", True)
            st["srow"] = k.sb("srow", [128, TM], F32); st["srb"] = Buf("srow")
            st["cT"] = k.sb("cT", [128, 4, TM], BF16); st["cTb"] = Buf("cT", True); st["cTs"] = k.slot("cTs")
            st["rc"] = k.sb("rc", [64, TM], F32); st["rs"] = k.sb("rs", [64, TM], F32); st["rcb"] = Buf("rc"); st["rsl"] = k.slot("rsl")
            st["krf"] = [k.sb(f"krf{i}", [64, TM], F32) for i in range(2)]; st["krfb"] = [Buf("krf0"), Buf("krf1")]
            st["krs"] = k.slot("krs"); st["krs2"] = k.slot("krs2")
            st["krb16"] = k.sb("krb16", [64, TM], BF16); st["krb16b"] = Buf("krb16")
            st["ctm"] = k.sb("ctm", [128, TM // 128, 512], F32); st["ctmb"] = Buf("ctm", True)
            st["acc"] = k.sb("acc", [128, 8], F32); st["accb"] = Buf("acc", True)
            st["junk"] = k.sb("junk", [128, 256], F32); st["junkb"] = Buf("junk", True)
            st["s1"] = k.sb("s1", [128, 1], F32); st["s1b"] = Buf("s1")
            st["tmpc"] = k.sb("tmpc", [128, 1], F32); st["tmpb"] = Buf("tmpc")
            return st

        def kv_project(a, T, ropeC_ap, ropeS_ap, kst, scr=None, out=None, kmax=None):
            hT, hb = a["hT"], a["hb"]
            ntb = (T + 127) // 128
            TB = min(T, 128)
            def ev_k(h, bank, bb):
                if scr is not None:
                    t, tb_, sl = nxt(kst["bf512"])
                    k.op(act, lambda e: e.activation(out=t[:, 0:T], in_=bank[:, 0:T], func=AF.Copy), [bb], [tb_])
                    for (dst, c0, c1) in scr["kT"](h):
                        k.dma(sp, dst, t[:, c0:c1], [tb_], [scr["buf"]], sl)
                if out is not None:
                    t, tb_, sl = nxt(kst["f512"])
                    k.op(dve, lambda e: e.tensor_copy(out=t[:, 0:T], in_=bank[:, 0:T]), [bb], [tb_])
                    k.dma(sp, out["kT"](h), t[:, 0:T], [tb_], [B_out], sl)
            gemm_fm("w_sbk", range(16), hT, hb, T, ev_k)
            craw = kst["craw"]; crawb = kst["crawb"]; csq = kst["csq"]; csqb = kst["csqb"]
            def ev_c(cc, bank, bb):
                k.op(act, lambda e: e.activation(out=craw[:, cc, 0:T], in_=bank[:, 0:T], func=AF.Copy), [bb], [crawb])
                k.op(act, lambda e: e.activation(out=csq[:, cc, 0:T], in_=bank[:, 0:T], func=AF.Square), [bb], [csqb])
            srow = kst["srow"]; srb = kst["srb"]
            cT = kst["cT"]; cTb = kst["cTb"]; cTs = kst["cTs"]
            if scr is not None:
                gemm_fm("w_ckvF", range(4), hT, hb, T, ev_c)
                b = next_bank()
                k.mm([(banks[b][:, 0:T], ones, csq[:, cc, 0:T], cc == 0, cc == 3) for cc in range(4)], [csqb, B_c], [bbuf[b]])
                rstd_from_bank(banks[b][:, 0:T], bbuf[b], 512, srow[:, 0:T], srb)
                for cc in range(4):
                    k.op(dve, lambda e: e.scalar_tensor_tensor(out=cT[:, cc, 0:T], in0=craw[:, cc, 0:T], scalar=gn[:, 168 + cc:169 + cc], in1=srow[:, 0:T], op0=ALU.mult, op1=ALU.mult), [crawb, srb, B_c], [cTb])
                for (dst, c0, c1) in scr["ckvT"]():
                    k.dma(sp, dst, cT[:, :, c0:c1], [cTb], [scr["buf"]], cTs)
                if kmax is not None:
                    k.op(act, lambda e: e.activation(out=csq[:, :, 0:T], in_=cT[:, :, 0:T], func=AF.Square), [cTb], [csqb])
                    b = next_bank()
                    k.mm([(banks[b][:, 0:T], ones, csq[:, cc, 0:T], cc == 0, cc == 3) for cc in range(4)], [csqb, B_c], [bbuf[b]])
                    k.op(dve, lambda e: e.tensor_reduce(out=kst["tmpc"][:, 0:1], in_=banks[b][:, 0:T], axis=AX.X, op=ALU.max), [bbuf[b]], [kst["tmpb"]])
                    k.op(dve, lambda e: e.tensor_tensor(out=kmx[:, kmax[0]:kmax[0] + 1], in0=kmx[:, kmax[0]:kmax[0] + 1], in1=kst["tmpc"][:, 0:1], op=ALU.max), [kst["tmpb"], kmxb], [kmxb])
            rc = kst["rc"]; rs = kst["rs"]; rcb = kst["rcb"]; rsl = kst["rsl"]
            k.dma(sp, rc[:, 0:T], ropeC_ap, [], [rcb], rsl)
            k.dma(sp, rs[:, 0:T], ropeS_ap, [], [rcb], rsl)
            krf = kst["krf"]; krfb = kst["krfb"]
            def ev_r(g, bank, bb):
                tab = rc if g == 0 else rs
                k.op(dve, lambda e: e.tensor_tensor(out=krf[g][:, 0:T], in0=bank[0:64, 0:T], in1=tab[:, 0:T], op=ALU.mult), [bb, rcb], [krfb[g]])
            gemm_fm("w_kr", range(2), hT, hb, T, ev_r, M=64, gpl=2)
            k.op(dve, lambda e: e.tensor_tensor(out=krf[0][:, 0:T], in0=krf[0][:, 0:T], in1=krf[1][:, 0:T], op=ALU.add), [krfb[0], krfb[1]], [krfb[0]])
            if out is not None:
                k.dma(sp, out["krT"](), krf[0][:, 0:T], [krfb[0]], [B_out], kst["krs"])
            if scr is not None:
                krb16 = kst["krb16"]; krb16b = kst["krb16b"]
                k.op(act, lambda e: e.activation(out=krb16[:, 0:T], in_=krf[0][:, 0:T], func=AF.Copy), [krfb[0]], [krb16b])
                for (dst, c0, c1) in scr["krT"]():
                    k.dma(sp, dst, krb16[:, c0:c1], [krb16b], [scr["buf"]], kst["krs2"])
                if kmax is not None:
                    k.op(act, lambda e: e.activation(out=csq[0:64, 0, 0:T], in_=krb16[:, 0:T], func=AF.Square), [krb16b], [csqb])
                    b = next_bank()
                    k.mm([(banks[b][:, 0:T], ones[0:64, :], csq[0:64, 0, 0:T], True, True)], [csqb, B_c], [bbuf[b]])
                    k.op(dve, lambda e: e.tensor_reduce(out=kst["tmpc"][:, 0:1], in_=banks[b][:, 0:T], axis=AX.X, op=ALU.max), [bbuf[b]], [kst["tmpb"]])
                    k.op(dve, lambda e: e.tensor_tensor(out=kmx[:, kmax[1]:kmax[1] + 1], in0=kmx[:, kmax[1]:kmax[1] + 1], in1=kst["tmpc"][:, 0:1], op=ALU.max), [kst["tmpb"], kmxb], [kmxb])
            ctm = kst["ctm"]; ctmb = kst["ctmb"]; acc = kst["acc"]; accb = kst["accb"]
            k.op(dve, lambda e: e.memset(acc[:], 0.0), [], [accb])

            def tm_group(w, g, evac_tb, wt, wb):
                for tb in range(ntb):
                    b = next_bank()
                    mms = [(banks[b][0:TB, 0:256], hT[:, kk, tb * 128:tb * 128 + TB], wt[:, 0, kk, :], kk == 0, kk == KC - 1) for kk in range(KC)]
                    k.mm(mms, wb + [hb], [bbuf[b]])
                    evac_tb(g, tb, banks[b], bbuf[b])

            def ev_ctm(g, tb, bank, bb):
                k.op(act, lambda e: e.activation(out=ctm[0:TB, tb, g * 256:(g + 1) * 256], in_=bank[0:TB, 0:256], func=AF.Copy), [bb], [ctmb])
                k.op(act, lambda e: e.activation(out=kst["junk"][0:TB, 0:256], in_=bank[0:TB, 0:256], func=AF.Square, accum_out=acc[0:TB, tb * 2 + g:tb * 2 + g + 1]), [bb, accb], [accb, kst["junkb"]])
            tm_jobs = [("w_ckvT", g, 0) for g in range(2)] + [("w_sbv", g, 1) for g in range(8)]
            tm_pend = [wload(tm_jobs[0][0], tm_jobs[0][1], 1)]

            def tm_run(ji, ev):
                if ji + 1 < len(tm_jobs):
                    tm_pend.append(wload(tm_jobs[ji + 1][0], tm_jobs[ji + 1][1], 1))
                wt, wb = tm_pend.pop(0)
                tm_group(tm_jobs[ji][0], tm_jobs[ji][1], ev, wt, wb)
            for g in range(2):
                tm_run(g, ev_ctm)
            s1 = kst["s1"]; s1b = kst["s1b"]
            for tb in range(ntb):
                k.op(dve, lambda e: e.tensor_tensor(out=s1[0:TB, 0:1], in0=acc[0:TB, tb * 2:tb * 2 + 1], in1=acc[0:TB, tb * 2 + 1:tb * 2 + 2], op=ALU.add), [accb], [s1b])
                k.op(act, lambda e: e.activation(out=s1[0:TB, 0:1], in_=s1[0:TB, 0:1], func=AF.Sqrt, bias=epsb[0:TB, 0:1], scale=1.0 / 512), [s1b, B_c], [s1b])
                k.op(dve, lambda e: e.reciprocal(out=s1[0:TB, 0:1], in_=s1[0:TB, 0:1]), [s1b], [s1b])
                t, tb_, sl = nxt(kst["f512"])
                k.op(dve, lambda e: e.scalar_tensor_tensor(out=t[0:TB, 0:512], in0=ctm[0:TB, tb, :], scalar=s1[0:TB, 0:1], in1=gkvr[0:TB, :], op0=ALU.mult, op1=ALU.mult), [ctmb, s1b, B_c], [tb_])
                if out is not None:
                    k.dma(sp, out["ckv"](tb), t[0:TB, 0:512], [tb_], [B_out], sl)
                if scr is not None:
                    t2, t2b, sl2 = nxt(kst["bf512"])
                    k.op(act, lambda e: e.activation(out=t2[0:TB, 0:512], in_=t[0:TB, 0:512], func=AF.Copy), [tb_], [t2b])
                    for (dst, p0, p1) in scr["ckv"](tb):
                        k.dma(sp, dst, t2[p0:p1, 0:512], [t2b], [scr["buf"]], sl2)

            def ev_v(g, tb, bank, bb):
                if out is not None:
                    t, tb_, sl = nxt(kst["f512"])
                    k.op(dve, lambda e: e.tensor_copy(out=t[0:TB, 0:256], in_=bank[0:TB, 0:256]), [bb], [tb_])
                    k.dma(sp, out["v"](tb, g), t[0:TB, 0:256], [tb_], [B_out], sl)
                if scr is not None:
                    t2, t2b, sl2 = nxt(kst["bf512"])
                    k.op(act, lambda e: e.activation(out=t2[0:TB, 0:256], in_=bank[0:TB, 0:256], func=AF.Copy), [bb], [t2b])
                    for (dst, p0, p1) in scr["v"](tb, g):
                        k.dma(sp, dst, t2[p0:p1, 0:256].rearrange("p (h d) -> p h d", h=2), [t2b], [scr["buf"]], sl2)
            for g in range(8):
                tm_run(2 + g, ev_v)

        if "PM" in stages:
            a = proj_arena(256)
            T = 256
            front(memT, T, 128, a["hT"], a["hb"], a["xs"], a["xsb"], a["xslots"], a["sq"], a["sqb"], a["rrow"], a["rb"])
            cp(3)
            stf = mk_stage("mstf", [128, 256], F32, 2)

            def ev_mk(h, bank, bb):
                k.op(act, lambda e: e.activation(out=MK[:, h, :], in_=bank[:, 0:256], func=AF.Copy), [bb], [MKb])
            gemm_fm("w_mkF", range(4), a["hT"], a["hb"], T, ev_mk)
            cp(4)
            for (w, dst, tomv) in (("w_mkT", o_memk, False), ("w_mvT", o_memv, True)):
                for g in range(2):
                    wt, wb = wload(w, g, 1)
                    cp(5)
                    for tb in range(2):
                        b = next_bank()
                        mms = [(banks[b][:, 0:256], a["hT"][:, kk, tb * 128:(tb + 1) * 128], wt[:, 0, kk, :], kk == 0, kk == KC - 1) for kk in range(KC)]
                        k.mm(mms, wb + [a["hb"]], [bbuf[b]])
                        cp(6)
                        t, tb_, sl = nxt(stf)
                        k.op(dve, lambda e: e.tensor_copy(out=t[:, 0:256], in_=banks[b][:, 0:256]), [bbuf[b]], [tb_])
                        cp(7)
                        k.dma(sp, dst[tb * 128:(tb + 1) * 128, g * 256:(g + 1) * 256], t[:, 0:256], [tb_], [B_out], sl)
                        cp(8)
                        if tomv:
                            cp(11)
                            k.op(act, lambda e: e.activation(out=MV[:, tb, g * 256:(g + 1) * 256], in_=t[:, 0:256], func=AF.Copy), [tb_], [MKb])
                            cp(12)
                if not tomv:
                    cp(10)
            cp(9)
            k.barrier()

        if "PA" in stages:
            a = proj_arena(512)
            kst = kv_stage(512)
            for t in range(16):
                T = 512; t0 = t * 512
                front(xT_seq[:, :, t0:t0 + T], T, 0, a["hT"], a["hb"], a["xs"], a["xsb"], a["xslots"], a["sq"], a["sqb"], a["rrow"], a["rb"])
                scr = {
                    "buf": B_S,
                    "kT": lambda h: [(S_kT[h, :, t0:t0 + 512], 0, 512)],
                    "ckvT": lambda: [(S_ckvT[:, :, t0:t0 + 512], 0, 512)],
                    "krT": lambda: [(S_krT[:, t0:t0 + 512], 0, 512)],
                    "ckv": lambda tb: [(S_ckv[:, t * 4 + tb, :], 0, 128)],
                    "v": lambda tb, g: [(S_v[2 * g:2 * g + 2, :, t * 4 + tb, :].rearrange("h p d -> p h d"), 0, 128)],
                }
                kv_project(a, T, ropeC_seq[:, t0:t0 + T], ropeS_seq[:, t0:t0 + T], kst, scr=scr, out=None, kmax=(0, 1))
            k.barrier()

        if "PB1" in stages:
            TM = 256
            own_tiles1 = [(i * 256, 256) for i in range(8)] + [(2048, 128)]
            a = proj_arena(TM)
            kst = kv_stage(TM)
            zs = k.slot("zcast")
            for s in range(2):
                k.dma(pool, Z_ckvT[s, :, :, 0:PAST], c_ckvT[s * 128:(s + 1) * 128, :].rearrange("p (c n) -> p c n", c=4), [], [B_Z], zs)
                k.dma(pool, Z_ckv[s, :, 0:32, :], c_ckv[s * 128:(s + 1) * 128, :].rearrange("p (j n) -> p j n", j=32), [], [B_Z], zs)
                k.dma(pool, Z_krT[s, :, 0:PAST], c_krT[s * 64:(s + 1) * 64, :], [], [B_Z], zs)
                k.dma(pool, Z_kT[s, :, :, 0:PAST], c_kT[s * 2048:(s + 1) * 2048, :].rearrange("(h p) n -> h p n", h=16), [], [B_Z], zs)
                k.dma(pool, Z_v[s, :, :, 0:32, :], c_v[s * 2048:(s + 1) * 2048, :].rearrange("(h p) (j d) -> h p j d", h=16, d=128), [], [B_Z], zs)
            cqraw = k.sb("cqraw", [128, 8, TM], F32); cqrawb = Buf("cqraw", True)
            cqsq = k.sb("cqsq", [128, 8, TM], BF16); cqsqb = Buf("cqsq", True)
            cqn = k.sb("cqn", [128, 8, TM], BF16); cqnb = Buf("cqn", True)
            qn = k.sb("qn", [128, 16, TM], BF16); qnb = Buf("qn", True)
            qst = mk_stage("qst", [128, 4, TM], BF16, 2)
            sbqst = mk_stage("sbqst", [128, TM], BF16, 3)
            qr0 = k.sb("qr0", [64, TM], F32); qr1 = k.sb("qr1", [64, TM], F32); qrb = [Buf("qr0"), Buf("qr1")]
            qrst = mk_stage("qrst", [64, TM], BF16, 2)
            wuk = k.sb("wuk", [128, 64, 128], BF16); wukb = Buf("wuk"); wuks = k.slot("wuk")
            k.dma(sp, wuk[:], Wb["w_ukT"].rearrange("p (k n) -> p k n", n=128), [Wbuf["w_ukT"]], [wukb], wuks)
            for (t0, T) in own_tiles1:
                front(xT_own[:, :, t0:t0 + T], T, 0, a["hT"], a["hb"], a["xs"], a["xsb"], a["xslots"], a["sq"], a["sqb"], a["rrow"], a["rb"])
                is_s = (T == 128)
                out = {
                    "kT": lambda h: o_kT[h, :, t0:t0 + T],
                    "krT": lambda: o_krT[:, t0:t0 + T],
                    "ckv": lambda tb: o_ckv[t0 + tb * 128:t0 + tb * 128 + 128, :],
                    "v": lambda tb, g: o_v[t0 + tb * 128:t0 + tb * 128 + 128, g * 256:(g + 1) * 256],
                }
                scr = None
                if is_s:
                    scr = {
                        "buf": B_Z,
                        "kT": lambda h: [(Z_kT[s, h, :, PAST:ZK], s * 64, s * 64 + 64) for s in range(2)],
                        "ckvT": lambda: [(Z_ckvT[s, :, :, PAST:ZK], s * 64, s * 64 + 64) for s in range(2)],
                        "krT": lambda: [(Z_krT[s, :, PAST:ZK], s * 64, s * 64 + 64) for s in range(2)],
                        "ckv": lambda tb: [(Z_ckv[s, 0:64, 32, :], s * 64, s * 64 + 64) for s in range(2)],
                        "v": lambda tb, g: [(Z_v[s, 2 * g:2 * g + 2, 0:64, 32, :].rearrange("h p d -> p h d"), s * 64, s * 64 + 64) for s in range(2)],
                    }
                kv_project(a, T, ropeC_own[:, t0:t0 + T], ropeS_own[:, t0:t0 + T], kst, scr=scr, out=out, kmax=None)
                hT, hb = a["hT"], a["hb"]

                def ev_cq(g, bank, bb):
                    k.op(act, lambda e: e.activation(out=cqraw[:, g, 0:T], in_=bank[:, 0:T], func=AF.Copy), [bb], [cqrawb])
                    k.op(act, lambda e: e.activation(out=cqsq[:, g, 0:T], in_=bank[:, 0:T], func=AF.Square), [bb], [cqsqb])
                gemm_fm("w_cq", range(8), hT, hb, T, ev_cq)
                b = next_bank()
                k.mm([(banks[b][:, 0:T], ones, cqsq[:, g, 0:T], g == 0, g == 7) for g in range(8)], [cqsqb, B_c], [bbuf[b]])
                srow = kst["srow"]; srb = kst["srb"]
                rstd_from_bank(banks[b][:, 0:T], bbuf[b], 1024, srow[:, 0:T], srb)
                for g in range(8):
                    eng = dve
                    k.op(eng, lambda e: e.scalar_tensor_tensor(out=cqn[:, g, 0:T], in0=cqraw[:, g, 0:T], scalar=gn[:, 160 + g:161 + g], in1=srow[:, 0:T], op0=ALU.mult, op1=ALU.mult), [cqrawb, srb, B_c], [cqnb])

                def ev_sbq(h, bank, bb):
                    t, tb_, sl = nxt(sbqst)
                    k.op(act, lambda e: e.activation(out=t[:, 0:T], in_=bank[:, 0:T], func=AF.Copy), [bb], [tb_])
                    k.dma(sp, Q_sb[:, h, t0:t0 + T], t[:, 0:T], [tb_], [B_Q], sl)
                gemm_fm("w_sbq", range(16), hT, hb, T, ev_sbq)

                def ev_qn(h, bank, bb):
                    k.op(act, lambda e: e.activation(out=qn[:, h, 0:T], in_=bank[:, 0:T], func=AF.Copy), [bb], [qnb])
                gemm_fm("w_uqn", range(16), cqn, cqnb, T, ev_qn, gpl=4)
                rc = kst["rc"]; rs = kst["rs"]; rcb = kst["rcb"]

                def ev_qr(g, bank, bb):
                    h, z = g // 2, g % 2
                    if z == 0:
                        k.op(dve, lambda e: e.tensor_tensor(out=qr0[:, 0:T], in0=bank[0:64, 0:T], in1=rc[:, 0:T], op=ALU.mult), [bb, rcb], [qrb[0]])
                    else:
                        k.op(dve, lambda e: e.tensor_tensor(out=qr1[:, 0:T], in0=bank[0:64, 0:T], in1=rs[:, 0:T], op=ALU.mult), [bb, rcb], [qrb[1]])
                        t, tb_, sl = nxt(qrst)
                        k.op(pool, lambda e: e.tensor_tensor(out=t[:, 0:T], in0=qr0[:, 0:T], in1=qr1[:, 0:T], op=ALU.add), [qrb[0], qrb[1]], [tb_])
                        k.dma(sp, Q_rope[:, h, t0:t0 + T], t[:, 0:T], [tb_], [B_Q], sl)
                gemm_fm("w_uqr", range(32), cqn, cqnb, T, ev_qr, M=64, gpl=8)
                for h in range(16):
                    t, tb_, sl = nxt(qst)
                    for cc in range(4):
                        b = next_bank()
                        k.mm([(banks[b][:, 0:T], wuk[:, h * 4 + cc, :], qn[:, h, 0:T], True, True)], [wukb, qnb], [bbuf[b]])
                        if cc % 2 == 0:
                            k.op(act, lambda e: e.activation(out=t[:, cc, 0:T], in_=banks[b][:, 0:T], func=AF.Copy), [bbuf[b]], [tb_])
                        else:
                            k.op(dve, lambda e: e.tensor_copy(out=t[:, cc, 0:T], in_=banks[b][:, 0:T]), [bbuf[b]], [tb_])
                    k.dma(sp, Q_lat[:, :, h, t0:t0 + T], t[:, :, 0:T], [tb_], [B_Q], sl)
            k.barrier()

        if "PB2" in stages:
            k.sb_off = ARENA0
            wuv = k.sb("wuv", [128, 64, 128], BF16); wuvb = Buf("wuv"); wuvs = k.slot("wuv")
            k.dma(sp, wuv[:], Wb["w_uvr"].rearrange("p (k n) -> p k n", n=128), [Wbuf["w_uvr"]], [wuvb], wuvs)
            qlat = k.sb("qlat", [128, 4, 2048], BF16); qrope = k.sb("qrope", [64, 2048], BF16); sbq = k.sb("sbq", [128, 2048], BF16)
            qb_ = Buf("qtiles"); qsl = k.slot("qtiles")
            qsq = k.sb("qsq", [128, 4, 512], BF16); qsqb = Buf("qsq")
            rrow = k.sb("r_row", [128, 512], F32); rrb = Buf("r_row")
            rbf = k.sb("r_bf", [1, 512], BF16); rbfb = Buf("r_bf")
            KT = [dict(ckvT=k.sb(f"ktc{i}", [128, 4, 512], BF16), krT=k.sb(f"ktr{i}", [64, 512], BF16), ckv=k.sb(f"ktv{i}", [128, 4, 512], BF16), b=Buf(f"kt{i}"), s=k.slot(f"kt{i}")) for i in range(2)]
            PT = [(k.sb(f"PT{i}", [128, 512], BF16), Buf(f"PT{i}")) for i in range(2)]
            linv = k.sb("linv", [128, 512], F32); linvb = Buf("linv")
            olat = k.sb("olat", [128, 4, 512], BF16); olatb = Buf("olat", True)
            oast = k.sb("oast", [128, 2048], BF16); oastb = Buf("oast", True); oasl = k.slot("oast")
            obst = k.sb("obst", [128, 2048], BF16); obstb = Buf("obst", True); obsl = k.slot("obst")
            NKMAX = 8704
            SK = [dict(kT=k.sb(f"skT{i}", [128, NKMAX], BF16), v=k.sb(f"sv{i}", [128, 68, 128], BF16), b=Buf(f"sk{i}"), s=k.slot(f"sk{i}")) for i in range(2)]
            ebuf = [(k.sb(f"e{i}", [128, 512], F32), Buf(f"e{i}")) for i in range(2)]
            spb = [(k.sb(f"sp{i}", [128, 512], BF16), Buf(f"sp{i}")) for i in range(2)]
            e2b = [(k.sb(f"e2{i}", [128, 512], F32), Buf(f"e2{i}")) for i in range(2)]
            wTb = [(k.sb(f"wT{i}", [128, 512], BF16), Buf(f"wT{i}")) for i in range(2)]
            carry = [(k.sb(f"carry{i}", [1, 128], BF16), Buf(f"carry{i}")) for i in range(2)]
            km2 = k.sb("km2", [128, 2], F32); km2b = Buf("km2")
            ztmp = k.sb("ztmp", [128, 4, 512], BF16); ztmpb = Buf("ztmp"); ztsl = k.slot("ztmp")

            qsets = []
            for m in range(16):
                tiles = [dict(k0=kt * 512, nb=4, bs=128, diag=(kt == m), jb0=kt * 4) for kt in range(m + 1)]
                qsets.append(dict(t0=m * 128, NQ=128, src="S", s=None, tiles=tiles))
            for s in range(2):
                tiles = [dict(k0=kt * 512, nb=4, bs=128, diag=False, jb0=kt * 4) for kt in range(8)]
                tiles.append(dict(k0=PAST, nb=1, bs=64, diag=True, jb0=32))
                qsets.append(dict(t0=2048 + s * 64, NQ=64, src="Z", s=s, tiles=tiles))

            def kmax_pass(src_ckvT, src_krT, n, ccol, rcol):
                k.dma(sp, ztmp[:, :, 0:n], src_ckvT, [B_Z], [ztmpb], ztsl)
                k.op(act, lambda e: e.activation(out=qsq[:, :, 0:n], in_=ztmp[:, :, 0:n], func=AF.Square), [ztmpb], [qsqb])
                b = next_bank()
                k.mm([(banks[b][:, 0:n], ones, qsq[:, cc, 0:n], cc == 0, cc == 3) for cc in range(4)], [qsqb, B_c], [bbuf[b]])
                k.op(dve, lambda e: e.tensor_reduce(out=km2[:, 0:1], in_=banks[b][:, 0:n], axis=AX.X, op=ALU.max), [bbuf[b]], [km2b])
                k.op(dve, lambda e: e.tensor_tensor(out=kmx[:, ccol:ccol + 1], in0=kmx[:, ccol:ccol + 1], in1=km2[:, 0:1], op=ALU.max), [km2b, kmxb], [kmxb])
                k.dma(sp, ztmp[0:64, 0, 0:n], src_krT, [B_Z], [ztmpb], ztsl)
                k.op(act, lambda e: e.activation(out=qsq[0:64, 0, 0:n], in_=ztmp[0:64, 0, 0:n], func=AF.Square), [ztmpb], [qsqb])
                b = next_bank()
                k.mm([(banks[b][:, 0:n], ones[0:64, :], qsq[0:64, 0, 0:n], True, True)], [qsqb, B_c], [bbuf[b]])
                k.op(dve, lambda e: e.tensor_reduce(out=km2[:, 0:1], in_=banks[b][:, 0:n], axis=AX.X, op=ALU.max), [bbuf[b]], [km2b])
                k.op(dve, lambda e: e.tensor_tensor(out=kmx[:, rcol:rcol + 1], in0=kmx[:, rcol:rcol + 1], in1=km2[:, 0:1], op=ALU.max), [km2b, kmxb], [kmxb])
            for s in range(2):
                for kt in range(8):
                    kmax_pass(Z_ckvT[s, :, :, kt * 512:(kt + 1) * 512], Z_krT[s, :, kt * 512:(kt + 1) * 512], 512, 2 + 2 * s, 3 + 2 * s)
                kmax_pass(Z_ckvT[s, :, :, PAST:ZK], Z_krT[s, :, PAST:ZK], 64, 2 + 2 * s, 3 + 2 * s)

            PT3 = PT + [(k.sb("PT2", [128, 512], BF16), Buf("PT2"))]
            rbfa = k.sb("r_bfa", [1, 2048], BF16); rbfab = Buf("r_bfa")
            e3 = ebuf + [(k.sb("e_2", [128, 512], F32), Buf("e_2"))]
            sp3 = spb + [(k.sb("sp_2", [128, 512], BF16), Buf("sp_2"))]
            w3 = wTb + [(k.sb("wT_2", [128, 512], BF16), Buf("wT_2"))]
            kt_rr = [0]; sk_rr = [0]
            for qs in qsets:
                t0, NQ, src, s = qs["t0"], qs["NQ"], qs["src"], qs["s"]
                HG = 512 // NQ; NG = 16 // HG
                scr_buf = B_S if src == "S" else B_Z
                for cc in range(4):
                    k.dma(sp, qlat[:, cc, 0:16 * NQ].rearrange("p (h q) -> p h q", q=NQ), Q_lat[:, cc, :, t0:t0 + NQ], [B_Q], [qb_], qsl)
                k.dma(sp, qrope[:, 0:16 * NQ].rearrange("p (h q) -> p h q", q=NQ), Q_rope[:, :, t0:t0 + NQ], [B_Q], [qb_], qsl)
                k.dma(sp, sbq[:, 0:16 * NQ].rearrange("p (h q) -> p h q", q=NQ), Q_sb[:, :, t0:t0 + NQ], [B_Q], [qb_], qsl)
                ccol, rcol = (0, 1) if src == "S" else (2 + 2 * s, 3 + 2 * s)
                k.op(dve, lambda e: e.tensor_tensor(out=km2[:, 1:2], in0=kmx[:, ccol:ccol + 1], in1=kmx[:, rcol:rcol + 1], op=ALU.add), [kmxb], [km2b])
                for hg in range(NG):
                    c0q = hg * 512
                    k.op(act, lambda e: e.activation(out=qsq[:, :, :], in_=qlat[:, :, c0q:c0q + 512], func=AF.Square), [qb_], [qsqb])
                    b = 7
                    k.mm([(banks[b][:, :], ones, qsq[:, cc, :], cc == 0, False) for cc in range(4)], [qsqb, B_c], [bbuf[b]])
                    k.op(act, lambda e: e.activation(out=qsq[0:64, 0, :], in_=qrope[:, c0q:c0q + 512], func=AF.Square), [qb_], [qsqb])
                    k.mm([(banks[b][:, :], ones[0:64, :], qsq[0:64, 0, :], False, True)], [qsqb, B_c], [bbuf[b]])
                    k.op(act, lambda e: e.activation(out=rrow[:, :], in_=banks[b][:, :], func=AF.Sqrt, scale=km2[:, 1:2]), [bbuf[b], km2b], [rrb])
                    k.op(dve, lambda e: e.tensor_scalar(out=rbfa[0:1, c0q:c0q + 512], in0=rrow[0:1, :], scalar1=-1.02, scalar2=None, op0=ALU.mult), [rrb], [rbfab])
                for hg in range(NG):
                    c0q = hg * 512
                    qrv = qrope[:, c0q:c0q + 512]
                    blocks = []
                    ntl = len(qs["tiles"])
                    for ti, tl in enumerate(qs["tiles"]):
                        for j in range(tl["nb"]):
                            blocks.append((ti, tl, j))
                    pend = None
                    K_ = None
                    for bi, (ti, tl, j) in enumerate(blocks):
                        nb, bs, k0, jb0 = tl["nb"], tl["bs"], tl["k0"], tl["jb0"]
                        nk = nb * bs
                        if j == 0:
                            K_ = KT[kt_rr[0] % 2]; kt_rr[0] += 1
                            if src == "S":
                                k.dma(sp, K_["ckvT"][:, :, 0:nk], S_ckvT[:, :, k0:k0 + nk], [scr_buf], [K_["b"]], K_["s"])
                                k.dma(sp, K_["krT"][:, 0:nk], S_krT[:, k0:k0 + nk], [scr_buf], [K_["b"]], K_["s"])
                                k.dma(sp, K_["ckv"][0:bs, 0:nb, :], S_ckv[0:bs, jb0:jb0 + nb, :], [scr_buf], [K_["b"]], K_["s"])
                            else:
                                k.dma(sp, K_["ckvT"][:, :, 0:nk], Z_ckvT[s, :, :, k0:k0 + nk], [scr_buf], [K_["b"]], K_["s"])
                                k.dma(sp, K_["krT"][:, 0:nk], Z_krT[s, :, k0:k0 + nk], [scr_buf], [K_["b"]], K_["s"])
                                k.dma(sp, K_["ckv"][0:bs, 0:nb, :], Z_ckv[s, 0:bs, jb0:jb0 + nb, :], [scr_buf], [K_["b"]], K_["s"])
                        bS = bi % 2
                        use_mask = tl["diag"] and src == "S"
                        mms = [(banks[bS][0:bs, :], K_["ckvT"][:, cc, j * bs:(j + 1) * bs], qlat[:, cc, c0q:c0q + 512], cc == 0, False) for cc in range(4)]
                        mms.append((banks[bS][0:bs, :], K_["krT"][:, j * bs:(j + 1) * bs], qrv, False, False))
                        mms.append((banks[bS][0:bs, :], ones[0:1, 0:bs], rbfa[0:1, c0q:c0q + 512], False, not use_mask))
                        if use_mask:
                            mms.append((banks[bS][0:bs, :], ident, msk[:, j * 512:(j + 1) * 512], False, True))
                        k.mm(mms, [K_["b"], qb_, rbfab, B_c], [bbuf[bS]])
                        pt, ptb = PT3[bi % 3]
                        k.op(act, lambda e: e.activation(out=pt[0:bs, :], in_=banks[bS][0:bs, :], func=AF.Exp, scale=MLA_SCALE), [bbuf[bS]], [ptb])

                        def pv(p_):
                            (pbi, pK, pbs, pj, ppt, pptb) = p_
                            first = (pbi == 0); last = (pbi == len(blocks) - 1)
                            mms2 = [(banks[2 + cc][:, :], pK["ckv"][0:pbs, pj, cc * 128:(cc + 1) * 128], ppt[0:pbs, :], first, last) for cc in range(4)]
                            mms2.append((banks[6][:, :], ones[0:pbs, :], ppt[0:pbs, :], first, last))
                            k.mm(mms2, [pK["b"], pptb, B_c], [bbuf[2], bbuf[3], bbuf[4], bbuf[5], bbuf[6]])
                        if pend is not None:
                            pv(pend)
                        pend = (bi, K_, bs, j, pt, ptb)
                    pv(pend)
                    k.op(dve, lambda e: e.reciprocal(out=linv[:, :], in_=banks[6][:, :]), [bbuf[6]], [linvb])
                    for cc in range(4):
                        k.op(dve, lambda e: e.tensor_tensor(out=olat[:, cc, :], in0=banks[2 + cc][:, :], in1=linv[:, :], op=ALU.mult), [bbuf[2 + cc], linvb], [olatb])
                    b = 7
                    mms = []
                    for hh in range(HG):
                        for cc in range(4):
                            mms.append((banks[b][:, hh * NQ:(hh + 1) * NQ], wuv[:, (hg * HG + hh) * 4 + cc, :], olat[:, cc, hh * NQ:(hh + 1) * NQ], cc == 0, cc == 3))
                    k.mm(mms, [wuvb, olatb], [bbuf[b]])
                    k.op(act, lambda e: e.activation(out=oast[:, c0q:c0q + 512], in_=banks[b][:, :], func=AF.Copy), [bbuf[b]], [oastb])
                k.dma(sp, O_a[:, :, t0:t0 + NQ], oast[:, 0:16 * NQ].rearrange("p (h q) -> p h q", q=NQ), [oastb], [B_O], oasl)
                nkeys = sum(tl["nb"] * tl["bs"] for tl in qs["tiles"])
                ntl = len(qs["tiles"])
                steps = [(h, ti) for h in range(16) for ti in range(ntl - 1, -1, -1)]
                NS = len(steps)
                skof = {}

                def sb_load(h):
                    S_ = SK[sk_rr[0] % 2]; sk_rr[0] += 1
                    if src == "S":
                        k.dma(sp, S_["kT"][:, 0:nkeys], S_kT[h, :, 0:nkeys], [scr_buf], [S_["b"]], S_["s"])
                        k.dma(sp, S_["v"][:, 0:nkeys // 128, :], S_v[h, :, 0:nkeys // 128, :], [scr_buf], [S_["b"]], S_["s"])
                    else:
                        k.dma(sp, S_["kT"][:, 0:nkeys], Z_kT[s, h, :, 0:nkeys], [scr_buf], [S_["b"]], S_["s"])
                        k.dma(sp, S_["v"][:, 0:33, :], Z_v[s, h, :, 0:33, :], [scr_buf], [S_["b"]], S_["s"])
                    skof[h] = S_
                first_idx = {h: h * ntl for h in range(16)}

                def stage1(i):
                    h, ti = steps[i]
                    S_ = skof[h]
                    tl = qs["tiles"][ti]
                    nb, bs, k0 = tl["nb"], tl["bs"], tl["k0"]
                    W_ = nb * NQ
                    qh = sbq[:, h * NQ:(h + 1) * NQ]
                    bA = i % 2
                    mms = []
                    for j in range(nb):
                        mms.append((banks[bA][0:bs, j * NQ:(j + 1) * NQ], S_["kT"][:, k0 + j * bs:k0 + (j + 1) * bs], qh, True, not tl["diag"]))
                        if tl["diag"]:
                            if src == "S":
                                mms.append((banks[bA][0:bs, j * NQ:(j + 1) * NQ], ident, msk[:, 2048 + j * 128:2048 + (j + 1) * 128], False, True))
                            else:
                                mms.append((banks[bA][0:bs, j * NQ:(j + 1) * NQ], ident[0:64, 0:64], msk[0:64, 2560:2624], False, True))
                    k.mm(mms, [S_["b"], qb_, B_c], [bbuf[bA]])
                    e_, eb = e3[i % 3]; sp_, spb_ = sp3[i % 3]
                    k.op(act, lambda e: e.activation(out=e_[0:bs, 0:W_], in_=banks[bA][0:bs, 0:W_], func=AF.Exp, scale=SB_SCALE), [bbuf[bA]], [eb])
                    k.op(act, lambda e: e.activation(out=sp_[0:bs, 0:W_], in_=e_[0:bs, 0:W_], func=AF.Ln, bias=1.0, scale=1.0), [eb], [spb_])

                def stage2(i):
                    h, ti = steps[i]
                    tl = qs["tiles"][ti]
                    nb, bs = tl["nb"], tl["bs"]
                    W_ = nb * NQ
                    e_, eb = e3[i % 3]; sp_, spb_ = sp3[i % 3]; e2_, e2b_ = e2b[i % 2]; w_, wb_ = w3[i % 3]
                    bB = 2 + (i % 2)
                    mms = [(banks[bB][0:bs, 0:W_], Umat[0:bs, 0:bs], sp_[0:bs, 0:W_], True, False)]
                    for sh in range(1, nb):
                        mms.append((banks[bB][0:bs, 0:(nb - sh) * NQ], ones[0:bs, 0:bs], sp_[0:bs, sh * NQ:W_], False, False))
                    rd = [spb_, B_c]
                    if ti != ntl - 1:
                        cp_ = carry[(i - 1) % 2]
                        for j in range(nb):
                            mms.append((banks[bB][0:bs, j * NQ:(j + 1) * NQ], ones[0:1, 0:bs], cp_[0][0:1, 0:NQ], False, False))
                        rd.append(cp_[1])
                    mms[-1] = mms[-1][:4] + (True,)
                    k.mm(mms, rd, [bbuf[bB]])
                    if ti > 0:
                        cn = carry[i % 2]
                        k.op(dve, lambda e: e.tensor_copy(out=cn[0][0:1, 0:NQ], in_=banks[bB][0:1, 0:NQ]), [bbuf[bB]], [cn[1]])
                    k.op(act, lambda e: e.activation(out=e2_[0:bs, 0:W_], in_=banks[bB][0:bs, 0:W_], func=AF.Exp, scale=-1.0), [bbuf[bB]], [e2b_])
                    k.op(pool, lambda e: e.tensor_tensor(out=w_[0:bs, 0:W_], in0=e_[0:bs, 0:W_], in1=e2_[0:bs, 0:W_], op=ALU.mult), [eb, e2b_], [wb_])

                def stage3(i):
                    h, ti = steps[i]
                    S_ = skof[h]
                    tl = qs["tiles"][ti]
                    nb, bs, jb0 = tl["nb"], tl["bs"], tl["jb0"]
                    w_, wb_ = w3[i % 3]
                    bC = 4 + (h % 2)
                    mms = []
                    for j in range(nb):
                        mms.append((banks[bC][:, 0:NQ], S_["v"][0:bs, jb0 + j, :], w_[0:bs, j * NQ:(j + 1) * NQ], ti == ntl - 1 and j == 0, ti == 0 and j == nb - 1))
                    k.mm(mms, [S_["b"], wb_], [bbuf[bC]])
                    if ti == 0:
                        k.op(act, lambda e: e.activation(out=obst[:, h * NQ:(h + 1) * NQ], in_=banks[bC][:, 0:NQ], func=AF.Copy), [bbuf[bC]], [obstb])
                DL = 3 if ntl >= 3 else 2
                for i in range(NS + DL):
                    if 0 <= i - DL < NS:
                        stage3(i - DL)
                    if 1 <= i <= NS:
                        stage2(i - 1)
                    if i < NS:
                        h_, ti_ = steps[i]
                        if h_ not in skof:
                            sb_load(h_)
                        if h_ + 1 < 16 and (h_ + 1) not in skof and i >= first_idx[h_] + DL - 1:
                            sb_load(h_ + 1)
                        stage1(i)
                k.dma(sp, O_b[:, :, t0:t0 + NQ], obst[:, 0:16 * NQ].rearrange("p (h q) -> p h q", q=NQ), [obstb], [B_O], obsl)

            k.barrier()

        own_tiles = [(0, 512), (512, 512), (1024, 512), (1536, 512), (2048, 128)]
        if "PB3" in stages:
            A0 = ARENA
            RA = Buf("p3_A", True)
            hT = k.sb("p3_hT", [128, KC, 512], BF16, off=A0)
            oa = k.sb("p3_oa", [128, 16, 512], BF16, off=A0 + 32768); ob = k.sb("p3_ob", [128, 16, 512], BF16, off=A0 + 49152)
            x1 = k.sb("p3_x1", [128, KC, 512], F32, off=A0)
            oasl2 = k.slot("p3_o"); x1sl = k.slot("p3_x1")
            mg = k.sb("p3_mg", [128, KC, 512], BF16, off=A0 + 65536); RB = Buf("p3_B", True)
            h2 = mg
            R0 = A0 + 98304
            RC = Buf("p3_C", True)
            ff = k.sb("p3_ff", [128, 22, 512], BF16, off=R0)
            xs = [k.sb(f"p3_xs{i}", [128, 4, 512], F32, off=R0 + i * 8192) for i in range(2)]
            sqf = [k.sb(f"p3_sqf{i}", [128, 4, 512], BF16, off=R0 + 16384 + i * 4096) for i in range(2)]
            xslots = [k.slot(f"p3xs{i}") for i in range(2)]
            mqT = k.sb("p3_mqT", [128, 4, 512], BF16, off=R0); atT = k.sb("p3_atT", [128, 4, 512], BF16, off=R0 + 4096)
            PTt = k.sb("p3_PT", [128, 2, 4, 512], BF16, off=R0 + 8192)
            Pn = [k.sb(f"p3_P{i}", [128, 256], BF16, off=R0 + 16384 + i * 512) for i in range(2)]
            ZMK = k.sb("p3_ZMK", [128, 2, 4, 256], BF16, off=R0 + 20480); ZMV = k.sb("p3_ZMV", [128, 2, 2, 512], BF16, off=R0 + 24576)
            k.sb_off = R0 + 32768
            sq = [k.sb(f"p3_sq{i}", [128, 4, 512], BF16) for i in range(2)]; sqb = [Buf("p3sq0"), Buf("p3sq1")]
            rrow = k.sb("p3_rrow", [128, 512], F32); rb = Buf("p3_rrow")
            ga = [(k.sb(f"p3_ga{i}", [128, 512], F32), Buf(f"p3_ga{i}")) for i in range(2)]
            tt = [(k.sb(f"p3_tt{i}", [128, 512], F32), Buf(f"p3_tt{i}")) for i in range(2)]
            st4 = [(k.sb(f"p3_st{i}", [128, 4], F32), Buf(f"p3_st{i}")) for i in range(2)]
            zms = k.slot("zm")
            yst = mk_stage("p3_y", [128, 512], F32, 3)
            bank_bf = [banks[i].bitcast(BF16) for i in range(8)]
            rr3 = [0]
            for (t0, T) in own_tiles:
                front(xT_own[:, :, t0:t0 + T], T, 0, hT, RA, xs, [RC, RC], xslots, sqf, [RC, RC], rrow, rb, NP=4)
                k.dma(sp, oa[:, :, 0:T], O_a[:, :, t0:t0 + T], [B_O], [RA], oasl2)
                k.dma(sp, ob[:, :, 0:T], O_b[:, :, t0:t0 + T], [B_O], [RA], oasl2)
                for j in range(32):
                    res = []
                    for (gidx, wbn, osrc, bcol) in ((j, "w_ba", oa, 172 + j), (32 + j, "w_bb", ob, 204 + j)):
                        wt, wb = wload("w_gate", gidx, 1)
                        bG = next_bank()
                        k.mm([(banks[bG][:, 0:T], wt[:, 0, kk, :], hT[:, kk, 0:T], kk == 0, kk == KC - 1) for kk in range(KC)], wb + [RA], [bbuf[bG]])
                        g_, gb_ = ga[rr3[0] % 2]
                        k.op(act, lambda e: e.activation(out=g_[:, 0:T], in_=banks[bG][:, 0:T], func=AF.Sigmoid, bias=gn[:, bcol:bcol + 1], scale=1.0), [bbuf[bG], B_c], [gb_])
                        wt2, wb2 = wload(wbn, j, 1)
                        bB = next_bank()
                        k.mm([(banks[bB][:, 0:T], wt2[:, 0, kk, :], osrc[:, kk, 0:T], kk == 0, kk == 15) for kk in range(16)], wb2 + [RA], [bbuf[bB]])
                        t_, tb_ = tt[rr3[0] % 2]; rr3[0] += 1
                        k.op(dve, lambda e: e.tensor_tensor(out=t_[:, 0:T], in0=banks[bB][:, 0:T], in1=g_[:, 0:T], op=ALU.mult), [bbuf[bB], gb_], [tb_])
                        res.append((t_, tb_))
                    k.op(pool, lambda e: e.tensor_tensor(out=mg[:, j, 0:T], in0=res[0][0][:, 0:T], in1=res[1][0][:, 0:T], op=ALU.add), [res[0][1], res[1][1]], [RB])
                for pc in range(4):
                    k.dma(sp, x1[:, pc * 8:(pc + 1) * 8, 0:T], xT_own[:, pc * 8:(pc + 1) * 8, t0:t0 + T], [], [RA], x1sl)

                def ev_res(j, bank, bb):
                    k.op(dve, lambda e: e.tensor_tensor(out=x1[:, j, 0:T], in0=bank[:, 0:T], in1=x1[:, j, 0:T], op=ALU.add), [bb, RA], [RA])
                gemm_fm("w_out", range(32), mg, RB, T, ev_res)
                norm_sb(x1, RA, T, 32, h2, RB, sq, sqb, rrow, rb, NP=4)

                def ev_mq(h, bank, bb):
                    k.op(act, lambda e: e.activation(out=mqT[:, h, 0:T], in_=bank[:, 0:T], func=AF.Copy), [bb], [RC])
                gemm_fm("w_mq", range(4), h2, RB, T, ev_mq)
                if T == 512:
                    msets = [(tb * 128, 128, None) for tb in range(4)]
                else:
                    k.dma(pool, ZMK[:], c_memkT.rearrange("p (s h m) -> p s h m", s=2, h=4), [], [RC], zms)
                    k.dma(pool, ZMV[:], c_memv.rearrange("p (s j n) -> p s j n", s=2, j=2), [], [RC], zms)
                    msets = [(s * 64, 64, s) for s in range(2)]
                for (c0, nq, s) in msets:
                    for h in range(4):
                        bS = next_bank(0, 4)
                        krhs = MK[:, h, :] if s is None else ZMK[:, s, h, :]
                        k.mm([(banks[bS][0:nq, 0:256], mqT[:, h, c0:c0 + nq], krhs, True, True)], [RC, MKb], [bbuf[bS]])
                        s4, s4b = st4[rr3[0] % 2]; p_ = Pn[rr3[0] % 2]; rr3[0] += 1
                        k.op(dve, lambda e: e.memset(s4[:, :], 0.0), [], [s4b])
                        k.op(dve, lambda e: e.tensor_reduce(out=s4[0:nq, 0:1], in_=banks[bS][0:nq, 0:256], axis=AX.X, op=ALU.max), [bbuf[bS], s4b], [s4b])
                        k.op(dve, lambda e: e.tensor_scalar(out=s4[0:nq, 1:2], in0=s4[0:nq, 0:1], scalar1=-MEM_SCALE, scalar2=None, op0=ALU.mult), [s4b], [s4b])
                        k.op(act, lambda e: e.activation(out=p_[0:nq, :], in_=banks[bS][0:nq, 0:256], func=AF.Exp, bias=s4[0:nq, 1:2], scale=MEM_SCALE, accum_out=s4[0:nq, 2:3]), [bbuf[bS], s4b, RC], [RC, s4b])
                        k.op(dve, lambda e: e.reciprocal(out=s4[0:nq, 3:4], in_=s4[0:nq, 2:3]), [s4b], [s4b])
                        k.op(dve, lambda e: e.tensor_scalar(out=p_[0:nq, :], in0=p_[0:nq, :], scalar1=s4[0:nq, 3:4], scalar2=None, op0=ALU.mult), [s4b, RC], [RC])
                        for jb in range(2):
                            bT = next_bank(4, 8)
                            k.op(pe, lambda e: e.transpose(out=bank_bf[bT][:, 0:nq], in_=p_[0:nq, jb * 128:(jb + 1) * 128], identity=ident[0:nq, 0:nq]), [RC, B_c], [bbuf[bT]])
                            k.op(act, lambda e: e.activation(out=PTt[:, jb, h, c0:c0 + nq], in_=bank_bf[bT][:, 0:nq], func=AF.Copy), [bbuf[bT]], [RC])
                for (c0, nq, s) in (msets if T != 512 else [(0, 512, None)]):
                    for h in range(4):
                        b = next_bank()
                        mms = []
                        for jb in range(2):
                            vl = MV[:, jb, h * 128:(h + 1) * 128] if s is None else ZMV[:, s, jb, h * 128:(h + 1) * 128]
                            mms.append((banks[b][:, 0:nq], vl, PTt[:, jb, h, c0:c0 + nq], jb == 0, jb == 1))
                        k.mm(mms, [RC, MKb], [bbuf[b]])
                        k.op(act, lambda e: e.activation(out=atT[:, h, c0:c0 + nq], in_=banks[b][:, 0:nq], func=AF.Copy), [bbuf[b]], [RC])
                gemm_fm("w_mo", range(32), atT, RC, T, ev_res, gpl=8)
                norm_sb(x1, RA, T, 64, h2, RB, sq, sqb, rrow, rb, NP=4)
                f0 = 0
                for nq_ in (22, 22, 21, 21):
                    for fl in range(nq_):
                        f = f0 + fl
                        wt, wb = wload("w_fg", f, 1)
                        bG = next_bank()
                        k.mm([(banks[bG][:, 0:T], wt[:, 0, kk, :], h2[:, kk, 0:T], kk == 0, kk == KC - 1) for kk in range(KC)], wb + [RB], [bbuf[bG]])
                        g_, gb_ = ga[rr3[0] % 2]; rr3[0] += 1
                        k.op(act, lambda e: e.activation(out=g_[:, 0:T], in_=banks[bG][:, 0:T], func=AF.Silu), [bbuf[bG]], [gb_])
                        wt2, wb2 = wload("w_fu", f, 1)
                        bU = next_bank()
                        k.mm([(banks[bU][:, 0:T], wt2[:, 0, kk, :], h2[:, kk, 0:T], kk == 0, kk == KC - 1) for kk in range(KC)], wb2 + [RB], [bbuf[bU]])
                        k.op(dve, lambda e: e.tensor_tensor(out=ff[:, fl, 0:T], in0=banks[bU][:, 0:T], in1=g_[:, 0:T], op=ALU.mult), [bbuf[bU], gb_], [RC])
                    gemm_fm("w_fd", range(32), ff, RC, T, ev_res, kc0=f0, nk=nq_)
                    f0 += nq_
                b = next_bank()
                for pc in range(8):
                    s_ = pc % 2
                    k.op(act, lambda e: e.activation(out=sq[s_][:, :, 0:T], in_=x1[:, pc * 4:(pc + 1) * 4, 0:T], func=AF.Square), [RA], [sqb[s_]])
                    k.mm([(banks[b][:, 0:T], ones, sq[s_][:, j, 0:T], pc == 0 and j == 0, pc == 7 and j == 3) for j in range(4)], [sqb[s_], B_c], [bbuf[b]])
                rstd_from_bank(banks[b][:, 0:T], bbuf[b], D, rrow[:, 0:T], rb)
                for kc in range(KC):
                    t, tb_, sl = nxt(yst)
                    eng = dve
                    k.op(eng, lambda e: e.scalar_tensor_tensor(out=t[:, 0:T], in0=x1[:, kc, 0:T], scalar=gn[:, 96 + kc:97 + kc], in1=rrow[:, 0:T], op0=ALU.mult, op1=ALU.mult), [RA, rb, B_c], [tb_])
                    k.dma(sp, o_yT[:, kc, t0:t0 + T], t[:, 0:T], [tb_], [B_out], sl)

    except _Stop:
        pass
    k.slots = k.all_slots
    k.barrier()
    blk.__exit__(None, None, None)
    return k


def _fm(w, N=128):
    Kd, NC = w.shape
    kc = Kd // 128; G = NC // N
    return np.ascontiguousarray(w.reshape(kc, 128, G, N).transpose(2, 1, 0, 3)).reshape(G * 128, kc * N)


def prep_weights(inp, need):
    w_in = inp["w_in"][0]
    out = {}
    def put(name, arr):
        if name in need:
            out[name] = arr
    if any(n in need for n in ("w_cq", "w_ckvF", "w_ckvT", "w_kr", "w_sbq", "w_sbk", "w_sbv", "w_gate")):
        put("w_cq", _fm(w_in[:, 0:1024])); put("w_ckvF", _fm(w_in[:, 1024:1536])); put("w_ckvT", _fm(w_in[:, 1024:1536], 256))
        kr = w_in[:, 1536:1600]
        krz = np.concatenate([kr[:, 32:64], kr[:, 0:32]], axis=1)
        put("w_kr", _fm(np.concatenate([kr, krz], axis=1), 64))
        put("w_sbq", _fm(w_in[:, 1600:3648])); put("w_sbk", _fm(w_in[:, 3648:5696])); put("w_sbv", _fm(w_in[:, 5696:7744], 256))
        put("w_gate", _fm(w_in[:, 7744:15936]))
    if "w_uqn" in need:
        wuq = inp["w_uq"][0]
        put("w_uqn", _fm(np.ascontiguousarray(wuq[:, :, 0:128]).reshape(1024, 2048)))
        r = wuq[:, :, 128:192]
        rz = np.concatenate([r[:, :, 32:64], r[:, :, 0:32]], axis=2)
        put("w_uqr", _fm(np.ascontiguousarray(np.stack([r, rz], axis=2)).reshape(1024, 16 * 2 * 64), 64))
    if "w_ukT" in need:
        wuk = inp["w_uk"][0]
        put("w_ukT", np.ascontiguousarray(wuk.reshape(4, 128, 16, 128).transpose(3, 2, 0, 1)).reshape(128, 64 * 128))
    if "w_uvr" in need:
        wuv = inp["w_uv"][0]
        put("w_uvr", np.ascontiguousarray(wuv.reshape(4, 128, 16, 128).transpose(1, 2, 0, 3)).reshape(128, 64 * 128))
    for nm, key in (("w_ba", "w_branch_a"), ("w_bb", "w_branch_b"), ("w_out", "w_out"), ("w_mq", "w_mq"), ("w_mkF", "w_mk"),
                    ("w_mo", "w_mo"), ("w_fg", "w_gate"), ("w_fu", "w_up"), ("w_fd", "w_down")):
        if nm in need:
            put(nm, _fm(inp[key][0]))
    if "w_mkT" in need:
        put("w_mkT", _fm(inp["w_mk"][0], 256)); put("w_mvT", _fm(inp["w_mv"][0], 256))
    return out


def rope_tabs(pos):
    half = 32
    inv = (10000.0 ** (-np.arange(half, dtype=np.float32) / half)).astype(np.float32)
    ang = pos.astype(np.float32)[:, None] * inv[None, :]
    cos = np.cos(ang).astype(np.float32).T; sin = np.sin(ang).astype(np.float32).T
    return np.ascontiguousarray(np.concatenate([cos, cos], 0)), np.ascontiguousarray(np.concatenate([-sin, sin], 0))


def prep_core(inp, core, stages, shared):
    b, c = core // 4, core % 4
    m = {}
    m.update(shared)
    xp = inp["x_prompt"][b]
    own_pos = np.concatenate([np.arange((4 * mm + c) * 128, (4 * mm + c + 1) * 128) for mm in range(16)])
    xs = inp["x_sample"][2 * core:2 * core + 2].reshape(128, D)
    xo = np.concatenate([xp[own_pos], xs], axis=0)
    m["xT_own"] = np.ascontiguousarray(xo.T.reshape(KC, 128, NOWN).transpose(1, 0, 2))
    pos_all = np.concatenate([own_pos, PAST + np.arange(64), PAST + np.arange(64)])
    m["ropeC_own"], m["ropeS_own"] = rope_tabs(pos_all)
    mk = np.zeros((128, 4 * 512 + 4 * 128 + 64), np.float32)
    kk = np.arange(128)[:, None]; qq = np.arange(128)[None, :]
    for j in range(4):
        kpos = j * 128 + kk; qpos = c * 128 + qq
        mla = np.where((kpos // 64) <= (qpos // 64), 0.0, NEG).astype(np.float32)
        mk[:, j * 512:(j + 1) * 512] = np.tile(mla, (1, 4))
        mk[:, 2048 + j * 128:2048 + (j + 1) * 128] = np.where(kpos < qpos, 0.0, NEG)
    mk[0:64, 2560:2624] = np.where(np.arange(64)[:, None] < np.arange(64)[None, :], 0.0, NEG)
    m["masks"] = mk
    if "PA" in stages:
        m["xT_seq"] = np.ascontiguousarray(xp.T.reshape(KC, 128, SEQ).transpose(1, 0, 2))
        m["ropeC_seq"], m["ropeS_seq"] = rope_tabs(np.arange(SEQ))
    if "PM" in stages:
        m["memT"] = np.ascontiguousarray(inp["mem_prompt"][b].T.reshape(KC, 128, 256).transpose(1, 0, 2))
    if "PB1" in stages:
        sl = slice(2 * core, 2 * core + 2)
        ck = inp["cache_mla_ckv"][0, sl]
        m["c_ckvT"] = np.ascontiguousarray(ck.reshape(2, PAST, 4, 128).transpose(0, 3, 2, 1)).reshape(256, 4 * PAST)
        m["c_ckv"] = np.ascontiguousarray(ck.reshape(2, 32, 128, 512).transpose(0, 2, 1, 3)).reshape(256, 32 * 512)
        m["c_krT"] = np.ascontiguousarray(inp["cache_mla_krope"][0, sl].transpose(0, 2, 1)).reshape(128, PAST)
        m["c_kT"] = np.ascontiguousarray(inp["cache_sb_k"][0, sl].transpose(0, 2, 3, 1)).reshape(2 * 16 * 128, PAST)
        m["c_v"] = np.ascontiguousarray(inp["cache_sb_v"][0, sl].reshape(2, 32, 128, 16, 128).transpose(0, 3, 2, 1, 4)).reshape(2 * 16 * 128, 32 * 128)
    if "PB3" in stages:
        sl = slice(2 * core, 2 * core + 2)
        mkc = inp["cache_mem_k"][0, sl]
        m["c_memkT"] = np.ascontiguousarray(mkc.transpose(3, 0, 2, 1)).reshape(128, 2 * 4 * 256)
        mvc = inp["cache_mem_v"][0, sl].reshape(2, 2, 128, 512)
        m["c_memv"] = np.ascontiguousarray(mvc.transpose(2, 0, 1, 3)).reshape(128, 2 * 2 * 512)
    return m


def prep_shared(inp, stages):
    need = []
    for s in stages:
        need += STAGE_W[s]
    sh = prep_weights(inp, set(need))
    cst = np.zeros((128, 1024), np.float32)
    cst[:, 0:128] = np.eye(128, dtype=np.float32)
    cst[:, 128:256] = (np.arange(128)[:, None] >= np.arange(128)[None, :]).astype(np.float32)
    cst[:, 256:384] = 1.0
    cst[0, 384] = 1.0
    sh["consts"] = cst
    g = np.zeros((128, 256), np.float32)
    def col(v):
        return v.reshape(-1, 128).T
    g[:, 0:32] = col(inp["g_mix"][0]); g[:, 32:64] = col(inp["g_xattn"][0]); g[:, 64:96] = col(inp["g_ffn"][0])
    g[:, 96:128] = col(inp["g_final"]); g[:, 128:160] = col(inp["g_mem"][0]); g[:, 160:168] = col(inp["g_q_lat"][0])
    g[:, 168:172] = col(inp["g_kv_lat"][0]); g[:, 172:236] = col(inp["b_gate"][0])
    sh["gains"] = g
    sh["gkv_row"] = np.ascontiguousarray(np.tile(inp["g_kv_lat"][0][None, :], (128, 1)))
    return sh


_CACHE = {}


def run(inp, stages=ALL_STAGES, cores=tuple(range(8))):
    inp = {kk: np.asarray(v) for kk, v in inp.items()}
    key = tuple(stages)
    if key not in _CACHE:
        _CACHE[key] = build(stages)
    kb = _CACHE[key]
    shared = prep_shared(inp, stages)
    maps = []
    for core in cores:
        mm = prep_core(inp, core, stages, shared)
        maps.append({n: np.ascontiguousarray(mm[n], dtype=np.float32) for n in kb.inputs})
    res = run_bass_kernel_spmd(kb.nc, maps, core_ids=list(range(len(cores))))
    return res.results


def assemble(results, cores=tuple(range(8))):
    y_p = np.zeros((2, SEQ, D), np.float32); y_s = np.zeros((16, 64, D), np.float32)
    p_ckv = np.zeros((1, 2, SEQ, 512), np.float32); p_kr = np.zeros((1, 2, SEQ, 64), np.float32)
    p_k = np.zeros((1, 2, SEQ, 16, 128), np.float32); p_v = np.zeros((1, 2, SEQ, 16, 128), np.float32)
    p_mk = np.zeros((1, 2, 256, 4, 128), np.float32); p_mv = np.zeros((1, 2, 256, 4, 128), np.float32)
    s_ckv = np.zeros((1, 16, 64, 512), np.float32); s_kr = np.zeros((1, 16, 64, 64), np.float32)
    s_k = np.zeros((1, 16, 64, 16, 128), np.float32); s_v = np.zeros((1, 16, 64, 16, 128), np.float32)
    for i, core in enumerate(cores):
        r = results[i]
        b, c = core // 4, core % 4
        own_pos = np.concatenate([np.arange((4 * mm + c) * 128, (4 * mm + c + 1) * 128) for mm in range(16)])
        y = r["o_yT"].transpose(2, 1, 0).reshape(NOWN, D)
        y_p[b, own_pos] = y[:NPO]; y_s[2 * core:2 * core + 2] = y[NPO:].reshape(2, 64, D)
        ck = r["o_ckv"]; p_ckv[0, b, own_pos] = ck[:NPO]; s_ckv[0, 2 * core:2 * core + 2] = ck[NPO:].reshape(2, 64, 512)
        kr = r["o_krT"].T; p_kr[0, b, own_pos] = kr[:NPO]; s_kr[0, 2 * core:2 * core + 2] = kr[NPO:].reshape(2, 64, 64)
        kT = r["o_kT"].transpose(2, 0, 1); p_k[0, b, own_pos] = kT[:NPO]; s_k[0, 2 * core:2 * core + 2] = kT[NPO:].reshape(2, 64, 16, 128)
        v = r["o_v"].reshape(NOWN, 16, 128); p_v[0, b, own_pos] = v[:NPO]; s_v[0, 2 * core:2 * core + 2] = v[NPO:].reshape(2, 64, 16, 128)
        if c == 0:
            p_mk[0, b] = r["o_memk"].reshape(256, 4, 128); p_mv[0, b] = r["o_memv"].reshape(256, 4, 128)
    return (y_p, y_s, p_ckv, p_kr, p_k, p_v, p_mk, p_mv, s_ckv, s_kr, s_k, s_v)


def kernel(**inputs):
    results = run(inputs)
    return assemble(results)
```
